# Optimizing a Trainium2 kernel written in Bass

```python
import jax
import jax.numpy as jnp
from jax import lax
import numpy as np


D_MODEL = 1024
BATCH = 4
SEQ = 8192
DEPTH = 4

GRID_W = 64
CTX_LEN = 256
EPS = 1e-6
F_MIN = 1e-6
F32 = jnp.float32
N_MOD = 9
D_FF = 2816
HG_HEADS = 4
HG_DK = 64
HG_DV = 64
HG_CHUNK = 64
MLA_HEADS = 8
MLA_Q_RANK = 384
MLA_KV_RANK = 256
MLA_NOPE = 64
MLA_ROPE = 32
MLA_V = 64
MLA_QK = MLA_NOPE + MLA_ROPE
ATTN_BLOCK = 128
ROPE_BASE = 10000.0
POOL_WINDOWS = (2, 4, 8, 16)
POOL_CH = 64
HG_K = HG_HEADS * HG_DK
HG_WIDTH = HG_HEADS * HG_DV
MLA_WIDTH = MLA_HEADS * MLA_V
POOL_WIDTH = len(POOL_WINDOWS) * POOL_CH
MIX_WIDTH = HG_WIDTH + MLA_WIDTH + POOL_WIDTH
IN_SPLITS = (HG_K, HG_K, HG_K, HG_WIDTH, HG_WIDTH, MLA_Q_RANK, MLA_KV_RANK, MLA_ROPE, POOL_WIDTH)
IN_WIDTH = sum(IN_SPLITS)

kernel_name = 'hybrid_hgrn2_mla_pool_macaron_dit'


def rms_norm(x, gain):
    xf = x.astype(F32)
    y = xf * lax.rsqrt(jnp.mean(xf * xf, axis=-1, keepdims=True) + EPS)
    return (y * gain.astype(F32)).astype(x.dtype)


def modulated_rms_norm(x, shift, scale):
    xf = x.astype(F32)
    y = xf * lax.rsqrt(jnp.mean(xf * xf, axis=-1, keepdims=True) + EPS)
    return (y * (1.0 + scale.astype(F32)) + shift.astype(F32)).astype(x.dtype)


def swiglu(h, w_in, w_out):
    g, u = jnp.split(h @ w_in, 2, axis=-1)
    return (jax.nn.silu(g) * u) @ w_out


def split_cols(p):
    return jnp.split(p, [int(o) for o in np.cumsum(IN_SPLITS)[:-1]], axis=-1)


def axial_rope(rows):
    row = jnp.repeat(jnp.arange(rows), GRID_W).astype(F32)
    col = jnp.tile(jnp.arange(GRID_W), rows).astype(F32)
    n_freq = MLA_ROPE // 4
    inv_freq = ROPE_BASE ** (-jnp.arange(n_freq, dtype=F32) / n_freq)
    ang = jnp.concatenate([row[:, None] * inv_freq, col[:, None] * inv_freq], axis=-1)
    return jnp.cos(ang)[:, None, :], jnp.sin(ang)[:, None, :]


def rotate_tail(x, cos, sin):
    nope, pe = x[..., :MLA_NOPE], x[..., MLA_NOPE:]
    half = MLA_ROPE // 2
    x1, x2 = pe[..., :half].astype(F32), pe[..., half:].astype(F32)
    rot = jnp.concatenate([x1 * cos - x2 * sin, x1 * sin + x2 * cos], axis=-1).astype(x.dtype)
    return jnp.concatenate([nope, rot], axis=-1)


def to_heads(a, n_heads):
    b, t, _ = a.shape
    return a.reshape(b, t, n_heads, -1).transpose(0, 2, 1, 3)


def hgrn2_gates(z, lb):
    zf = z.astype(F32)
    f = lb + (1.0 - lb) * jax.nn.sigmoid(zf)
    log_f = jnp.log(jnp.clip(f, F_MIN, 1.0))
    k = 1.0 - f
    return log_f, k


def hgrn2_inputs(q_z, f_fwd_z, f_bwd_z, i_z, lb_fwd, lb_bwd):
    q = to_heads(jax.nn.silu(q_z.astype(F32)), HG_HEADS)
    v = to_heads(i_z.astype(F32), HG_HEADS)
    lf_f, k_f = hgrn2_gates(f_fwd_z, lb_fwd)
    lf_b, k_b = hgrn2_gates(f_bwd_z, lb_bwd)
    return (q, v, to_heads(lf_f, HG_HEADS), to_heads(k_f, HG_HEADS),
            to_heads(lf_b, HG_HEADS), to_heads(k_b, HG_HEADS))


def gla_chunk_scan(q, k, v, log_f, s0):
    b_, h_, t_, _ = q.shape
    dv = v.shape[-1]
    n = t_ // HG_CHUNK

    def chunks(a):
        return jnp.moveaxis(a.reshape(b_, h_, n, HG_CHUNK, a.shape[-1]), 2, 0)

    incl = jnp.tril(jnp.ones((HG_CHUNK, HG_CHUNK), dtype=bool))[:, :, None]

    def step(s, blk):
        qc, kc, vc, lfc = blk
        b = jnp.cumsum(lfc, axis=2)
        o = jnp.einsum('bhck,bhkv->bhcv', qc * jnp.exp(b), s)
        rel = jnp.where(incl, b[:, :, :, None, :] - b[:, :, None, :, :], 0.0)
        decay = jnp.where(incl, jnp.exp(rel), 0.0)
        att = jnp.einsum('bhck,bhcsk,bhsk->bhcs', qc, decay, kc)
        o = o + jnp.einsum('bhcs,bhsv->bhcv', att, vc)
        b_end = b[:, :, -1:, :]
        s = jnp.exp(b_end[:, :, 0, :, None]) * s + jnp.einsum('bhsk,bhsv->bhkv', kc * jnp.exp(b_end - b), vc)
        return s, o

    s_fin, o = lax.scan(step, s0, (chunks(q), chunks(k), chunks(v), chunks(log_f)))
    return jnp.moveaxis(o, 0, 2).reshape(b_, h_, t_, dv), s_fin


def hgrn2_bidir(q, v, lf_f, k_f, lf_b, k_b, s_f0, s_b0):
    o_f, s_f = gla_chunk_scan(q, k_f, v, lf_f, s_f0)
    rev = lambda a: jnp.flip(a, axis=2)
    o_b, s_b = gla_chunk_scan(rev(q), rev(k_b), rev(v), rev(lf_b), s_b0)
    return o_f + rev(o_b), s_f, s_b


def hgrn2_readout(o, g_z, gain, dtype):
    b_, h_, t_, dv = o.shape
    o = o * lax.rsqrt(jnp.mean(o * o, axis=-1, keepdims=True) + EPS) * gain.astype(F32)
    o = o.transpose(0, 2, 1, 3).reshape(b_, t_, h_ * dv)
    return (o * jax.nn.silu(g_z.astype(F32))).astype(dtype)


def mla_queries(cq, q_a_gain, w_uq, q_gain, rope):
    b_, t_, _ = cq.shape
    q = (rms_norm(cq, q_a_gain) @ w_uq).reshape(b_, t_, MLA_HEADS, MLA_QK)
    q = rms_norm(q, q_gain)
    if rope is not None:
        q = rotate_tail(q, rope[0], rope[1])
    return q


def mla_keys_values(ckv, kpe, kv_a_gain, w_ukv, k_gain, rope):
    b_, t_, _ = ckv.shape
    kv = (rms_norm(ckv, kv_a_gain) @ w_ukv).reshape(b_, t_, MLA_HEADS, MLA_NOPE + MLA_V)
    k_nope, v = kv[..., :MLA_NOPE], kv[..., MLA_NOPE:]
    k_pe = jnp.broadcast_to(kpe[:, :, None, :], (b_, t_, MLA_HEADS, MLA_ROPE)).astype(k_nope.dtype)
    k = rms_norm(jnp.concatenate([k_nope, k_pe], axis=-1), k_gain)
    if rope is not None:
        k = rotate_tail(k, rope[0], rope[1])
    return k, v


def block_attention(q, k, v):
    b_, t_, h_, dq = q.shape
    n = t_ // ATTN_BLOCK
    scale = dq ** -0.5
    qb = jnp.moveaxis(q.reshape(b_, n, ATTN_BLOCK, h_, dq), 1, 0)

    def attend(qblk):
        s = jnp.einsum('bqhd,bkhd->bhqk', qblk, k).astype(F32) * scale
        p = jax.nn.softmax(s, axis=-1).astype(v.dtype)
        return jnp.einsum('bhqk,bkhd->bqhd', p, v)

    o = lax.map(attend, qb)
    return jnp.moveaxis(o, 0, 1).reshape(b_, t_, h_ * v.shape[-1])


def multiscale_pool(u, w_pool, pool_scale):
    b_, t_, _ = u.shape
    n_g = len(POOL_WINDOWS)
    ug = u.reshape(b_, t_, n_g, POOL_CH).astype(F32)
    csum = jnp.concatenate([jnp.zeros((b_, 1, n_g, POOL_CH), F32), jnp.cumsum(ug, axis=1)], axis=1)
    pos = jnp.arange(t_)
    means = []
    for g, w in enumerate(POOL_WINDOWS):
        lo = jnp.clip(pos - w // 2, 0, t_ - 1)
        hi = jnp.clip(pos + w - 1 - w // 2, 0, t_ - 1)
        cs = csum[:, :, g]
        cnt = (hi - lo + 1).astype(F32)[None, :, None]
        means.append((cs[:, hi + 1] - cs[:, lo]) / cnt)
    pooled = (jnp.stack(means, axis=2) - ug).astype(u.dtype)
    y = jnp.einsum('btgc,gcd->btgd', pooled, w_pool).reshape(b_, t_, n_g * POOL_CH)
    return y * pool_scale


def token_mixers(parts_l, parts_c, lb_fwd, lb_bwd, hg_gain, q_a_gain, w_uq, kv_a_gain, w_ukv,
                 q_gain, k_gain, w_pool, pool_scale, rope, need_ctx_out):
    q_l, ff_l, fb_l, i_l, g_l, cq_l, ckv_l, kpe_l, pool_l = parts_l
    q_c, ff_c, fb_c, i_c, g_c, cq_c, ckv_c, kpe_c, pool_c = parts_c
    dtype = q_l.dtype
    b_ = q_l.shape[0]
    s0 = jnp.zeros((b_, HG_HEADS, HG_DK, HG_DV), F32)
    o_c, s_f, s_b = hgrn2_bidir(*hgrn2_inputs(q_c, ff_c, fb_c, i_c, lb_fwd, lb_bwd), s0, s0)
    o_l, _, _ = hgrn2_bidir(*hgrn2_inputs(q_l, ff_l, fb_l, i_l, lb_fwd, lb_bwd), s_f, s_b)
    hg_out = hgrn2_readout(o_l, g_l, hg_gain, dtype)
    k_c, v_c = mla_keys_values(ckv_c, kpe_c, kv_a_gain, w_ukv, k_gain, None)
    k_l, v_l = mla_keys_values(ckv_l, kpe_l, kv_a_gain, w_ukv, k_gain, rope)
    q_lat = mla_queries(cq_l, q_a_gain, w_uq, q_gain, rope)
    att_out = block_attention(q_lat, jnp.concatenate([k_l, k_c], axis=1), jnp.concatenate([v_l, v_c], axis=1))
    pool_out = multiscale_pool(pool_l, w_pool, pool_scale)
    mix_l = jnp.concatenate([hg_out, att_out.astype(dtype), pool_out.astype(dtype)], axis=-1)
    if not need_ctx_out:
        return mix_l, None
    hg_c = hgrn2_readout(o_c, g_c, hg_gain, dtype)
    att_c = block_attention(mla_queries(cq_c, q_a_gain, w_uq, q_gain, None), k_c, v_c)
    pool_cx = multiscale_pool(pool_c, w_pool, pool_scale)
    mix_c = jnp.concatenate([hg_c, att_c.astype(dtype), pool_cx.astype(dtype)], axis=-1)
    return mix_l, mix_c


def setup_inputs(seed: int = 0) -> dict:
    key = jax.random.key(seed)
    ks = jax.random.split(key, 22)
    nrm = lambda k, shape, s: jax.random.normal(k, shape, F32) * s
    gain = lambda k, shape: 1.0 + 0.1 * jax.random.normal(k, shape, F32)
    return {
        'x': nrm(ks[0], (BATCH, SEQ, D_MODEL), 1.0),
        'c': nrm(ks[1], (BATCH, D_MODEL), 1.0),
        'ctx': nrm(ks[2], (BATCH, CTX_LEN, D_MODEL), 1.0),
        'c_ctx': nrm(ks[3], (D_MODEL,), 1.0),
        'w_mod': nrm(ks[4], (DEPTH, D_MODEL, N_MOD * D_MODEL), 0.5 * D_MODEL ** -0.5),
        'b_mod': nrm(ks[5], (DEPTH, N_MOD * D_MODEL), 0.01),
        'ffn1_w_in': nrm(ks[6], (DEPTH, D_MODEL, 2 * D_FF), D_MODEL ** -0.5),
        'ffn1_w_out': nrm(ks[7], (DEPTH, D_FF, D_MODEL), D_FF ** -0.5),
        'w_in': nrm(ks[8], (DEPTH, D_MODEL, IN_WIDTH), D_MODEL ** -0.5),
        'w_out': nrm(ks[9], (DEPTH, MIX_WIDTH, D_MODEL), MIX_WIDTH ** -0.5),
        'hg_lb_logits': nrm(ks[10], (DEPTH, 2, HG_K), 0.1),
        'hg_out_gain': gain(ks[11], (DEPTH, HG_DV)),
        'mla_q_a_gain': gain(ks[12], (DEPTH, MLA_Q_RANK)),
        'mla_w_uq': nrm(ks[13], (DEPTH, MLA_Q_RANK, MLA_HEADS * MLA_QK), MLA_Q_RANK ** -0.5),
        'mla_kv_a_gain': gain(ks[14], (DEPTH, MLA_KV_RANK)),
        'mla_w_ukv': nrm(ks[15], (DEPTH, MLA_KV_RANK, MLA_HEADS * (MLA_NOPE + MLA_V)), MLA_KV_RANK ** -0.5),
        'mla_q_gain': gain(ks[16], (DEPTH, MLA_QK)),
        'mla_k_gain': gain(ks[17], (DEPTH, MLA_QK)),
        'pool_w': nrm(ks[18], (DEPTH, len(POOL_WINDOWS), POOL_CH, POOL_CH), POOL_CH ** -0.5),
        'pool_scale': gain(ks[19], (DEPTH, POOL_WIDTH)),
        'ffn2_w_in': nrm(ks[20], (DEPTH, D_MODEL, 2 * D_FF), D_MODEL ** -0.5),
        'ffn2_w_out': nrm(ks[21], (DEPTH, D_FF, D_MODEL), D_FF ** -0.5),
    }


def reference(x, c, ctx, c_ctx, w_mod, b_mod, ffn1_w_in, ffn1_w_out, w_in, w_out, hg_lb_logits,
              hg_out_gain, mla_q_a_gain, mla_w_uq, mla_kv_a_gain, mla_w_ukv, mla_q_gain, mla_k_gain,
              pool_w, pool_scale, ffn2_w_in, ffn2_w_out):
    b_, n_lat, d_ = x.shape
    rows = n_lat // GRID_W
    rope = axial_rope(rows)
    p = jax.nn.softmax(hg_lb_logits.astype(F32), axis=0)
    lower_bounds = jnp.cumsum(p, axis=0) - p[0]
    c_act = jax.nn.silu(c)
    cc_act = jax.nn.silu(c_ctx)
    xl, xc = x, ctx
    for l in range(DEPTH):
        need_ctx_out = l < DEPTH - 1
        mod_l = (c_act @ w_mod[l] + b_mod[l]).reshape(b_, N_MOD, 1, d_)
        mod_c = (cc_act @ w_mod[l] + b_mod[l]).reshape(N_MOD, 1, 1, d_)
        ml = [mod_l[:, j] for j in range(N_MOD)]
        mc = [mod_c[j] for j in range(N_MOD)]
        xl = xl + ml[2] * (0.5 * swiglu(modulated_rms_norm(xl, ml[0], ml[1]), ffn1_w_in[l], ffn1_w_out[l]))
        xc = xc + mc[2] * (0.5 * swiglu(modulated_rms_norm(xc, mc[0], mc[1]), ffn1_w_in[l], ffn1_w_out[l]))
        parts_l = split_cols(modulated_rms_norm(xl, ml[3], ml[4]) @ w_in[l])
        parts_c = split_cols(modulated_rms_norm(xc, mc[3], mc[4]) @ w_in[l])
        mix_l, mix_c = token_mixers(parts_l, parts_c, lower_bounds[l, 0], lower_bounds[l, 1], hg_out_gain[l],
                                    mla_q_a_gain[l], mla_w_uq[l], mla_kv_a_gain[l], mla_w_ukv[l],
                                    mla_q_gain[l], mla_k_gain[l], pool_w[l], pool_scale[l], rope, need_ctx_out)
        xl = xl + ml[5] * (mix_l @ w_out[l])
        xl = xl + ml[8] * (0.5 * swiglu(modulated_rms_norm(xl, ml[6], ml[7]), ffn2_w_in[l], ffn2_w_out[l]))
        if need_ctx_out:
            xc = xc + mc[5] * (mix_c @ w_out[l])
            xc = xc + mc[8] * (0.5 * swiglu(modulated_rms_norm(xc, mc[6], mc[7]), ffn2_w_in[l], ffn2_w_out[l]))
    return xl
```

```python
import numpy as np
import ml_dtypes
from contextlib import ExitStack
import concourse.bass as bass
import concourse.mybir as mybir
from concourse.bass_utils import run_bass_kernel_spmd
F32 = mybir.dt.float32
BF16 = mybir.dt.bfloat16
AF = mybir.ActivationFunctionType
ALU = mybir.AluOpType
D = 1024
TL = 8192
TC = 256
T = TL + TC
DFF = 2816
NL = 4
EPS = 1e-06
INW = 2208
CH = 32
NCHUNK = T // CH
ENGS = ('pe', 'act', 'dve', 'pool', 'sp')


def _mk(method, *args, **kwargs):
    return lambda e: getattr(e, method)(*args, **kwargs)


class Ev:
    __slots__ = ('sem', 'val')

    def __init__(self, sem, val):
        self.sem = sem
        self.val = val

class Prog:

    def __init__(self, nc, n_dma_sems=12):
        self.nc = nc
        self.ops = {e: [] for e in ENGS}
        self.cnt = {e: 0 for e in ENGS}
        self.known = {e: {} for e in ENGS}
        self.last_w = {}
        self.readers = {}
        self.n_dma_sems = n_dma_sems
        self.dma_uses = {}
        self.dma_rr = {e: 0 for e in ENGS}
        self.pending = {e: [] for e in ENGS}

    def _need(self, eng, ev, waits):
        if ev is None:
            return
        if ev.val is None:
            if ev.sem == ('eng', 'pe') and eng == 'pe':
                return
            raise RuntimeError('wait on unresolved event')
        k = self.known[eng]
        if k.get(ev.sem, 0) >= ev.val:
            return
        if ev.sem == ('eng', 'pe') and eng == 'pe':
            return
        k[ev.sem] = ev.val
        waits[ev.sem] = max(waits.get(ev.sem, 0), ev.val)

    def _deps(self, eng, r, w):
        waits = {}
        for t in r:
            self._need(eng, self.last_w.get(t), waits)
        for t in w:
            self._need(eng, self.last_w.get(t), waits)
            for ev in self.readers.get(t, ()):
                self._need(eng, ev, waits)
        return waits

    def _commit(self, ev, r, w):
        for t in r:
            self.readers.setdefault(t, []).append(ev)
        for t in w:
            self.last_w[t] = ev
            self.readers[t] = []

    def op(self, eng, fn, r=(), w=(), signal=True):
        w = list(w) + ['bank' + t[2] for t in list(r) + list(w) if t.startswith('ps') and t[2:3].isdigit()]
        waits = self._deps(eng, r, w)
        if signal:
            self.cnt[eng] += 1
            ev = Ev(('eng', eng), self.cnt[eng])
            for p in self.pending[eng]:
                p.val = ev.val
            self.pending[eng] = []
        else:
            ev = Ev(('eng', eng), None)
            self.pending[eng].append(ev)
        self.ops[eng].append((fn, waits, ('eng', eng) if signal else None, 1))
        self._commit(ev, r, w)
        return ev

    def mm(self, out, pairs, r=(), w=()):
        n = len(pairs)

        def mk(i, lhsT, rhs):
            return _mk('matmul', out, lhsT, rhs, start=i == 0, stop=i == n - 1)
        ev = None
        for i, (lhsT, rhs) in enumerate(pairs):
            ev = self.op('pe', mk(i, lhsT, rhs), r=r if i == 0 else (), w=w if i == 0 else (), signal=i == n - 1)
        return ev

    def dma(self, eng, out, in_, r=(), w=(), **kw):
        waits = self._deps(eng, r, w)
        idx = self.dma_rr[eng]
        self.dma_rr[eng] = (idx + 1) % self.n_dma_sems
        key = ('dma', eng, idx)
        uses = self.dma_uses.get(key, 0)
        if uses > 0:
            k = self.known[eng]
            if k.get(key, 0) < 16 * uses:
                k[key] = 16 * uses
                waits[key] = max(waits.get(key, 0), 16 * uses)
        self.dma_uses[key] = uses + 1
        ev = Ev(key, 16 * (uses + 1))
        self.ops[eng].append((_mk('dma_start', out=out, in_=in_, **kw), waits, key, 16))
        self._commit(ev, r, w)
        return ev

    def barrier(self, final=False):
        for e in ENGS:
            waits = {}
            k = self.known[e]
            for e2 in ENGS:
                v = self.cnt[e2]
                key = ('eng', e2)
                if v > 0 and k.get(key, 0) < v:
                    k[key] = v
                    waits[key] = v
            for key, uses in self.dma_uses.items():
                v = 16 * uses
                if k.get(key, 0) < v:
                    k[key] = v
                    waits[key] = v
            self.ops[e].append((None, waits, None, 0))
        self.last_w.clear()
        self.readers.clear()

    def emit(self):
        nc = self.nc
        handles = {'pe': 'tensor', 'act': 'scalar', 'dve': 'vector', 'pool': 'gpsimd', 'sp': 'sync'}
        with ExitStack() as st:
            sems = {}
            for e in ENGS:
                sems['eng', e] = st.enter_context(nc.semaphore('s_' + e))
            for key in self.dma_uses:
                sems[key] = st.enter_context(nc.semaphore('d_%s_%d' % (key[1], key[2])))
            block = st.enter_context(nc.Block())

            def run(e):

                def body(h):
                    for fn, waits, sig, inc in self.ops[e]:
                        for s, v in waits.items():
                            h.wait_ge(sems[s], v)
                        if fn is not None:
                            ins = fn(h)
                            if sig is not None:
                                ins.then_inc(sems[sig], inc)
                return body
            for e in ENGS:
                getattr(block, handles[e])(run(e))

class Arena:

    def __init__(self, t, n):
        self.t = t
        self.n = n
        self.off = 0

    def reset(self):
        self.off = 0

    def take(self, size, pat=None, parts=128, **kw):
        assert self.off + size <= self.n, (self.off, size, self.n)
        v = self.t[0:parts, self.off:self.off + size]
        self.off += size
        if pat:
            v = v.rearrange(pat, **kw)
        return v
TILES = [(i * 512, 512, 0) for i in range(TL // 512)] + [(TL, TC, 1)]

import os
STOP = int(os.environ.get('KSTOP', '99'))
STOP2 = int(os.environ.get('KSTOP2', '0'))


class _Stop(Exception):
    pass


def build(n_layers=NL, debug=(), upto=None):
    nc = bass.Bass('TRN2', target_bir_lowering=False)

    def din(name, shape, dt=F32):
        return nc.dram_tensor(name, list(shape), dt, kind='ExternalInput').ap()

    def dscr(name, shape, dt=F32):
        if name in debug:
            return nc.dram_tensor(name, list(shape), dt, kind='ExternalOutput').ap()
        return nc.dram_tensor(name, list(shape), dt).ap()
    I = {}
    for name, shape in [('x_in', (TL, D)), ('ctx_in', (TC, D)), ('cvT', (128, 16)), ('w_mod', (NL, D, 9 * D)), ('b_mod', (NL, 9 * D)), ('f1i', (NL, D, 2 * DFF)), ('f1o', (NL, DFF, D)), ('f2i', (NL, D, 2 * DFF)), ('f2o', (NL, DFF, D)), ('w_in', (NL, D, INW)), ('w_out', (NL, D, D)), ('lblB', (128, NL * 512)), ('lblF', (128, 16)), ('g_hg', (128, NL)), ('g_qa', (128, NL * 3)), ('g_kva', (128, NL * 2)), ('g_q', (96, NL)), ('g_k', (96, NL)), ('g_ps', (128, NL * 2)), ('w_uq', (NL, 384, 768)), ('wk_pad', (NL, 256, 768)), ('wv', (NL, 256, 512)), ('pwb', (NL, 2, 128, 128)), ('ident', (128, 128)), ('d1f', (128, 128)), ('d1b', (128, 128)), ('trf', (128, 128)), ('trb', (128, 128)), ('maskf', (32, 32)), ('maskb', (32, 32)), ('bd64', (128, 128)), ('rot', (96, 96)), ('shift', (32, 96)), ('cosT', (96, TL)), ('sinT', (96, TL)), ('corr', (128, 32))]:
        I[name] = din(name, shape)
    out = nc.dram_tensor('out', [TL, D], F32, kind='ExternalOutput').ap()
    xT = dscr('xT', (D, T))
    hT = dscr('hT', (D, T), BF16)
    mixT = dscr('mixT', (D, T), BF16)
    QT = dscr('QT', (8, 96, T), BF16)
    KT = dscr('KT', (8, 96, T), BF16)
    VV = dscr('VV', (T, 8, 65), BF16)
    uT = dscr('uT', (256, T))
    KH = dscr('KH', (T, 512), BF16)
    VH = dscr('VH', (T, 256), BF16)
    HQ = dscr('HQ', (2, 256, T), BF16)
    HK = dscr('HK', (2, 256, T), BF16)
    EBE = dscr('EBE', (2, 2, 128, NCHUNK))
    GT = dscr('GT', (256, T), BF16)
    OFB = dscr('OFB', (2, 256, T))
    MODD = dscr('MODD', (128, NL * 144))
    with ExitStack() as st:

        def sb(n, s, d=F32):
            return st.enter_context(nc.sbuf_tensor(n, list(s), d))
        NBW, NBA, NFA = (36000, 20480, 13312)
        BWt = sb('BW', (128, NBW), BF16)
        BAt = sb('BA', (128, NBA), BF16)
        FAt = sb('FA', (128, NFA), F32)
        BW, BA, FA = (Arena(BWt, NBW), Arena(BAt, NBA), Arena(FAt, NFA))
        ps = [st.enter_context(nc.psum_tensor('ps%d' % i, [128, 512], F32)) for i in range(8)]
        MOD = sb('MOD', (128, NL * 144))
        SC1 = sb('SC1', (128, NL * 48))
        GHT = sb('GHT', (128, NL * 48))
        cact = sb('cact', (128, 16))
        identF = sb('identF', (128, 128))
        d1f = sb('d1fS', (128, 128))
        d1b = sb('d1bS', (128, 128))
        trf = sb('trfS', (128, 128))
        trb = sb('trbS', (128, 128))
        maskf = sb('maskfS', (32, 32))
        maskb = sb('maskbS', (32, 32))
        bdF = sb('bdF', (128, 128))
        bd16 = sb('bd16', (128, 128), BF16)
        ones16 = sb('ones16', (128, 128), BF16)
        onesF = sb('onesF', (128, 128))
        rotF = sb('rotF', (96, 96))
        shF = sb('shF', (32, 96))
        sh16 = sb('sh16', (32, 96), BF16)
        corr = sb('corrS', (128, 32))
        epsT = sb('epsT', (128, 1))
        gains = sb('gains', (128, 64))
        lbF = sb('lbF', (128, 16))
        omlF = sb('omlF', (128, 16))
        lbtmp = sb('lbtmp', (128, 16))
        S32 = sb('S32', (128, 4 * 64))
        S16 = sb('S16', (128, 4 * 64), BF16)
        P = Prog(nc)
        ckc = [0]

        def ck():
            ckc[0] += 1
            if ckc[0] == STOP2:
                P.barrier()
                raise _Stop()

        def mod_ap(tile, l, s, c, m, n=48):
            i = ((l * (n // 16) + s) * 8 + c) * 2 + m
            return tile[:, i:i + 1]

        def load_consts():
            for dst, name in [(identF, 'ident'), (d1f, 'd1f'), (d1b, 'd1b'), (trf, 'trf'), (trb, 'trb'), (maskf, 'maskf'), (maskb, 'maskb'), (bdF, 'bd64'), (rotF, 'rot'), (shF, 'shift'), (corr, 'corr'), (cact, 'cvT'), (lbF, 'lblF')]:
                P.dma('sp', dst[:], I[name][:, :], w=[name])
            P.dma('sp', gains[:, 0:4], I['g_hg'][:, :], w=['gains'])
            P.dma('sp', gains[:, 4:16], I['g_qa'][:, :], w=['gains'])
            P.dma('sp', gains[:, 16:24], I['g_kva'][:, :], w=['gains'])
            P.dma('sp', gains[:, 24:32], I['g_ps'][:, :], w=['gains'])
            P.dma('sp', gains[0:96, 32:36], I['g_q'][:, :], w=['gains'])
            P.dma('sp', gains[0:96, 36:40], I['g_k'][:, :], w=['gains'])
            P.op('dve', _mk('memset', onesF[:], 1.0), w=['onesF'])
            P.op('dve', _mk('memset', epsT[:], EPS), w=['epsT'])
            P.op('pool', _mk('memset', ones16[:], 1.0), w=['ones16'])
            P.op('pool', _mk('tensor_copy', bd16[:], bdF[:]), r=['bd64'], w=['bd16'])
            P.op('pool', _mk('tensor_copy', sh16[:], shF[:]), r=['shift'], w=['sh16'])
            P.op('act', _mk('activation', out=cact[:], in_=cact[:], func=AF.Silu), r=['cvT'], w=['cvT'])
            L = lambda l: lbF[:, 4 * l:4 * l + 4]
            mx = lbtmp[:, 0:4]
            sm = lbtmp[:, 4:8]
            rc = lbtmp[:, 8:12]
            P.op('dve', _mk('tensor_tensor', out=mx, in0=L(0), in1=L(1), op=ALU.max), r=['lblF'], w=['lbt'])
            P.op('dve', _mk('tensor_tensor', out=mx, in0=mx, in1=L(2), op=ALU.max), r=['lbt'], w=['lbt'])
            P.op('dve', _mk('tensor_tensor', out=mx, in0=mx, in1=L(3), op=ALU.max), r=['lbt'], w=['lbt'])
            for l in range(4):
                P.op('dve', _mk('tensor_tensor', out=L(l), in0=L(l), in1=mx, op=ALU.subtract), r=['lbt', 'lblF'], w=['lblF'])
            P.op('act', _mk('activation', out=lbF[:], in_=lbF[:], func=AF.Exp), r=['lblF'], w=['lblF'])
            P.op('dve', _mk('tensor_tensor', out=sm, in0=L(0), in1=L(1), op=ALU.add), r=['lblF'], w=['lbt'])
            P.op('dve', _mk('tensor_tensor', out=sm, in0=sm, in1=L(2), op=ALU.add), r=['lbt'], w=['lbt'])
            P.op('dve', _mk('tensor_tensor', out=sm, in0=sm, in1=L(3), op=ALU.add), r=['lbt'], w=['lbt'])
            P.op('dve', _mk('reciprocal', out=rc, in_=sm), r=['lbt'], w=['lbt'])
            for l in range(4):
                P.op('dve', _mk('tensor_tensor', out=L(l), in0=L(l), in1=rc, op=ALU.mult), r=['lbt', 'lblF'], w=['lblF'])
            P.op('dve', _mk('memset', L(0), 0.0), r=['lblF'], w=['lblF'])
            P.op('dve', _mk('tensor_tensor', out=L(2), in0=L(2), in1=L(1), op=ALU.add), r=['lblF'], w=['lblF'])
            P.op('dve', _mk('tensor_tensor', out=L(3), in0=L(3), in1=L(2), op=ALU.add), r=['lblF'], w=['lblF'])
            P.op('dve', _mk('tensor_scalar', out=omlF[:], in0=lbF[:], scalar1=-1.0, scalar2=1.0, op0=ALU.mult, op1=ALU.add), r=['lblF'], w=['omlF'])
            P.barrier()

        def phase_mod():
            FA.reset()
            stg = [FA.take(4096, 'p (c n) -> p c n', c=8) for _ in range(2)]
            brow = [FA.take(512, parts=1) for _ in range(2)]
            k = 0
            for l in range(n_layers):
                for nb in range(18):
                    b = k % 2
                    P.dma('sp', stg[b], I['w_mod'][l, :, nb * 512:(nb + 1) * 512].rearrange('(c p) n -> p c n', p=128), w=['stg%d' % b])
                    P.dma('pool', brow[b], I['b_mod'][l:l + 1, nb * 512:(nb + 1) * 512], w=['brow%d' % b])
                    pt = ps[k % 4]
                    for fc in range(4):
                        pairs = [(stg[b][:, c, 128 * fc:128 * fc + 128], cact[:, 2 * c:2 * c + 2]) for c in range(8)]
                        pairs.append((brow[b][0:1, 128 * fc:128 * fc + 128], onesF[0:1, 0:2]))
                        P.mm(pt[:, 2 * fc:2 * fc + 2], pairs, r=['stg%d' % b, 'brow%d' % b, 'cvT', 'onesF'], w=['ps%d' % (k % 4)])
                    g0 = l * 144 + nb * 8
                    P.op('dve', _mk('tensor_copy', MOD[:, g0:g0 + 8], pt[:, 0:8]), r=['ps%d' % (k % 4)], w=['MOD'])
                    k += 1
            for l in range(n_layers):
                for s in range(3):
                    src = MOD[:, l * 144 + (3 * s + 1) * 16:l * 144 + (3 * s + 2) * 16]
                    dst = SC1[:, (l * 3 + s) * 16:(l * 3 + s + 1) * 16]
                    P.op('dve', _mk('tensor_scalar', out=dst, in0=src, scalar1=1.0, scalar2=None, op0=ALU.add), r=['MOD'], w=['SC1'])
                    srcg = MOD[:, l * 144 + (3 * s + 2) * 16:l * 144 + (3 * s + 3) * 16]
                    dstg = GHT[:, (l * 3 + s) * 16:(l * 3 + s + 1) * 16]
                    fac = 1.0 if s == 1 else 0.5
                    P.op('dve', _mk('tensor_scalar', out=dstg, in0=srcg, scalar1=fac, scalar2=None, op0=ALU.mult), r=['MOD'], w=['GHT'])
            if 'MODD' in debug:
                P.dma('sp', MODD[:, :], MOD[:], r=['MOD'], w=['MODD'])
            P.barrier()

        def shift_ap(l, s, c, m):
            i = l * 144 + 3 * s * 16 + c * 2 + m
            return MOD[:, i:i + 1]

        def phase_in_transpose():
            FA.reset()
            xb = [FA.take(1024) for _ in range(4)]
            xt = FA.take(4096, 'p (c t) -> p c t', c=8)
            for t0, tw, m in TILES:
                nb = tw // 128
                for i in range(nb):
                    src = I['x_in'][t0 + 128 * i:t0 + 128 * i + 128, :] if m == 0 else I['ctx_in'][128 * i:128 * i + 128, :]
                    P.dma('sp', xb[i], src, w=['xb%d' % i])
                for c in range(8):
                    for i in range(nb):
                        P.op('pe', _mk('transpose', ps[c][:, 128 * i:128 * i + 128], xb[i][:, 128 * c:128 * c + 128], identF[:]), r=['xb%d' % i, 'ident'], w=['ps%d' % c])
                    if c % 2 == 0:
                        P.op('act', _mk('activation', out=xt[:, c, 0:tw], in_=ps[c][:, 0:tw], func=AF.Copy), r=['ps%d' % c], w=['xt%d' % c])
                    else:
                        P.op('dve', _mk('tensor_copy', xt[:, c, 0:tw], ps[c][:, 0:tw]), r=['ps%d' % c], w=['xt%d' % c])
                P.dma('pool', xT[:, t0:t0 + tw].rearrange('(c p) t -> p c t', p=128), xt[:, :, 0:tw], r=['xt%d' % c for c in range(8)], w=['xT%d' % t0])
            P.barrier()

        def phase_out_transpose():
            FA.reset()
            xt = [FA.take(4096, 'p (c t) -> p c t', c=8) for _ in range(2)]
            ob = [FA.take(1024) for _ in range(4)]
            for ti, (t0, tw, m) in enumerate(TILES):
                if m == 1:
                    continue
                b = ti % 2
                P.dma('sp', xt[b][:, :, 0:tw], xT[:, t0:t0 + tw].rearrange('(c p) t -> p c t', p=128), w=['xt%d' % b])
                for i in range(tw // 128):
                    for c in range(8):
                        pt = ps[(i * 8 + c) % 8]
                        P.op('pe', _mk('transpose', pt[:, 0:128], xt[b][:, c, 128 * i:128 * i + 128], identF[:]), r=['xt%d' % b, 'ident'], w=['ps%d' % ((i * 8 + c) % 8)])
                        if c % 2 == 0:
                            P.op('act', _mk('activation', out=ob[i][:, 128 * c:128 * c + 128], in_=pt[:, 0:128], func=AF.Copy), r=['ps%d' % ((i * 8 + c) % 8)], w=['ob%d_%d' % (i, c)])
                        else:
                            P.op('dve', _mk('tensor_copy', ob[i][:, 128 * c:128 * c + 128], pt[:, 0:128]), r=['ps%d' % ((i * 8 + c) % 8)], w=['ob%d_%d' % (i, c)])
                    P.dma('pool', out[t0 + 128 * i:t0 + 128 * i + 128, :], ob[i], r=['ob%d_%d' % (i, c) for c in range(8)], w=['out%d_%d' % (t0, i)])
            P.barrier()

        def phase_norm(l, s):
            FA.reset()
            BA.reset()
            xt = [FA.take(4096, 'p (c t) -> p c t', c=8) for _ in range(2)]
            tmp = [FA.take(512) for _ in range(2)]
            rstd = FA.take(512)
            sq = BA.take(4096, 'p (c t) -> p c t', c=8)
            h = [BA.take(4096, 'p (c t) -> p c t', c=8) for _ in range(2)]
            for ti, (t0, tw, m) in enumerate(TILES):
                b = ti % 2
                X = xt[b]
                P.dma('sp', X[:, :, 0:tw], xT[:, t0:t0 + tw].rearrange('(c p) t -> p c t', p=128), w=['xt%d' % b])
                P.op('act', _mk('activation', out=sq[:, :, 0:tw], in_=X[:, :, 0:tw], func=AF.Square), r=['xt%d' % b], w=['sq'])
                P.mm(ps[0][:, 0:tw], [(ones16[:], sq[:, c, 0:tw]) for c in range(8)], r=['sq', 'ones16'], w=['ps0'])
                P.op('act', _mk('activation', out=rstd[:, 0:tw], in_=ps[0][:, 0:tw], func=AF.Sqrt, bias=epsT[:, 0:1], scale=1.0 / D), r=['ps0', 'epsT'], w=['rstd'])
                P.op('dve', _mk('reciprocal', out=rstd[:, 0:tw], in_=rstd[:, 0:tw]), r=['rstd'], w=['rstd'])
                for c in range(8):
                    tb = tmp[c % 2]
                    P.op('dve', _mk('scalar_tensor_tensor', out=tb[:, 0:tw], in0=X[:, c, 0:tw], scalar=mod_ap(SC1, l, s, c, m), in1=rstd[:, 0:tw], op0=ALU.mult, op1=ALU.mult), r=['xt%d' % b, 'rstd', 'SC1'], w=['tmp%d' % (c % 2)])
                    P.op('act', _mk('activation', out=h[b][:, c, 0:tw], in_=tb[:, 0:tw], func=AF.Identity, bias=shift_ap(l, s, c, m), scale=1.0), r=['tmp%d' % (c % 2), 'MOD'], w=['h%d_%d' % (b, c)])
                P.dma('pool', hT[:, t0:t0 + tw].rearrange('(c p) t -> p c t', p=128), h[b][:, :, 0:tw], r=['h%d_%d' % (b, c) for c in range(8)], w=['hT%d' % t0])
            P.barrier()

        def residual_update(pt, ptok, X, xtok, j, tw, l, s, m):
            P.op('dve', _mk('scalar_tensor_tensor', out=X[:, j, 0:tw], in0=pt[:, 0:tw], scalar=mod_ap(GHT, l, s, j, m), in1=X[:, j, 0:tw], op0=ALU.mult, op1=ALU.add), r=[ptok, 'GHT'], w=[xtok + '_%d' % j])

        def phase_ffn(l, s, wi, wo):
            NH = 11
            for half in range(2):
                FA.reset()
                BA.reset()
                BW.reset()
                wg = BW.take(8 * 1408, 'p (c n) -> p c n', c=8)
                wu = BW.take(8 * 1408, 'p (c n) -> p c n', c=8)
                wob = BW.take(NH * 1024, 'p (c n) -> p c n', c=NH)
                stg = [FA.take(1408) for _ in range(2)]
                k = 0
                f0 = half * 1408
                for c in range(8):
                    for dst, col0 in ((wg, f0), (wu, DFF + f0)):
                        bb = k % 2
                        P.dma('sp', stg[bb], wi[l, 128 * c:128 * c + 128, col0:col0 + 1408], w=['stg%d' % bb])
                        P.op('pool', _mk('tensor_copy', dst[:, c, :], stg[bb][:, 0:1408]), r=['stg%d' % bb], w=['wres'])
                        k += 1
                for i in range(NH):
                    bb = k % 2
                    P.dma('sp', stg[bb][:, 0:1024], wo[l, f0 + 128 * i:f0 + 128 * i + 128, :], w=['stg%d' % bb])
                    P.op('pool', _mk('tensor_copy', wob[:, i, :], stg[bb][:, 0:1024]), r=['stg%d' % bb], w=['wres'])
                    k += 1
                xt = [FA.take(4096, 'p (c t) -> p c t', c=8) for _ in range(2)]
                sg = [FA.take(512) for _ in range(2)]
                ht = [BA.take(4096, 'p (c t) -> p c t', c=8) for _ in range(2)]
                act = BA.take(NH * 512, 'p (c t) -> p c t', c=NH)
                for ti, (t0, tw, m) in enumerate(TILES):
                    b = ti % 2
                    H = ht[b]
                    X = xt[b]
                    P.dma('sp', H[:, :, 0:tw], hT[:, t0:t0 + tw].rearrange('(c p) t -> p c t', p=128), w=['ht%d' % b])
                    P.dma('sp', X[:, :, 0:tw], xT[:, t0:t0 + tw].rearrange('(c p) t -> p c t', p=128), r=['xT%d' % t0], w=['xt%d_%d' % (b, j) for j in range(8)])
                    for i in range(NH):
                        pg = ps[2 * i % 4]
                        pu = ps[(2 * i + 1) % 4]
                        tg = 'ps%d' % (2 * i % 4)
                        tu = 'ps%d' % ((2 * i + 1) % 4)
                        P.mm(pg[:, 0:tw], [(wg[:, c, 128 * i:128 * i + 128], H[:, c, 0:tw]) for c in range(8)], r=['ht%d' % b, 'wres'], w=[tg])
                        P.mm(pu[:, 0:tw], [(wu[:, c, 128 * i:128 * i + 128], H[:, c, 0:tw]) for c in range(8)], r=['ht%d' % b, 'wres'], w=[tu])
                        sgb = sg[i % 2]
                        P.op('act', _mk('activation', out=sgb[:, 0:tw], in_=pg[:, 0:tw], func=AF.Silu), r=[tg], w=['sg%d' % (i % 2)])
                        P.op('dve', _mk('tensor_tensor', out=act[:, i, 0:tw], in0=sgb[:, 0:tw], in1=pu[:, 0:tw], op=ALU.mult), r=['sg%d' % (i % 2), tu], w=['act%d' % i])
                    for j in range(8):
                        py = ps[4 + j % 2]
                        ty = 'ps%d' % (4 + j % 2)
                        P.mm(py[:, 0:tw], [(wob[:, i, 128 * j:128 * j + 128], act[:, i, 0:tw]) for i in range(NH)], r=['act%d' % i for i in range(NH)] + ['wres'], w=[ty])
                        residual_update(py, ty, X, 'xt%d' % b, j, tw, l, s, m)
                    P.dma('pool', xT[:, t0:t0 + tw].rearrange('(c p) t -> p c t', p=128), X[:, :, 0:tw], r=['xt%d_%d' % (b, j) for j in range(8)], w=['xT%d' % t0])
                P.barrier()

        def phase_outproj(l):
            FA.reset()
            BA.reset()
            BW.reset()
            wo = BW.take(8 * 1024, 'p (c n) -> p c n', c=8)
            stg = [FA.take(1024) for _ in range(2)]
            for c in range(8):
                bb = c % 2
                P.dma('sp', stg[bb], I['w_out'][l, 128 * c:128 * c + 128, :], w=['stg%d' % bb])
                P.op('pool', _mk('tensor_copy', wo[:, c, :], stg[bb][:]), r=['stg%d' % bb], w=['wres'])
            xt = [FA.take(4096, 'p (c t) -> p c t', c=8) for _ in range(2)]
            mt = [BA.take(4096, 'p (c t) -> p c t', c=8) for _ in range(2)]
            for ti, (t0, tw, m) in enumerate(TILES):
                b = ti % 2
                M = mt[b]
                X = xt[b]
                P.dma('sp', M[:, :, 0:tw], mixT[:, t0:t0 + tw].rearrange('(c p) t -> p c t', p=128), w=['mt%d' % b])
                P.dma('sp', X[:, :, 0:tw], xT[:, t0:t0 + tw].rearrange('(c p) t -> p c t', p=128), w=['xt%d_%d' % (b, j) for j in range(8)])
                for j in range(8):
                    py = ps[j % 4]
                    ty = 'ps%d' % (j % 4)
                    P.mm(py[:, 0:tw], [(wo[:, c, 128 * j:128 * j + 128], M[:, c, 0:tw]) for c in range(8)], r=['mt%d' % b, 'wres'], w=[ty])
                    residual_update(py, ty, X, 'xt%d' % b, j, tw, l, 1, m)
                P.dma('pool', xT[:, t0:t0 + tw].rearrange('(c p) t -> p c t', p=128), X[:, :, 0:tw], r=['xt%d_%d' % (b, j) for j in range(8)], w=['xT%d' % t0])
            P.barrier()

        def rstd_from(pt, ptok, dst, dtok, tw, n, parts=128):
            P.op('act', _mk('activation', out=dst[0:parts, 0:tw], in_=pt[0:parts, 0:tw], func=AF.Sqrt, bias=epsT[0:parts, 0:1], scale=1.0 / n), r=[ptok, 'epsT'], w=[dtok])
            P.op('dve', _mk('reciprocal', out=dst[0:parts, 0:tw], in_=dst[0:parts, 0:tw]), r=[dtok], w=[dtok])

        def phase_inproj(l):
            FA.reset()
            BA.reset()
            BW.reset()
            win = BW.take(8 * INW, 'p (c n) -> p c n', c=8)
            wuq = BW.take(3 * 768, 'p (c n) -> p c n', c=3)
            wkp = BW.take(2 * 768, 'p (c n) -> p c n', c=2)
            wvv = BW.take(2 * 512, 'p (c n) -> p c n', c=2)
            stg = [FA.take(INW) for _ in range(2)]
            k = 0
            for c in range(8):
                bb = k % 2
                P.dma('sp', stg[bb], I['w_in'][l, 128 * c:128 * c + 128, :], w=['stg%d' % bb])
                P.op('pool', _mk('tensor_copy', win[:, c, :], stg[bb][:]), r=['stg%d' % bb], w=['wres'])
                k += 1
            for dst, name, ncc, n in ((wuq, 'w_uq', 3, 768), (wkp, 'wk_pad', 2, 768), (wvv, 'wv', 2, 512)):
                for c in range(ncc):
                    bb = k % 2
                    P.dma('sp', stg[bb][:, 0:n], I[name][l, 128 * c:128 * c + 128, :], w=['stg%d' % bb])
                    P.op('pool', _mk('tensor_copy', dst[:, c, :], stg[bb][:, 0:n]), r=['stg%d' % bb], w=['wres'])
                    k += 1
            P.barrier()
            FA.reset()
            LBB = FA.take(512)
            OMLB = FA.take(512)
            lbl = FA.take(2048)
            tmpA = FA.take(512)
            tmpB = FA.take(512)
            P.dma('sp', lbl, I['lblB'][:, :], w=['lbl'])
            Lr = lambda i: lbl[:, 512 * i:512 * i + 512]
            P.op('dve', _mk('tensor_tensor', out=tmpA, in0=Lr(0), in1=Lr(1), op=ALU.max), r=['lbl'], w=['tA'])
            P.op('dve', _mk('tensor_tensor', out=tmpA, in0=tmpA, in1=Lr(2), op=ALU.max), r=['tA'], w=['tA'])
            P.op('dve', _mk('tensor_tensor', out=tmpA, in0=tmpA, in1=Lr(3), op=ALU.max), r=['tA'], w=['tA'])
            for i in range(4):
                P.op('dve', _mk('tensor_tensor', out=Lr(i), in0=Lr(i), in1=tmpA, op=ALU.subtract), r=['tA', 'lbl'], w=['lbl'])
            P.op('act', _mk('activation', out=lbl, in_=lbl, func=AF.Exp), r=['lbl'], w=['lbl'])
            P.op('dve', _mk('tensor_tensor', out=tmpB, in0=Lr(0), in1=Lr(1), op=ALU.add), r=['lbl'], w=['tB'])
            P.op('dve', _mk('tensor_tensor', out=tmpB, in0=tmpB, in1=Lr(2), op=ALU.add), r=['tB'], w=['tB'])
            P.op('dve', _mk('tensor_tensor', out=tmpB, in0=tmpB, in1=Lr(3), op=ALU.add), r=['tB'], w=['tB'])
            P.op('dve', _mk('reciprocal', out=tmpB, in_=tmpB), r=['tB'], w=['tB'])
            if l == 0:
                P.op('dve', _mk('memset', LBB, 0.0), w=['LBB'])
            else:
                P.op('dve', _mk('tensor_copy', LBB, Lr(1)), r=['lbl'], w=['LBB'])
                for i in range(2, l + 1):
                    P.op('dve', _mk('tensor_tensor', out=LBB, in0=LBB, in1=Lr(i), op=ALU.add), r=['lbl', 'LBB'], w=['LBB'])
                P.op('dve', _mk('tensor_tensor', out=LBB, in0=LBB, in1=tmpB, op=ALU.mult), r=['tB', 'LBB'], w=['LBB'])
            P.op('dve', _mk('tensor_scalar', out=OMLB, in0=LBB, scalar1=-1.0, scalar2=1.0, op0=ALU.mult, op1=ALU.add), r=['LBB'], w=['OMLB'])
            P.barrier()
            FA.off = 1024
            if STOP == 0:
                P.barrier()
                return
            tk = [FA.take(512) for _ in range(4)]
            fm = [FA.take(512) for _ in range(4)]
            cqF = FA.take(1536, 'p (c t) -> p c t', c=3)
            ckvF = FA.take(1024, 'p (c t) -> p c t', c=2)
            rs = FA.take(512)
            hq = [FA.take(512) for _ in range(4)]
            csT = FA.take(512, parts=96)
            snT = FA.take(512, parts=96)
            uF = FA.take(1024, 'p (c t) -> p c t', c=2)
            ebT = FA.take(64)
            ht = [BA.take(4096, 'p (c t) -> p c t', c=8) for _ in range(2)]
            kh16 = [BA.take(512) for _ in range(2)]
            vh16 = [BA.take(256) for _ in range(2)]
            qk16 = [BA.take(512) for _ in range(4)]
            g16 = [BA.take(512) for _ in range(2)]
            sq3 = BA.take(1536, 'p (c t) -> p c t', c=3)
            cqn = BA.take(1536, 'p (c t) -> p c t', c=3)
            ckvn = BA.take(1024, 'p (c t) -> p c t', c=2)
            kpe16 = BA.take(512, parts=32)
            sqh = BA.take(512)
            o16 = [BA.take(512) for _ in range(2)]
            v16 = [BA.take(520, 'p (h e) -> p h e', h=8) for _ in range(2)]
            for b in range(2):
                P.op('pool', _mk('memset', v16[b][:, :, 64:65], 1.0), w=['v16_%d' % b])
            gq = lambda c: gains[:, 4 + l * 3 + c:5 + l * 3 + c]
            gkv = lambda c: gains[:, 16 + l * 2 + c:17 + l * 2 + c]
            cnt = {'ps': 0, 'o': 0}

            def nps():
                i = cnt['ps'] % 8
                cnt['ps'] += 1
                return (ps[i], 'ps%d' % i)

            def nps4():
                i = cnt['ps'] % 4
                cnt['ps'] += 1
                return (ps[i], 'ps%d' % i)

            def featmm(col0, ncols, H, b, tw):
                pt, tok = nps()
                P.mm(pt[0:ncols, 0:tw], [(win[:, c, col0:col0 + ncols], H[:, c, 0:tw]) for c in range(8)], r=['ht%d' % b, 'wres'], w=[tok])
                return (pt, tok)

            def head_norm_rope(pt, tok, gcol, rope, dst_dram, h, t0, tw):
                raw, rst, qn, t1 = hq
                P.op('act', _mk('activation', out=sqh[0:96, 0:tw], in_=pt[0:96, 0:tw], func=AF.Square), r=[tok], w=['sqh'])
                P.op('dve', _mk('tensor_copy', raw[0:96, 0:tw], pt[0:96, 0:tw]), r=[tok], w=['hraw'])
                ck()
                p2, tok2 = nps()
                P.mm(p2[0:96, 0:tw], [(ones16[0:96, 0:96], sqh[0:96, 0:tw])], r=['sqh', 'ones16'], w=[tok2])
                ck()
                rstd_from(p2, tok2, rst, 'hrst', tw, 96.0, parts=96)
                ck()
                P.op('dve', _mk('scalar_tensor_tensor', out=qn[0:96, 0:tw], in0=raw[0:96, 0:tw], scalar=gains[0:96, gcol:gcol + 1], in1=rst[0:96, 0:tw], op0=ALU.mult, op1=ALU.mult), r=['hraw', 'hrst', 'gains'], w=['hqn'])
                ck()
                ob = o16[cnt['o'] % 2]
                otok = 'o16_%d' % (cnt['o'] % 2)
                cnt['o'] += 1
                if rope:
                    p3, tok3 = nps()
                    P.mm(p3[0:96, 0:tw], [(rotF[:, :], qn[0:96, 0:tw])], r=['hqn', 'rot'], w=[tok3])
                    P.op('dve', _mk('tensor_tensor', out=t1[0:96, 0:tw], in0=qn[0:96, 0:tw], in1=csT[0:96, 0:tw], op=ALU.mult), r=['hqn', 'cs'], w=['ht1'])
                    P.op('dve', _mk('tensor_tensor', out=qn[0:96, 0:tw], in0=p3[0:96, 0:tw], in1=snT[0:96, 0:tw], op=ALU.mult), r=[tok3, 'sn', 'ht1'], w=['hqn'])
                    P.op('dve', _mk('tensor_tensor', out=ob[0:96, 0:tw], in0=t1[0:96, 0:tw], in1=qn[0:96, 0:tw], op=ALU.add), r=['ht1', 'hqn'], w=[otok])
                else:
                    P.op('act', _mk('activation', out=ob[0:96, 0:tw], in_=qn[0:96, 0:tw], func=AF.Copy), r=['hqn'], w=[otok])
                ck()
                P.dma('pool', dst_dram[h, :, t0:t0 + tw], ob[0:96, 0:tw], r=[otok], w=['hd'])
                ck()
            for ti, (t0, tw, m) in enumerate(TILES):
                b = ti % 2
                H = ht[b]
                P.dma('sp', H[:, :, 0:tw], hT[:, t0:t0 + tw].rearrange('(c p) t -> p c t', p=128), w=['ht%d' % b])
                rope = m == 0
                if rope:
                    P.dma('sp', csT[:, 0:tw], I['cosT'][:, t0:t0 + tw], w=['cs'])
                    P.dma('sp', snT[:, 0:tw], I['sinT'][:, t0:t0 + tw], w=['sn'])
                for i in range(tw // 128):
                    bb = i % 2
                    pz, tz = nps4()
                    P.mm(pz[:, 0:512], [(H[:, c, 128 * i:128 * i + 128], win[:, c, 256:768]) for c in range(8)], r=['ht%d' % b, 'wres'], w=[tz])
                    f, lf, kk, ee = tk
                    P.op('act', _mk('activation', out=f, in_=pz[:, 0:512], func=AF.Sigmoid), r=[tz], w=['tk_f'])
                    P.op('dve', _mk('tensor_tensor', out=f, in0=f, in1=OMLB, op=ALU.mult), r=['tk_f', 'OMLB'], w=['tk_f'])
                    P.op('dve', _mk('tensor_tensor', out=f, in0=f, in1=LBB, op=ALU.add), r=['tk_f', 'LBB'], w=['tk_f'])
                    P.op('dve', _mk('tensor_scalar', out=kk, in0=f, scalar1=-1.0, scalar2=1.0, op0=ALU.mult, op1=ALU.add), r=['tk_f'], w=['tk_kk'])
                    P.op('dve', _mk('tensor_scalar', out=f, in0=f, scalar1=1e-06, scalar2=1.0, op0=ALU.max, op1=ALU.min), r=['tk_f', 'tk_kk'], w=['tk_f'])
                    P.op('act', _mk('activation', out=lf, in_=f, func=AF.Ln), r=['tk_f'], w=['tk_lf'])
                    pd, td = nps4()
                    P.mm(pd[:, 0:256], [(d1f[:], lf[:, 0:256])], r=['tk_lf', 'd1f'], w=[td])
                    P.mm(pd[:, 256:512], [(d1b[:], lf[:, 256:512])], r=['tk_lf', 'd1b'], w=[td])
                    P.op('act', _mk('activation', out=ee, in_=pd[:, 0:512], func=AF.Exp), r=[td], w=['tk_e'])
                    P.op('dve', _mk('tensor_tensor', out=kh16[bb], in0=kk, in1=ee, op=ALU.mult), r=['tk_kk', 'tk_e'], w=['kh16_%d' % bb])
                    P.dma('pool', KH[t0 + 128 * i:t0 + 128 * i + 128, :], kh16[bb], r=['kh16_%d' % bb], w=['KHd'])
                    for g in range(4):
                        dd, pr = (g // 2, g % 2)
                        pb_, tb_ = (ps[4 + g], 'ps%d' % (4 + g))
                        P.mm(pb_[:, 128 * i:128 * i + 128], [(lf[:, dd * 256 + pr * 128:dd * 256 + pr * 128 + 128], (trf if dd == 0 else trb)[:])], r=['tk_lf', 'trf', 'trb'], w=[tb_])
                    pv, tv = nps4()
                    P.mm(pv[:, 0:256], [(H[:, c, 128 * i:128 * i + 128], win[:, c, 768:1024]) for c in range(8)], r=['ht%d' % b, 'wres'], w=[tv])
                    P.op('act', _mk('activation', out=vh16[bb], in_=pv[:, 0:256], func=AF.Copy), r=[tv], w=['vh16_%d' % bb])
                    P.dma('pool', VH[t0 + 128 * i:t0 + 128 * i + 128, :], vh16[bb], r=['vh16_%d' % bb], w=['VHd'])
                cnt['ps'] = 0
                if STOP == 1:
                    P.barrier()
                    return
                EBs = {}
                for g in range(4):
                    dd, pr = (g // 2, g % 2)
                    pb_, tb_ = (ps[4 + g], 'ps%d' % (4 + g))
                    eb, enb, kkf, qs = fm
                    P.op('act', _mk('activation', out=eb[:, 0:tw], in_=pb_[:, 0:tw], func=AF.Exp), r=[tb_], w=['fm_eb'])
                    P.op('act', _mk('activation', out=enb[:, 0:tw], in_=pb_[:, 0:tw], func=AF.Exp, scale=-1.0), r=[tb_], w=['fm_enb'])
                    pq, tq = featmm(128 * pr, 128, H, b, tw)
                    P.op('act', _mk('activation', out=qs[:, 0:tw], in_=pq[:, 0:tw], func=AF.Silu), r=[tq], w=['fm_qs'])
                    qb = qk16[0]
                    P.op('dve', _mk('tensor_tensor', out=qb[:, 0:tw], in0=qs[:, 0:tw], in1=eb[:, 0:tw], op=ALU.mult), r=['fm_qs', 'fm_eb'], w=['qk16_0'])
                    P.dma('pool', HQ[dd, 128 * pr:128 * pr + 128, t0:t0 + tw], qb[:, 0:tw], r=['qk16_0'], w=['HQd'])
                    pk, tkk = featmm(256 + dd * 256 + 128 * pr, 128, H, b, tw)
                    P.op('act', _mk('activation', out=kkf[:, 0:tw], in_=pk[:, 0:tw], func=AF.Sigmoid, scale=-1.0), r=[tkk], w=['fm_kk'])
                    kb = qk16[1]
                    gi = l * 4 + dd * 2 + pr
                    P.op('dve', _mk('scalar_tensor_tensor', out=kb[:, 0:tw], in0=kkf[:, 0:tw], scalar=omlF[:, gi:gi + 1], in1=enb[:, 0:tw], op0=ALU.mult, op1=ALU.mult), r=['fm_kk', 'fm_enb', 'omlF'], w=['qk16_1'])
                    P.dma('pool', HK[dd, 128 * pr:128 * pr + 128, t0:t0 + tw], kb[:, 0:tw], r=['qk16_1'], w=['HKd'])
                    nch = tw // CH
                    ebv = eb[:, 0:tw].rearrange('p (n s) -> p n s', s=CH)
                    sel = ebv[:, :, CH - 1:CH] if dd == 0 else ebv[:, :, 0:1]
                    P.op('dve', _mk('tensor_copy', ebT[:, 16 * g:16 * g + nch].rearrange('p (n o) -> p n o', o=1), sel), r=['fm_eb'], w=['ebT%d' % g])
                    P.dma('pool', EBE[dd, pr, :, t0 // CH:t0 // CH + nch], ebT[:, 16 * g:16 * g + nch], r=['ebT%d' % g], w=['EBEd'])
                for pr in range(2):
                    pg, tg = featmm(1024 + 128 * pr, 128, H, b, tw)
                    P.op('act', _mk('activation', out=g16[pr][:, 0:tw], in_=pg[:, 0:tw], func=AF.Silu), r=[tg], w=['g16_%d' % pr])
                    P.dma('pool', GT[128 * pr:128 * pr + 128, t0:t0 + tw], g16[pr][:, 0:tw], r=['g16_%d' % pr], w=['GTd'])
                for ch in range(2):
                    pu, tu = featmm(1952 + 128 * ch, 128, H, b, tw)
                    P.op('dve', _mk('tensor_copy', uF[:, ch, 0:tw], pu[:, 0:tw]), r=[tu], w=['uF%d' % ch])
                P.dma('pool', uT[:, t0:t0 + tw].rearrange('(c p) t -> p c t', p=128), uF[:, :, 0:tw], r=['uF0', 'uF1'], w=['uTd'])
                if STOP == 3:
                    P.barrier()
                    return
                for c in range(3):
                    pc, tc_ = featmm(1280 + 128 * c, 128, H, b, tw)
                    if 'a' in os.environ.get('KFLAG', 'ad'):
                        P.op('act', _mk('activation', out=sq3[:, c, 0:tw], in_=pc[:, 0:tw], func=AF.Square), r=[tc_], w=['sq3_%d' % c])
                    if 'd' in os.environ.get('KFLAG', 'ad'):
                        P.op('dve', _mk('tensor_copy', cqF[:, c, 0:tw], pc[:, 0:tw]), r=[tc_], w=['cqF%d' % c])
                ck()
                pss, tss = nps()
                P.mm(pss[:, 0:tw], [(ones16[:], sq3[:, c, 0:tw]) for c in range(3)], r=['sq3_0', 'sq3_1', 'sq3_2', 'ones16'], w=[tss])
                ck()
                rstd_from(pss, tss, rs, 'rs', tw, 384.0)
                ck()
                for c in range(3):
                    P.op('dve', _mk('scalar_tensor_tensor', out=cqn[:, c, 0:tw], in0=cqF[:, c, 0:tw], scalar=gq(c), in1=rs[:, 0:tw], op0=ALU.mult, op1=ALU.mult), r=['cqF%d' % c, 'rs', 'gains'], w=['cqn%d' % c])
                for h in range(8):
                    pt, tok = nps()
                    P.mm(pt[0:96, 0:tw], [(wuq[:, c, 96 * h:96 * h + 96], cqn[:, c, 0:tw]) for c in range(3)], r=['cqn0', 'cqn1', 'cqn2', 'wres'], w=[tok])
                    ck()
                    head_norm_rope(pt, tok, 32 + l, rope, QT, h, t0, tw)
                if STOP == 4:
                    P.barrier()
                    return
                for c in range(2):
                    pc, tc_ = featmm(1664 + 128 * c, 128, H, b, tw)
                    P.op('act', _mk('activation', out=sq3[:, c, 0:tw], in_=pc[:, 0:tw], func=AF.Square), r=[tc_], w=['sq3_%d' % c])
                    P.op('dve', _mk('tensor_copy', ckvF[:, c, 0:tw], pc[:, 0:tw]), r=[tc_], w=['ckvF%d' % c])
                pss, tss = nps()
                P.mm(pss[:, 0:tw], [(ones16[:], sq3[:, c, 0:tw]) for c in range(2)], r=['sq3_0', 'sq3_1', 'ones16'], w=[tss])
                rstd_from(pss, tss, rs, 'rs', tw, 256.0)
                for c in range(2):
                    P.op('dve', _mk('scalar_tensor_tensor', out=ckvn[:, c, 0:tw], in0=ckvF[:, c, 0:tw], scalar=gkv(c), in1=rs[:, 0:tw], op0=ALU.mult, op1=ALU.mult), r=['ckvF%d' % c, 'rs', 'gains'], w=['ckvn%d' % c])
                pkp, tkp = featmm(1920, 32, H, b, tw)
                P.op('act', _mk('activation', out=kpe16[0:32, 0:tw], in_=pkp[0:32, 0:tw], func=AF.Copy), r=[tkp], w=['kpe16'])
                for h in range(8):
                    pt, tok = nps()
                    P.mm(pt[0:96, 0:tw], [(wkp[:, c, 96 * h:96 * h + 96], ckvn[:, c, 0:tw]) for c in range(2)] + [(sh16[:, :], kpe16[0:32, 0:tw])], r=['ckvn0', 'ckvn1', 'kpe16', 'sh16', 'wres'], w=[tok])
                    head_norm_rope(pt, tok, 36 + l, rope, KT, h, t0, tw)
                if STOP == 5:
                    P.barrier()
                    return
                for i in range(tw // 128):
                    bb = i % 2
                    pv, tv = nps()
                    P.mm(pv[:, 0:512], [(ckvn[:, c, 128 * i:128 * i + 128], wvv[:, c, :]) for c in range(2)], r=['ckvn0', 'ckvn1', 'wres'], w=[tv])
                    P.op('act', _mk('activation', out=v16[bb][:, :, 0:64], in_=pv[:, 0:512].rearrange('p (h e) -> p h e', h=8), func=AF.Copy), r=[tv], w=['v16_%d' % bb])
                    P.dma('pool', VV[t0 + 128 * i:t0 + 128 * i + 128, :, :], v16[bb], r=['v16_%d' % bb], w=['VVd'])
            P.barrier()

        def phase_hgrn(l):
            FA.reset()
            BA.reset()
            BW.reset()
            chains = [(dd, pr) for dd in range(2) for pr in range(2)]
            qt = {}
            kt = {}
            kh = {}
            vh = {}
            eb = {}
            osb = {}
            a16 = {}
            for ci, ch in enumerate(chains):
                qt[ch] = BW.take(512)
                kt[ch] = BW.take(512)
                kh[ch] = BW.take(2048, 'p (n c) -> p n c', n=16, parts=32)
                vh[ch] = BW.take(2048, 'p (n c) -> p n c', n=16, parts=32)
                eb[ch] = FA.take(16)
                osb[ch] = FA.take(512)
                a16[ch] = BA.take(64, 'p (h c) -> p h c', h=2, parts=32)
            P.op('dve', _mk('memset', S32[:], 0.0), w=['S32_%d' % i for i in range(4)])
            P.op('pool', _mk('memset', S16[:], 0.0), w=['S16_%d' % i for i in range(4)])
            order = {0: [TILES[16]] + TILES[0:16], 1: [TILES[16]] + TILES[15::-1]}
            for step in range(17):
                for ci, ch in enumerate(chains):
                    dd, pr = ch
                    t0, tw, m = order[dd][step]
                    nch = tw // CH
                    c0 = 'c%d' % ci
                    P.dma('sp', qt[ch][:, 0:tw], HQ[dd, 128 * pr:128 * pr + 128, t0:t0 + tw], w=[c0 + 'q'])
                    P.dma('sp', kt[ch][:, 0:tw], HK[dd, 128 * pr:128 * pr + 128, t0:t0 + tw], w=[c0 + 'k'])
                    P.dma('sp', kh[ch][:, 0:nch, :], KH[t0:t0 + tw, dd * 256 + pr * 128:dd * 256 + pr * 128 + 128].rearrange('(n s) c -> s n c', s=CH), w=[c0 + 'kh'])
                    P.dma('sp', vh[ch][:, 0:nch, :], VH[t0:t0 + tw, pr * 128:pr * 128 + 128].rearrange('(n s) c -> s n c', s=CH), w=[c0 + 'vh'])
                    P.dma('sp', eb[ch][:, 0:nch], EBE[dd, pr, :, t0 // CH:t0 // CH + nch], w=[c0 + 'eb'])
                nchs = order[0][step][1] // CH
                for cidx in range(nchs):
                    for ci, ch in enumerate(chains):
                        dd, pr = ch
                        t0, tw, m = order[dd][step]
                        nch = tw // CH
                        c0 = 'c%d' % ci
                        bank = ps[ci]
                        btok = 'ps%d' % ci
                        obank = ps[4 + ci]
                        otok = 'ps%d' % (4 + ci)
                        msk = maskf if dd == 0 else maskb
                        s32 = S32[:, 64 * ci:64 * ci + 64]
                        s16 = S16[:, 64 * ci:64 * ci + 64]
                        cc = cidx if dd == 0 else nch - 1 - cidx
                        first = cidx == 0
                        sl = slice(CH * cc, CH * cc + CH)
                        for hh in range(2):
                            pb = 64 * hh
                            P.mm(bank[pb:pb + 64, 0:64], [(kh[ch][0:32, cc, pb:pb + 64], vh[ch][0:32, cc, pb:pb + 64])], r=[c0 + 'kh', c0 + 'vh'], w=[btok + 'U'])
                            P.mm(bank[0:32, 64 + 32 * hh:96 + 32 * hh], [(kt[ch][pb:pb + 64, sl], qt[ch][pb:pb + 64, sl])], r=[c0 + 'k', c0 + 'q'], w=[btok + 'A%d' % hh])
                            P.op('dve', _mk('tensor_tensor', out=a16[ch][0:32, hh, :], in0=bank[0:32, 64 + 32 * hh:96 + 32 * hh], in1=msk[:, :], op=ALU.mult), r=[btok + 'A%d' % hh, 'maskf', 'maskb'], w=[c0 + 'a%d' % hh])
                            P.mm(obank[pb:pb + 64, sl], [(s16[pb:pb + 64, :], qt[ch][pb:pb + 64, sl]), (vh[ch][0:32, cc, pb:pb + 64], a16[ch][0:32, hh, :])], r=['S16_%d' % ci, c0 + 'q', c0 + 'vh', c0 + 'a%d' % hh], w=[otok] if first else [otok + 'x'])
                        P.op('dve', _mk('scalar_tensor_tensor', out=s32, in0=s32, scalar=eb[ch][:, cc:cc + 1], in1=bank[:, 0:64], op0=ALU.mult, op1=ALU.add), r=[btok + 'U', c0 + 'eb', 'S32_%d' % ci], w=['S32_%d' % ci])
                        P.op('act', _mk('activation', out=s16, in_=s32, func=AF.Copy), r=['S32_%d' % ci], w=['S16_%d' % ci])
                for ci, ch in enumerate(chains):
                    dd, pr = ch
                    t0, tw, m = order[dd][step]
                    c0 = 'c%d' % ci
                    obank = ps[4 + ci]
                    otok = 'ps%d' % (4 + ci)
                    P.op('act', _mk('activation', out=osb[ch][:, 0:tw], in_=obank[:, 0:tw], func=AF.Copy), r=[otok, otok + 'x'], w=[c0 + 'o'])
                    P.dma('pool', OFB[dd, 128 * pr:128 * pr + 128, t0:t0 + tw], osb[ch][:, 0:tw], r=[c0 + 'o'], w=['OFBd'])
            P.barrier()
            FA.reset()
            BA.reset()
            of = [FA.take(512) for _ in range(2)]
            obb = [FA.take(512) for _ in range(2)]
            rs = FA.take(512)
            gt = [BA.take(512) for _ in range(2)]
            sq = BA.take(512)
            mo = [BA.take(512) for _ in range(2)]
            k = 0
            for t0, tw, m in TILES:
                for pr in range(2):
                    b = k % 2
                    k += 1
                    P.dma('sp', of[b][:, 0:tw], OFB[0, 128 * pr:128 * pr + 128, t0:t0 + tw], w=['of%d' % b])
                    P.dma('sp', obb[b][:, 0:tw], OFB[1, 128 * pr:128 * pr + 128, t0:t0 + tw], w=['ob%d' % b])
                    P.dma('sp', gt[b][:, 0:tw], GT[128 * pr:128 * pr + 128, t0:t0 + tw], w=['gt%d' % b])
                    P.op('dve', _mk('tensor_tensor', out=of[b][:, 0:tw], in0=of[b][:, 0:tw], in1=obb[b][:, 0:tw], op=ALU.add), r=['ob%d' % b], w=['of%d' % b])
                    P.op('act', _mk('activation', out=sq[:, 0:tw], in_=of[b][:, 0:tw], func=AF.Square), r=['of%d' % b], w=['sq'])
                    pt, tok = (ps[k % 4], 'ps%d' % (k % 4))
                    P.mm(pt[:, 0:tw], [(bd16[:], sq[:, 0:tw])], r=['sq', 'bd16'], w=[tok])
                    rstd_from(pt, tok, rs, 'rs', tw, 64.0)
                    P.op('dve', _mk('scalar_tensor_tensor', out=of[b][:, 0:tw], in0=of[b][:, 0:tw], scalar=gains[:, l:l + 1], in1=rs[:, 0:tw], op0=ALU.mult, op1=ALU.mult), r=['rs', 'gains'], w=['of%d' % b])
                    P.op('dve', _mk('tensor_tensor', out=mo[b][:, 0:tw], in0=of[b][:, 0:tw], in1=gt[b][:, 0:tw], op=ALU.mult), r=['of%d' % b, 'gt%d' % b], w=['mo%d' % b])
                    P.dma('pool', mixT[128 * pr:128 * pr + 128, t0:t0 + tw], mo[b][:, 0:tw], r=['mo%d' % b], w=['mixd'])
            P.barrier()

        def phase_attn(l, with_ctx):
            FA.reset()
            BA.reset()
            BW.reset()
            ktb = [BW.take(T, parts=96) for _ in range(2)]
            vtb = [BW.take(66 * 65, 'p (k e) -> p k e', e=65) for _ in range(2)]
            qtb = [BA.take(512, parts=96) for _ in range(2)]
            pb16 = [BA.take(512) for _ in range(4)]
            mo = [BA.take(512, parts=64) for _ in range(2)]
            osb = [FA.take(512, parts=64) for _ in range(2)]
            rden = FA.take(512)
            scale = 96.0 ** (-0.5)
            qi = 0
            pi = 0
            for h in range(8):
                hb = h % 2
                P.dma('sp', ktb[hb][:, :], KT[h, :, :], w=['kt%d' % hb])
                P.dma('sp', vtb[hb][:, :, :], VV[:, h, :].rearrange('(k p) e -> p k e', p=128), w=['vt%d' % hb])
                qtiles = [(t0, tw, list(range(66))) for t0, tw, m in TILES if m == 0]
                if with_ctx:
                    qtiles.append((TL, TC, [64, 65]))
                for t0, tw, kcs in qtiles:
                    qb = qi % 2
                    qi += 1
                    P.dma('sp', qtb[qb][:, 0:tw], QT[h, :, t0:t0 + tw], w=['q%d' % qb])
                    po = ps[4 + qb]
                    tpo = 'ps%d' % (4 + qb)
                    for n, kc in enumerate(kcs):
                        sp_ = ps[pi % 4]
                        tsp = 'ps%d' % (pi % 4)
                        pbuf = pb16[pi % 4]
                        tpb = 'pb%d' % (pi % 4)
                        pi += 1
                        P.mm(sp_[:, 0:tw], [(ktb[hb][:, 128 * kc:128 * kc + 128], qtb[qb][:, 0:tw])], r=['kt%d' % hb, 'q%d' % qb], w=[tsp])
                        P.op('act', _mk('activation', out=pbuf[:, 0:tw], in_=sp_[:, 0:tw], func=AF.Exp, scale=scale), r=[tsp], w=[tpb])
                        first = n == 0
                        last = n == len(kcs) - 1
                        P.op('pe', _mk('matmul', po[0:65, 0:tw], vtb[hb][:, kc, :], pbuf[:, 0:tw], start=first, stop=last), r=[tpb, 'vt%d' % hb], w=[tpo] if first else [tpo + 'x'], signal=True)
                    P.op('dve', _mk('reciprocal', out=rden[64:65, 0:tw], in_=po[64:65, 0:tw]), r=[tpo, tpo + 'x'], w=['rden'])
                    P.op('act', _mk('activation', out=osb[qb][0:64, 0:tw], in_=po[0:64, 0:tw], func=AF.Copy), r=[tpo, tpo + 'x'], w=['osb%d' % qb])
                    P.mm(ps[6][0:64, 0:tw], [(onesF[64:65, 0:64], rden[64:65, 0:tw])], r=['rden', 'onesF'], w=['ps6'])
                    P.op('dve', _mk('tensor_tensor', out=mo[qb][0:64, 0:tw], in0=osb[qb][0:64, 0:tw], in1=ps[6][0:64, 0:tw], op=ALU.mult), r=['osb%d' % qb, 'ps6'], w=['mo%d' % qb])
                    P.dma('pool', mixT[256 + 64 * h:256 + 64 * h + 64, t0:t0 + tw], mo[qb][0:64, 0:tw], r=['mo%d' % qb], w=['mixd'])
            P.barrier()

        def phase_pool(l, with_ctx):
            FA.reset()
            BA.reset()
            BW.reset()
            pw = BW.take(256, 'p (c n) -> p c n', c=2)
            stg = FA.take(256, 'p (c n) -> p c n', c=2)
            for ch in range(2):
                P.dma('sp', stg[:, ch, :], I['pwb'][l, ch, :, :], w=['stg'])
            P.op('pool', _mk('tensor_copy', pw, stg), r=['stg'], w=['wres'])
            W = 512 + 16
            ub = [FA.take(2 * W, 'p (c t) -> p c t', c=2) for _ in range(2)]
            a1 = FA.take(2 * W, 'p (c t) -> p c t', c=2)
            a2 = FA.take(2 * W, 'p (c t) -> p c t', c=2)
            a3 = FA.take(W)
            a4 = FA.take(W)
            sm = FA.take(1024, 'p (c t) -> p c t', c=2)
            pl16 = BA.take(1024, 'p (c t) -> p c t', c=2)
            mo = [BA.take(512) for _ in range(4)]
            k = 0
            for ti, (t0, tw, m) in enumerate(TILES):
                if m == 1 and (not with_ctx):
                    continue
                b = ti % 2
                U = ub[b]
                seq0, seq1 = (0, TL) if m == 0 else (TL, T)
                lo = max(t0 - 8, seq0)
                hi = min(t0 + tw + 8, seq1)
                if lo > t0 - 8 or hi < t0 + tw + 8:
                    P.op('pool', _mk('memset', U[:, :, :], 0.0), w=['ub%d' % b])
                P.dma('sp', U[:, :, lo - (t0 - 8):hi - (t0 - 8)], uT[:, lo:hi].rearrange('(c p) t -> p c t', p=128), w=['ub%d' % b])
                n1 = tw + 15
                P.op('dve', _mk('tensor_tensor', out=a1[:, :, 0:n1], in0=U[:, :, 0:n1], in1=U[:, :, 1:n1 + 1], op=ALU.add), r=['ub%d' % b], w=['a1'])
                n2 = tw + 13
                P.op('dve', _mk('tensor_tensor', out=a2[:, :, 0:n2], in0=a1[:, :, 0:n2], in1=a1[:, :, 2:n2 + 2], op=ALU.add), r=['a1'], w=['a2'])
                n3 = tw + 9
                P.op('dve', _mk('tensor_tensor', out=a3[:, 0:n3], in0=a2[:, 1, 0:n3], in1=a2[:, 1, 4:n3 + 4], op=ALU.add), r=['a2'], w=['a3'])
                n4 = tw + 1
                P.op('dve', _mk('tensor_tensor', out=a4[:, 0:n4], in0=a3[:, 0:n4], in1=a3[:, 8:n4 + 8], op=ALU.add), r=['a3'], w=['a4'])
                P.op('dve', _mk('tensor_scalar', out=sm[0:64, 0, 0:tw], in0=a1[0:64, 0, 7:7 + tw], scalar1=0.5, scalar2=None, op0=ALU.mult), r=['a1'], w=['sm0a'])
                P.op('dve', _mk('tensor_scalar', out=sm[64:128, 0, 0:tw], in0=a2[64:128, 0, 6:6 + tw], scalar1=0.25, scalar2=None, op0=ALU.mult), r=['a2'], w=['sm0b'])
                P.op('dve', _mk('tensor_scalar', out=sm[0:64, 1, 0:tw], in0=a3[0:64, 4:4 + tw], scalar1=0.125, scalar2=None, op0=ALU.mult), r=['a3'], w=['sm1a'])
                P.op('dve', _mk('tensor_scalar', out=sm[64:128, 1, 0:tw], in0=a4[64:128, 0:tw], scalar1=0.0625, scalar2=None, op0=ALU.mult), r=['a4'], w=['sm1b'])
                smt = ['sm0a', 'sm0b', 'sm1a', 'sm1b']
                if t0 == seq0:
                    P.op('dve', _mk('tensor_tensor', out=sm[:, :, 0:8], in0=sm[:, :, 0:8], in1=corr[:, 0:32].rearrange('p (c t) -> p c t', c=2)[:, :, 0:8], op=ALU.mult), r=smt + ['corr'], w=smt)
                if t0 + tw == seq1:
                    P.op('dve', _mk('tensor_tensor', out=sm[:, :, tw - 8:tw], in0=sm[:, :, tw - 8:tw], in1=corr[:, 0:32].rearrange('p (c t) -> p c t', c=2)[:, :, 8:16], op=ALU.mult), r=smt + ['corr'], w=smt)
                P.op('dve', _mk('tensor_tensor', out=pl16[:, :, 0:tw], in0=sm[:, :, 0:tw], in1=U[:, :, 8:8 + tw], op=ALU.subtract), r=smt + ['ub%d' % b], w=['pl16'])
                for ch in range(2):
                    pt, tok = (ps[k % 4], 'ps%d' % (k % 4))
                    mb = mo[k % 4]
                    mtok = 'mo%d' % (k % 4)
                    k += 1
                    P.mm(pt[:, 0:tw], [(pw[:, ch, :], pl16[:, ch, 0:tw])], r=['pl16', 'wres'], w=[tok])
                    P.op('act', _mk('activation', out=mb[:, 0:tw], in_=pt[:, 0:tw], func=AF.Identity, scale=gains[:, 24 + 2 * l + ch:25 + 2 * l + ch]), r=[tok, 'gains'], w=[mtok])
                    P.dma('pool', mixT[768 + 128 * ch:768 + 128 * ch + 128, t0:t0 + tw], mb[:, 0:tw], r=[mtok], w=['mixd'])
            P.barrier()
        steps = [load_consts, phase_mod, phase_in_transpose]
        for l in range(n_layers):
            last = l == NL - 1
            steps += [
                (lambda l=l: phase_norm(l, 0)),
                (lambda l=l: phase_ffn(l, 0, I['f1i'], I['f1o'])),
                (lambda l=l: phase_norm(l, 1)),
                (lambda l=l: phase_inproj(l)),
                (lambda l=l: phase_hgrn(l)),
                (lambda l=l, last=last: phase_attn(l, not last)),
                (lambda l=l, last=last: phase_pool(l, not last)),
                (lambda l=l: phase_outproj(l)),
                (lambda l=l: phase_norm(l, 2)),
                (lambda l=l: phase_ffn(l, 2, I['f2i'], I['f2o'])),
            ]
        steps.append(phase_out_transpose)
        for si, fn in enumerate(steps):
            if upto is not None and si >= upto:
                break
            try:
                fn()
            except _Stop:
                break
        P.barrier()
        P.emit()
    return nc

def host_constants():
    c = {}
    c['ident'] = np.eye(128, dtype=np.float32)
    s = np.arange(128)[:, None]
    t = np.arange(128)[None, :]
    same = s // CH == t // CH
    c['d1f'] = (same & (s > t)).astype(np.float32)
    c['d1b'] = (same & (s < t)).astype(np.float32)
    c['trf'] = (same & (s <= t)).astype(np.float32)
    c['trb'] = (same & (s >= t)).astype(np.float32)
    s2 = np.arange(CH)[:, None]
    t2 = np.arange(CH)[None, :]
    c['maskf'] = (s2 <= t2).astype(np.float32)
    c['maskb'] = (s2 >= t2).astype(np.float32)
    c['bd64'] = (s // 64 == t // 64).astype(np.float32)
    rot = np.zeros((96, 96), np.float32)
    for i in range(16):
        rot[80 + i, 64 + i] = -1.0
        rot[64 + i, 80 + i] = 1.0
    c['rot'] = rot
    sh = np.zeros((32, 96), np.float32)
    sh[np.arange(32), 64 + np.arange(32)] = 1.0
    c['shift'] = sh
    pos = np.arange(TL)
    row = (pos // 64).astype(np.float32)
    col = (pos % 64).astype(np.float32)
    inv = (np.float32(10000.0) ** (-np.arange(8, dtype=np.float32) / np.float32(8))).astype(np.float32)
    ang = np.concatenate([row[:, None] * inv[None, :], col[:, None] * inv[None, :]], axis=1).astype(np.float32)
    cs = np.cos(ang).astype(np.float32).T
    sn = np.sin(ang).astype(np.float32).T
    cosT = np.ones((96, TL), np.float32)
    sinT = np.zeros((96, TL), np.float32)
    cosT[64:80] = cs
    cosT[80:96] = cs
    sinT[64:80] = sn
    sinT[80:96] = sn
    c['cosT'] = cosT
    c['sinT'] = sinT
    corr = np.ones((128, 2, 16), np.float32)
    for g, w in enumerate((2, 4, 8, 16)):
        ch, p0 = (g // 2, g % 2 * 64)
        for i in range(8):
            lo = max(i - w // 2, 0)
            hi = i + w - 1 - w // 2
            corr[p0:p0 + 64, ch, i] = w / float(hi - lo + 1)
            d = 7 - i
            hi2 = min(w - 1 - w // 2, d)
            cnt = hi2 + w // 2 + 1
            corr[p0:p0 + 64, ch, 8 + i] = w / float(cnt)
    c['corr'] = corr.reshape(128, 32)
    return c
_CACHE = {}

def prep_inputs(inputs, b):
    f = lambda a: np.ascontiguousarray(np.asarray(a, dtype=np.float32))
    m = {}
    m['x_in'] = f(inputs['x'][b])
    m['ctx_in'] = f(inputs['ctx'][b])
    cv = np.stack([np.asarray(inputs['c'][b]), np.asarray(inputs['c_ctx'])], 0)
    m['cvT'] = f(cv.reshape(2, 8, 128).transpose(2, 1, 0).reshape(128, 16))
    m['w_mod'] = f(inputs['w_mod'])
    m['b_mod'] = f(inputs['b_mod'])
    m['f1i'] = f(inputs['ffn1_w_in'])
    m['f1o'] = f(inputs['ffn1_w_out'])
    m['f2i'] = f(inputs['ffn2_w_in'])
    m['f2o'] = f(inputs['ffn2_w_out'])
    m['w_in'] = f(inputs['w_in'])
    m['w_out'] = f(inputs['w_out'])
    lb = np.asarray(inputs['hg_lb_logits'], np.float32)
    m['lblB'] = f(np.broadcast_to(lb.reshape(1, NL * 512), (128, NL * 512)))
    m['lblF'] = f(lb.reshape(NL, 2, 2, 128).transpose(3, 0, 1, 2).reshape(128, 16))
    m['g_hg'] = f(np.tile(np.asarray(inputs['hg_out_gain'], np.float32), (1, 2)).T)
    m['g_qa'] = f(np.asarray(inputs['mla_q_a_gain'], np.float32).reshape(NL, 3, 128).transpose(2, 0, 1).reshape(128, NL * 3))
    m['g_kva'] = f(np.asarray(inputs['mla_kv_a_gain'], np.float32).reshape(NL, 2, 128).transpose(2, 0, 1).reshape(128, NL * 2))
    m['g_q'] = f(np.asarray(inputs['mla_q_gain'], np.float32).T)
    m['g_k'] = f(np.asarray(inputs['mla_k_gain'], np.float32).T)
    m['g_ps'] = f(np.asarray(inputs['pool_scale'], np.float32).reshape(NL, 2, 128).transpose(2, 0, 1).reshape(128, NL * 2))
    m['w_uq'] = f(inputs['mla_w_uq'])
    wukv = np.asarray(inputs['mla_w_ukv'], np.float32).reshape(NL, 256, 8, 128)
    wk = np.zeros((NL, 256, 8, 96), np.float32)
    wk[..., 0:64] = wukv[..., 0:64]
    m['wk_pad'] = f(wk.reshape(NL, 256, 768))
    m['wv'] = f(wukv[..., 64:128].reshape(NL, 256, 512))
    pw = np.asarray(inputs['pool_w'], np.float32)
    pwb = np.zeros((NL, 2, 128, 128), np.float32)
    for ch in range(2):
        pwb[:, ch, 0:64, 0:64] = pw[:, 2 * ch]
        pwb[:, ch, 64:128, 64:128] = pw[:, 2 * ch + 1]
    m['pwb'] = pwb
    return m

def kernel(**inputs):
    if 'nc' not in _CACHE:
        _CACHE['nc'] = build()
        _CACHE['consts'] = host_constants()
    nc = _CACHE['nc']
    in_maps = []
    per_b = [prep_inputs(inputs, b) for b in range(4)]
    for core in range(8):
        mm = dict(per_b[core % 4])
        mm.update(_CACHE['consts'])
        in_maps.append(mm)
    res = run_bass_kernel_spmd(nc, in_maps, core_ids=list(range(8)))
    out = np.stack([np.asarray(res.results[b]['out'], np.float32) for b in range(4)], 0)
    return out
```

```python
import numpy as np
import ml_dtypes
from contextlib import ExitStack
import concourse.bass as bass
import concourse.mybir as mybir
from concourse.bass_utils import run_bass_kernel_spmd
F32 = mybir.dt.float32
BF16 = mybir.dt.bfloat16
AF = mybir.ActivationFunctionType
ALU = mybir.AluOpType
D = 1024
TL = 8192
TC = 256
T = TL + TC
DFF = 2816
NL = 4
EPS = 1e-06
INW = 2208
CH = 32
NCHUNK = T // CH
ENGS = ('pe', 'act', 'dve', 'pool', 'sp')


def _mk(method, *args, **kwargs):
    return lambda e: getattr(e, method)(*args, **kwargs)


class Ev:
    __slots__ = ('sem', 'val')

    def __init__(self, sem, val):
        self.sem = sem
        self.val = val

class Prog:

    def __init__(self, nc, n_dma_sems=12):
        self.nc = nc
        self.ops = {e: [] for e in ENGS}
        self.cnt = {e: 0 for e in ENGS}
        self.known = {e: {} for e in ENGS}
        self.last_w = {}
        self.readers = {}
        self.n_dma_sems = n_dma_sems
        self.dma_uses = {}
        self.dma_rr = {e: 0 for e in ENGS}
        self.pending = {e: [] for e in ENGS}

    def _need(self, eng, ev, waits):
        if ev is None:
            return
        if ev.val is None:
            if ev.sem == ('eng', 'pe') and eng == 'pe':
                return
            raise RuntimeError('wait on unresolved event')
        k = self.known[eng]
        if k.get(ev.sem, 0) >= ev.val:
            return
        if ev.sem == ('eng', 'pe') and eng == 'pe':
            return
        k[ev.sem] = ev.val
        waits[ev.sem] = max(waits.get(ev.sem, 0), ev.val)

    def _deps(self, eng, r, w):
        waits = {}
        for t in r:
            self._need(eng, self.last_w.get(t), waits)
        for t in w:
            self._need(eng, self.last_w.get(t), waits)
            for ev in self.readers.get(t, ()):
                self._need(eng, ev, waits)
        return waits

    def _commit(self, ev, r, w):
        for t in r:
            self.readers.setdefault(t, []).append(ev)
        for t in w:
            self.last_w[t] = ev
            self.readers[t] = []

    def op(self, eng, fn, r=(), w=(), signal=True):
        w = list(w) + ['bank' + t[2] for t in list(r) + list(w) if t.startswith('ps') and t[2:3].isdigit()]
        waits = self._deps(eng, r, w)
        if signal:
            self.cnt[eng] += 1
            ev = Ev(('eng', eng), self.cnt[eng])
            for p in self.pending[eng]:
                p.val = ev.val
            self.pending[eng] = []
        else:
            ev = Ev(('eng', eng), None)
            self.pending[eng].append(ev)
        self.ops[eng].append((fn, waits, ('eng', eng) if signal else None, 1))
        self._commit(ev, r, w)
        return ev

    def mm(self, out, pairs, r=(), w=()):
        n = len(pairs)

        def mk(i, lhsT, rhs):
            return _mk('matmul', out, lhsT, rhs, start=i == 0, stop=i == n - 1)
        ev = None
        for i, (lhsT, rhs) in enumerate(pairs):
            ev = self.op('pe', mk(i, lhsT, rhs), r=r if i == 0 else (), w=w if i == 0 else (), signal=i == n - 1)
        return ev

    def dma(self, eng, out, in_, r=(), w=(), **kw):
        waits = self._deps(eng, r, w)
        idx = self.dma_rr[eng]
        self.dma_rr[eng] = (idx + 1) % self.n_dma_sems
        key = ('dma', eng, idx)
        uses = self.dma_uses.get(key, 0)
        if uses > 0:
            k = self.known[eng]
            if k.get(key, 0) < 16 * uses:
                k[key] = 16 * uses
                waits[key] = max(waits.get(key, 0), 16 * uses)
        self.dma_uses[key] = uses + 1
        ev = Ev(key, 16 * (uses + 1))
        self.ops[eng].append((_mk('dma_start', out=out, in_=in_, **kw), waits, key, 16))
        self._commit(ev, r, w)
        return ev

    def barrier(self, final=False):
        for e in ENGS:
            waits = {}
            k = self.known[e]
            for e2 in ENGS:
                v = self.cnt[e2]
                key = ('eng', e2)
                if v > 0 and k.get(key, 0) < v:
                    k[key] = v
                    waits[key] = v
            for key, uses in self.dma_uses.items():
                v = 16 * uses
                if k.get(key, 0) < v:
                    k[key] = v
                    waits[key] = v
            self.ops[e].append((None, waits, None, 0))
        self.last_w.clear()
        self.readers.clear()

    def emit(self):
        nc = self.nc
        handles = {'pe': 'tensor', 'act': 'scalar', 'dve': 'vector', 'pool': 'gpsimd', 'sp': 'sync'}
        with ExitStack() as st:
            sems = {}
            for e in ENGS:
                sems['eng', e] = st.enter_context(nc.semaphore('s_' + e))
            for key in self.dma_uses:
                sems[key] = st.enter_context(nc.semaphore('d_%s_%d' % (key[1], key[2])))
            block = st.enter_context(nc.Block())

            def run(e):

                def body(h):
                    for fn, waits, sig, inc in self.ops[e]:
                        for s, v in waits.items():
                            h.wait_ge(sems[s], v)
                        if fn is not None:
                            ins = fn(h)
                            if sig is not None:
                                ins.then_inc(sems[sig], inc)
                return body
            for e in ENGS:
                getattr(block, handles[e])(run(e))

class Arena:

    def __init__(self, t, n):
        self.t = t
        self.n = n
        self.off = 0

    def reset(self):
        self.off = 0

    def take(self, size, pat=None, parts=128, **kw):
        assert self.off + size <= self.n, (self.off, size, self.n)
        v = self.t[0:parts, self.off:self.off + size]
        self.off += size
        if pat:
            v = v.rearrange(pat, **kw)
        return v
TILES = [(i * 512, 512, 0) for i in range(TL // 512)] + [(TL, TC, 1)]

import os
STOP = int(os.environ.get('KSTOP', '99'))
STOP2 = int(os.environ.get('KSTOP2', '0'))


class _Stop(Exception):
    pass


def build(n_layers=NL, debug=(), upto=None):
    nc = bass.Bass('TRN2', target_bir_lowering=False)

    def din(name, shape, dt=F32):
        return nc.dram_tensor(name, list(shape), dt, kind='ExternalInput').ap()

    def dscr(name, shape, dt=F32):
        if name in debug:
            return nc.dram_tensor(name, list(shape), dt, kind='ExternalOutput').ap()
        return nc.dram_tensor(name, list(shape), dt).ap()
    I = {}
    for name, shape in [('x_in', (TL, D)), ('ctx_in', (TC, D)), ('cvT', (128, 16)), ('w_mod', (NL, D, 9 * D)), ('b_mod', (NL, 9 * D)), ('f1i', (NL, D, 2 * DFF)), ('f1o', (NL, DFF, D)), ('f2i', (NL, D, 2 * DFF)), ('f2o', (NL, DFF, D)), ('w_in', (NL, D, INW)), ('w_out', (NL, D, D)), ('lblB', (128, NL * 512)), ('lblF', (128, 16)), ('g_hg', (128, NL)), ('g_qa', (128, NL * 3)), ('g_kva', (128, NL * 2)), ('g_q', (96, NL)), ('g_k', (96, NL)), ('g_ps', (128, NL * 2)), ('w_uq', (NL, 384, 768)), ('wk_pad', (NL, 256, 768)), ('wv', (NL, 256, 512)), ('pwb', (NL, 2, 128, 128)), ('ident', (128, 128)), ('d1f', (128, 128)), ('d1b', (128, 128)), ('trf', (128, 128)), ('trb', (128, 128)), ('maskf', (32, 32)), ('maskb', (32, 32)), ('bd64', (128, 128)), ('rot', (96, 96)), ('shift', (32, 96)), ('cosT', (96, TL)), ('sinT', (96, TL)), ('corr', (128, 32))]:
        I[name] = din(name, shape)
    out = nc.dram_tensor('out', [TL, D], F32, kind='ExternalOutput').ap()
    xT = dscr('xT', (D, T))
    hT = dscr('hT', (D, T), BF16)
    mixT = dscr('mixT', (D, T), BF16)
    QT = dscr('QT', (8, 96, T), BF16)
    KT = dscr('KT', (8, 96, T), BF16)
    VV = dscr('VV', (T, 8, 65), BF16)
    uT = dscr('uT', (256, T))
    KH = dscr('KH', (T, 512), BF16)
    VH = dscr('VH', (T, 256), BF16)
    HQ = dscr('HQ', (2, 256, T), BF16)
    HK = dscr('HK', (2, 256, T), BF16)
    EBE = dscr('EBE', (2, 2, 128, NCHUNK))
    GT = dscr('GT', (256, T), BF16)
    OFB = dscr('OFB', (2, 256, T))
    MODD = dscr('MODD', (128, NL * 144))
    with ExitStack() as st:

        def sb(n, s, d=F32):
            return st.enter_context(nc.sbuf_tensor(n, list(s), d))
        NBW, NBA, NFA = (36000, 20480, 13312)
        BWt = sb('BW', (128, NBW), BF16)
        BAt = sb('BA', (128, NBA), BF16)
        FAt = sb('FA', (128, NFA), F32)
        BW, BA, FA = (Arena(BWt, NBW), Arena(BAt, NBA), Arena(FAt, NFA))
        ps = [st.enter_context(nc.psum_tensor('ps%d' % i, [128, 512], F32)) for i in range(8)]
        MOD = sb('MOD', (128, NL * 144))
        SC1 = sb('SC1', (128, NL * 48))
        GHT = sb('GHT', (128, NL * 48))
        cact = sb('cact', (128, 16))
        identF = sb('identF', (128, 128))
        d1f = sb('d1fS', (128, 128))
        d1b = sb('d1bS', (128, 128))
        trf = sb('trfS', (128, 128))
        trb = sb('trbS', (128, 128))
        maskf = sb('maskfS', (32, 32))
        maskb = sb('maskbS', (32, 32))
        bdF = sb('bdF', (128, 128))
        bd16 = sb('bd16', (128, 128), BF16)
        ones16 = sb('ones16', (128, 128), BF16)
        onesF = sb('onesF', (128, 128))
        rotF = sb('rotF', (96, 96))
        shF = sb('shF', (32, 96))
        sh16 = sb('sh16', (32, 96), BF16)
        corr = sb('corrS', (128, 32))
        epsT = sb('epsT', (128, 1))
        gains = sb('gains', (128, 64))
        lbF = sb('lbF', (128, 16))
        omlF = sb('omlF', (128, 16))
        lbtmp = sb('lbtmp', (128, 16))
        S32 = sb('S32', (128, 4 * 64))
        S16 = sb('S16', (128, 4 * 64), BF16)
        P = Prog(nc)
        ckc = [0]

        def ck():
            ckc[0] += 1
            if ckc[0] == STOP2:
                P.barrier()
                raise _Stop()

        def mod_ap(tile, l, s, c, m, n=48):
            i = ((l * (n // 16) + s) * 8 + c) * 2 + m
            return tile[:, i:i + 1]

        def load_consts():
            for dst, name in [(identF, 'ident'), (d1f, 'd1f'), (d1b, 'd1b'), (trf, 'trf'), (trb, 'trb'), (maskf, 'maskf'), (maskb, 'maskb'), (bdF, 'bd64'), (rotF, 'rot'), (shF, 'shift'), (corr, 'corr'), (cact, 'cvT'), (lbF, 'lblF')]:
                P.dma('sp', dst[:], I[name][:, :], w=[name])
            P.dma('sp', gains[:, 0:4], I['g_hg'][:, :], w=['gains'])
            P.dma('sp', gains[:, 4:16], I['g_qa'][:, :], w=['gains'])
            P.dma('sp', gains[:, 16:24], I['g_kva'][:, :], w=['gains'])
            P.dma('sp', gains[:, 24:32], I['g_ps'][:, :], w=['gains'])
            P.dma('sp', gains[0:96, 32:36], I['g_q'][:, :], w=['gains'])
            P.dma('sp', gains[0:96, 36:40], I['g_k'][:, :], w=['gains'])
            P.op('dve', _mk('memset', onesF[:], 1.0), w=['onesF'])
            P.op('dve', _mk('memset', epsT[:], EPS), w=['epsT'])
            P.op('pool', _mk('memset', ones16[:], 1.0), w=['ones16'])
            P.op('pool', _mk('tensor_copy', bd16[:], bdF[:]), r=['bd64'], w=['bd16'])
            P.op('pool', _mk('tensor_copy', sh16[:], shF[:]), r=['shift'], w=['sh16'])
            P.op('act', _mk('activation', out=cact[:], in_=cact[:], func=AF.Silu), r=['cvT'], w=['cvT'])
            L = lambda l: lbF[:, 4 * l:4 * l + 4]
            mx = lbtmp[:, 0:4]
            sm = lbtmp[:, 4:8]
            rc = lbtmp[:, 8:12]
            P.op('dve', _mk('tensor_tensor', out=mx, in0=L(0), in1=L(1), op=ALU.max), r=['lblF'], w=['lbt'])
            P.op('dve', _mk('tensor_tensor', out=mx, in0=mx, in1=L(2), op=ALU.max), r=['lbt'], w=['lbt'])
            P.op('dve', _mk('tensor_tensor', out=mx, in0=mx, in1=L(3), op=ALU.max), r=['lbt'], w=['lbt'])
            for l in range(4):
                P.op('dve', _mk('tensor_tensor', out=L(l), in0=L(l), in1=mx, op=ALU.subtract), r=['lbt', 'lblF'], w=['lblF'])
            P.op('act', _mk('activation', out=lbF[:], in_=lbF[:], func=AF.Exp), r=['lblF'], w=['lblF'])
            P.op('dve', _mk('tensor_tensor', out=sm, in0=L(0), in1=L(1), op=ALU.add), r=['lblF'], w=['lbt'])
            P.op('dve', _mk('tensor_tensor', out=sm, in0=sm, in1=L(2), op=ALU.add), r=['lbt'], w=['lbt'])
            P.op('dve', _mk('tensor_tensor', out=sm, in0=sm, in1=L(3), op=ALU.add), r=['lbt'], w=['lbt'])
            P.op('dve', _mk('reciprocal', out=rc, in_=sm), r=['lbt'], w=['lbt'])
            for l in range(4):
                P.op('dve', _mk('tensor_tensor', out=L(l), in0=L(l), in1=rc, op=ALU.mult), r=['lbt', 'lblF'], w=['lblF'])
            P.op('dve', _mk('memset', L(0), 0.0), r=['lblF'], w=['lblF'])
            P.op('dve', _mk('tensor_tensor', out=L(2), in0=L(2), in1=L(1), op=ALU.add), r=['lblF'], w=['lblF'])
            P.op('dve', _mk('tensor_tensor', out=L(3), in0=L(3), in1=L(2), op=ALU.add), r=['lblF'], w=['lblF'])
            P.op('dve', _mk('tensor_scalar', out=omlF[:], in0=lbF[:], scalar1=-1.0, scalar2=1.0, op0=ALU.mult, op1=ALU.add), r=['lblF'], w=['omlF'])
            P.barrier()

        def phase_mod():
            FA.reset()
            stg = [FA.take(4096, 'p (c n) -> p c n', c=8) for _ in range(2)]
            brow = [FA.take(512, parts=1) for _ in range(2)]
            k = 0
            for l in range(n_layers):
                for nb in range(18):
                    b = k % 2
                    P.dma('sp', stg[b], I['w_mod'][l, :, nb * 512:(nb + 1) * 512].rearrange('(c p) n -> p c n', p=128), w=['stg%d' % b])
                    P.dma('pool', brow[b], I['b_mod'][l:l + 1, nb * 512:(nb + 1) * 512], w=['brow%d' % b])
                    pt = ps[k % 4]
                    for fc in range(4):
                        pairs = [(stg[b][:, c, 128 * fc:128 * fc + 128], cact[:, 2 * c:2 * c + 2]) for c in range(8)]
                        pairs.append((brow[b][0:1, 128 * fc:128 * fc + 128], onesF[0:1, 0:2]))
                        P.mm(pt[:, 2 * fc:2 * fc + 2], pairs, r=['stg%d' % b, 'brow%d' % b, 'cvT', 'onesF'], w=['ps%d' % (k % 4)])
                    g0 = l * 144 + nb * 8
                    P.op('dve', _mk('tensor_copy', MOD[:, g0:g0 + 8], pt[:, 0:8]), r=['ps%d' % (k % 4)], w=['MOD'])
                    k += 1
            for l in range(n_layers):
                for s in range(3):
                    src = MOD[:, l * 144 + (3 * s + 1) * 16:l * 144 + (3 * s + 2) * 16]
                    dst = SC1[:, (l * 3 + s) * 16:(l * 3 + s + 1) * 16]
                    P.op('dve', _mk('tensor_scalar', out=dst, in0=src, scalar1=1.0, scalar2=None, op0=ALU.add), r=['MOD'], w=['SC1'])
                    srcg = MOD[:, l * 144 + (3 * s + 2) * 16:l * 144 + (3 * s + 3) * 16]
                    dstg = GHT[:, (l * 3 + s) * 16:(l * 3 + s + 1) * 16]
                    fac = 1.0 if s == 1 else 0.5
                    P.op('dve', _mk('tensor_scalar', out=dstg, in0=srcg, scalar1=fac, scalar2=None, op0=ALU.mult), r=['MOD'], w=['GHT'])
            if 'MODD' in debug:
                P.dma('sp', MODD[:, :], MOD[:], r=['MOD'], w=['MODD'])
            P.barrier()

        def shift_ap(l, s, c, m):
            i = l * 144 + 3 * s * 16 + c * 2 + m
            return MOD[:, i:i + 1]

        def phase_in_transpose():
            FA.reset()
            xb = [FA.take(1024) for _ in range(4)]
            xt = FA.take(4096, 'p (c t) -> p c t', c=8)
            for t0, tw, m in TILES:
                nb = tw // 128
                for i in range(nb):
                    src = I['x_in'][t0 + 128 * i:t0 + 128 * i + 128, :] if m == 0 else I['ctx_in'][128 * i:128 * i + 128, :]
                    P.dma('sp', xb[i], src, w=['xb%d' % i])
                for c in range(8):
                    for i in range(nb):
                        P.op('pe', _mk('transpose', ps[c][:, 128 * i:128 * i + 128], xb[i][:, 128 * c:128 * c + 128], identF[:]), r=['xb%d' % i, 'ident'], w=['ps%d' % c])
                    if c % 2 == 0:
                        P.op('act', _mk('activation', out=xt[:, c, 0:tw], in_=ps[c][:, 0:tw], func=AF.Copy), r=['ps%d' % c], w=['xt%d' % c])
                    else:
                        P.op('dve', _mk('tensor_copy', xt[:, c, 0:tw], ps[c][:, 0:tw]), r=['ps%d' % c], w=['xt%d' % c])
                P.dma('pool', xT[:, t0:t0 + tw].rearrange('(c p) t -> p c t', p=128), xt[:, :, 0:tw], r=['xt%d' % c for c in range(8)], w=['xT%d' % t0])
            P.barrier()

        def phase_out_transpose():
            FA.reset()
            xt = [FA.take(4096, 'p (c t) -> p c t', c=8) for _ in range(2)]
            ob = [FA.take(1024) for _ in range(4)]
            for ti, (t0, tw, m) in enumerate(TILES):
                if m == 1:
                    continue
                b = ti % 2
                P.dma('sp', xt[b][:, :, 0:tw], xT[:, t0:t0 + tw].rearrange('(c p) t -> p c t', p=128), w=['xt%d' % b])
                for i in range(tw // 128):
                    for c in range(8):
                        pt = ps[(i * 8 + c) % 8]
                        P.op('pe', _mk('transpose', pt[:, 0:128], xt[b][:, c, 128 * i:128 * i + 128], identF[:]), r=['xt%d' % b, 'ident'], w=['ps%d' % ((i * 8 + c) % 8)])
                        if c % 2 == 0:
                            P.op('act', _mk('activation', out=ob[i][:, 128 * c:128 * c + 128], in_=pt[:, 0:128], func=AF.Copy), r=['ps%d' % ((i * 8 + c) % 8)], w=['ob%d_%d' % (i, c)])
                        else:
                            P.op('dve', _mk('tensor_copy', ob[i][:, 128 * c:128 * c + 128], pt[:, 0:128]), r=['ps%d' % ((i * 8 + c) % 8)], w=['ob%d_%d' % (i, c)])
                    P.dma('pool', out[t0 + 128 * i:t0 + 128 * i + 128, :], ob[i], r=['ob%d_%d' % (i, c) for c in range(8)], w=['out%d_%d' % (t0, i)])
            P.barrier()

        def phase_norm(l, s):
            FA.reset()
            BA.reset()
            xt = [FA.take(4096, 'p (c t) -> p c t', c=8) for _ in range(2)]
            tmp = [FA.take(512) for _ in range(2)]
            rstd = FA.take(512)
            sq = BA.take(4096, 'p (c t) -> p c t', c=8)
            h = [BA.take(4096, 'p (c t) -> p c t', c=8) for _ in range(2)]
            for ti, (t0, tw, m) in enumerate(TILES):
                b = ti % 2
                X = xt[b]
                P.dma('sp', X[:, :, 0:tw], xT[:, t0:t0 + tw].rearrange('(c p) t -> p c t', p=128), w=['xt%d' % b])
                P.op('act', _mk('activation', out=sq[:, :, 0:tw], in_=X[:, :, 0:tw], func=AF.Square), r=['xt%d' % b], w=['sq'])
                P.mm(ps[0][:, 0:tw], [(ones16[:], sq[:, c, 0:tw]) for c in range(8)], r=['sq', 'ones16'], w=['ps0'])
                P.op('act', _mk('activation', out=rstd[:, 0:tw], in_=ps[0][:, 0:tw], func=AF.Sqrt, bias=epsT[:, 0:1], scale=1.0 / D), r=['ps0', 'epsT'], w=['rstd'])
                P.op('dve', _mk('reciprocal', out=rstd[:, 0:tw], in_=rstd[:, 0:tw]), r=['rstd'], w=['rstd'])
                for c in range(8):
                    tb = tmp[c % 2]
                    P.op('dve', _mk('scalar_tensor_tensor', out=tb[:, 0:tw], in0=X[:, c, 0:tw], scalar=mod_ap(SC1, l, s, c, m), in1=rstd[:, 0:tw], op0=ALU.mult, op1=ALU.mult), r=['xt%d' % b, 'rstd', 'SC1'], w=['tmp%d' % (c % 2)])
                    P.op('act', _mk('activation', out=h[b][:, c, 0:tw], in_=tb[:, 0:tw], func=AF.Identity, bias=shift_ap(l, s, c, m), scale=1.0), r=['tmp%d' % (c % 2), 'MOD'], w=['h%d_%d' % (b, c)])
                P.dma('pool', hT[:, t0:t0 + tw].rearrange('(c p) t -> p c t', p=128), h[b][:, :, 0:tw], r=['h%d_%d' % (b, c) for c in range(8)], w=['hT%d' % t0])
            P.barrier()

        def residual_update(pt, ptok, X, xtok, j, tw, l, s, m):
            P.op('dve', _mk('scalar_tensor_tensor', out=X[:, j, 0:tw], in0=pt[:, 0:tw], scalar=mod_ap(GHT, l, s, j, m), in1=X[:, j, 0:tw], op0=ALU.mult, op1=ALU.add), r=[ptok, 'GHT'], w=[xtok + '_%d' % j])

        def phase_ffn(l, s, wi, wo):
            NH = 11
            for half in range(2):
                FA.reset()
                BA.reset()
                BW.reset()
                wg = BW.take(8 * 1408, 'p (c n) -> p c n', c=8)
                wu = BW.take(8 * 1408, 'p (c n) -> p c n', c=8)
                wob = BW.take(NH * 1024, 'p (c n) -> p c n', c=NH)
                stg = [FA.take(1408) for _ in range(2)]
                k = 0
                f0 = half * 1408
                for c in range(8):
                    for dst, col0 in ((wg, f0), (wu, DFF + f0)):
                        bb = k % 2
                        P.dma('sp', stg[bb], wi[l, 128 * c:128 * c + 128, col0:col0 + 1408], w=['stg%d' % bb])
                        P.op('pool', _mk('tensor_copy', dst[:, c, :], stg[bb][:, 0:1408]), r=['stg%d' % bb], w=['wres'])
                        k += 1
                for i in range(NH):
                    bb = k % 2
                    P.dma('sp', stg[bb][:, 0:1024], wo[l, f0 + 128 * i:f0 + 128 * i + 128, :], w=['stg%d' % bb])
                    P.op('pool', _mk('tensor_copy', wob[:, i, :], stg[bb][:, 0:1024]), r=['stg%d' % bb], w=['wres'])
                    k += 1
                xt = [FA.take(4096, 'p (c t) -> p c t', c=8) for _ in range(2)]
                sg = [FA.take(512) for _ in range(2)]
                ht = [BA.take(4096, 'p (c t) -> p c t', c=8) for _ in range(2)]
                act = BA.take(NH * 512, 'p (c t) -> p c t', c=NH)
                for ti, (t0, tw, m) in enumerate(TILES):
                    b = ti % 2
                    H = ht[b]
                    X = xt[b]
                    P.dma('sp', H[:, :, 0:tw], hT[:, t0:t0 + tw].rearrange('(c p) t -> p c t', p=128), w=['ht%d' % b])
                    P.dma('sp', X[:, :, 0:tw], xT[:, t0:t0 + tw].rearrange('(c p) t -> p c t', p=128), r=['xT%d' % t0], w=['xt%d_%d' % (b, j) for j in range(8)])
                    for i in range(NH):
                        pg = ps[2 * i % 4]
                        pu = ps[(2 * i + 1) % 4]
                        tg = 'ps%d' % (2 * i % 4)
                        tu = 'ps%d' % ((2 * i + 1) % 4)
                        P.mm(pg[:, 0:tw], [(wg[:, c, 128 * i:128 * i + 128], H[:, c, 0:tw]) for c in range(8)], r=['ht%d' % b, 'wres'], w=[tg])
                        P.mm(pu[:, 0:tw], [(wu[:, c, 128 * i:128 * i + 128], H[:, c, 0:tw]) for c in range(8)], r=['ht%d' % b, 'wres'], w=[tu])
                        sgb = sg[i % 2]
                        P.op('act', _mk('activation', out=sgb[:, 0:tw], in_=pg[:, 0:tw], func=AF.Silu), r=[tg], w=['sg%d' % (i % 2)])
                        P.op('dve', _mk('tensor_tensor', out=act[:, i, 0:tw], in0=sgb[:, 0:tw], in1=pu[:, 0:tw], op=ALU.mult), r=['sg%d' % (i % 2), tu], w=['act%d' % i])
                    for j in range(8):
                        py = ps[4 + j % 2]
                        ty = 'ps%d' % (4 + j % 2)
                        P.mm(py[:, 0:tw], [(wob[:, i, 128 * j:128 * j + 128], act[:, i, 0:tw]) for i in range(NH)], r=['act%d' % i for i in range(NH)] + ['wres'], w=[ty])
                        residual_update(py, ty, X, 'xt%d' % b, j, tw, l, s, m)
                    P.dma('pool', xT[:, t0:t0 + tw].rearrange('(c p) t -> p c t', p=128), X[:, :, 0:tw], r=['xt%d_%d' % (b, j) for j in range(8)], w=['xT%d' % t0])
                P.barrier()

        def phase_outproj(l):
            FA.reset()
            BA.reset()
            BW.reset()
            wo = BW.take(8 * 1024, 'p (c n) -> p c n', c=8)
            stg = [FA.take(1024) for _ in range(2)]
            for c in range(8):
                bb = c % 2
                P.dma('sp', stg[bb], I['w_out'][l, 128 * c:128 * c + 128, :], w=['stg%d' % bb])
                P.op('pool', _mk('tensor_copy', wo[:, c, :], stg[bb][:]), r=['stg%d' % bb], w=['wres'])
            xt = [FA.take(4096, 'p (c t) -> p c t', c=8) for _ in range(2)]
            mt = [BA.take(4096, 'p (c t) -> p c t', c=8) for _ in range(2)]
            for ti, (t0, tw, m) in enumerate(TILES):
                b = ti % 2
                M = mt[b]
                X = xt[b]
                P.dma('sp', M[:, :, 0:tw], mixT[:, t0:t0 + tw].rearrange('(c p) t -> p c t', p=128), w=['mt%d' % b])
                P.dma('sp', X[:, :, 0:tw], xT[:, t0:t0 + tw].rearrange('(c p) t -> p c t', p=128), w=['xt%d_%d' % (b, j) for j in range(8)])
                for j in range(8):
                    py = ps[j % 4]
                    ty = 'ps%d' % (j % 4)
                    P.mm(py[:, 0:tw], [(wo[:, c, 128 * j:128 * j + 128], M[:, c, 0:tw]) for c in range(8)], r=['mt%d' % b, 'wres'], w=[ty])
                    residual_update(py, ty, X, 'xt%d' % b, j, tw, l, 1, m)
                P.dma('pool', xT[:, t0:t0 + tw].rearrange('(c p) t -> p c t', p=128), X[:, :, 0:tw], r=['xt%d_%d' % (b, j) for j in range(8)], w=['xT%d' % t0])
            P.barrier()

        def rstd_from(pt, ptok, dst, dtok, tw, n, parts=128):
            P.op('act', _mk('activation', out=dst[0:parts, 0:tw], in_=pt[0:parts, 0:tw], func=AF.Sqrt, bias=epsT[0:parts, 0:1], scale=1.0 / n), r=[ptok, 'epsT'], w=[dtok])
            P.op('dve', _mk('reciprocal', out=dst[0:parts, 0:tw], in_=dst[0:parts, 0:tw]), r=[dtok], w=[dtok])

        def phase_inproj(l):
            FA.reset()
            BA.reset()
            BW.reset()
            win = BW.take(8 * INW, 'p (c n) -> p c n', c=8)
            wuq = BW.take(3 * 768, 'p (c n) -> p c n', c=3)
            wkp = BW.take(2 * 768, 'p (c n) -> p c n', c=2)
            wvv = BW.take(2 * 512, 'p (c n) -> p c n', c=2)
            stg = [FA.take(INW) for _ in range(2)]
            k = 0
            for c in range(8):
                bb = k % 2
                P.dma('sp', stg[bb], I['w_in'][l, 128 * c:128 * c + 128, :], w=['stg%d' % bb])
                P.op('pool', _mk('tensor_copy', win[:, c, :], stg[bb][:]), r=['stg%d' % bb], w=['wres'])
                k += 1
            for dst, name, ncc, n in ((wuq, 'w_uq', 3, 768), (wkp, 'wk_pad', 2, 768), (wvv, 'wv', 2, 512)):
                for c in range(ncc):
                    bb = k % 2
                    P.dma('sp', stg[bb][:, 0:n], I[name][l, 128 * c:128 * c + 128, :], w=['stg%d' % bb])
                    P.op('pool', _mk('tensor_copy', dst[:, c, :], stg[bb][:, 0:n]), r=['stg%d' % bb], w=['wres'])
                    k += 1
            P.barrier()
            FA.reset()
            LBB = FA.take(512)
            OMLB = FA.take(512)
            lbl = FA.take(2048)
            tmpA = FA.take(512)
            tmpB = FA.take(512)
            P.dma('sp', lbl, I['lblB'][:, :], w=['lbl'])
            Lr = lambda i: lbl[:, 512 * i:512 * i + 512]
            P.op('dve', _mk('tensor_tensor', out=tmpA, in0=Lr(0), in1=Lr(1), op=ALU.max), r=['lbl'], w=['tA'])
            P.op('dve', _mk('tensor_tensor', out=tmpA, in0=tmpA, in1=Lr(2), op=ALU.max), r=['tA'], w=['tA'])
            P.op('dve', _mk('tensor_tensor', out=tmpA, in0=tmpA, in1=Lr(3), op=ALU.max), r=['tA'], w=['tA'])
            for i in range(4):
                P.op('dve', _mk('tensor_tensor', out=Lr(i), in0=Lr(i), in1=tmpA, op=ALU.subtract), r=['tA', 'lbl'], w=['lbl'])
            P.op('act', _mk('activation', out=lbl, in_=lbl, func=AF.Exp), r=['lbl'], w=['lbl'])
            P.op('dve', _mk('tensor_tensor', out=tmpB, in0=Lr(0), in1=Lr(1), op=ALU.add), r=['lbl'], w=['tB'])
            P.op('dve', _mk('tensor_tensor', out=tmpB, in0=tmpB, in1=Lr(2), op=ALU.add), r=['tB'], w=['tB'])
            P.op('dve', _mk('tensor_tensor', out=tmpB, in0=tmpB, in1=Lr(3), op=ALU.add), r=['tB'], w=['tB'])
            P.op('dve', _mk('reciprocal', out=tmpB, in_=tmpB), r=['tB'], w=['tB'])
            if l == 0:
                P.op('dve', _mk('memset', LBB, 0.0), w=['LBB'])
            else:
                P.op('dve', _mk('tensor_copy', LBB, Lr(1)), r=['lbl'], w=['LBB'])
                for i in range(2, l + 1):
                    P.op('dve', _mk('tensor_tensor', out=LBB, in0=LBB, in1=Lr(i), op=ALU.add), r=['lbl', 'LBB'], w=['LBB'])
                P.op('dve', _mk('tensor_tensor', out=LBB, in0=LBB, in1=tmpB, op=ALU.mult), r=['tB', 'LBB'], w=['LBB'])
            P.op('dve', _mk('tensor_scalar', out=OMLB, in0=LBB, scalar1=-1.0, scalar2=1.0, op0=ALU.mult, op1=ALU.add), r=['LBB'], w=['OMLB'])
            P.barrier()
            FA.off = 1024
            if STOP == 0:
                P.barrier()
                return
            SL = [FA.take(512) for _ in range(12)]
            ST = ['S%d' % i for i in range(12)]
            cqF = FA.take(1536, 'p (c t) -> p c t', c=3)
            ckvF = FA.take(1024, 'p (c t) -> p c t', c=2)
            rs = FA.take(512)
            csT = FA.take(512, parts=96)
            snT = FA.take(512, parts=96)
            uF = FA.take(1024, 'p (c t) -> p c t', c=2)
            ebT = FA.take(64)
            ht = [BW.take(4096, 'p (c t) -> p c t', c=8) for _ in range(2)]
            kh16 = [BA.take(512) for _ in range(4)]
            vh16 = [BA.take(256) for _ in range(4)]
            qk16 = [BA.take(512) for _ in range(4)]
            g16 = [BA.take(512) for _ in range(2)]
            sq3 = BA.take(1536, 'p (c t) -> p c t', c=3)
            cqn = BA.take(1536, 'p (c t) -> p c t', c=3)
            ckvn = BA.take(1024, 'p (c t) -> p c t', c=2)
            kpe16 = BA.take(512, parts=32)
            sqh = [BA.take(512) for _ in range(4)]
            o16 = [BA.take(512) for _ in range(4)]
            v16 = [BA.take(520, 'p (h e) -> p h e', h=8) for _ in range(2)]
            for b in range(2):
                P.op('pool', _mk('memset', v16[b][:, :, 64:65], 1.0), w=['v16_%d' % b])
            gq = lambda c: gains[:, 4 + l * 3 + c:5 + l * 3 + c]
            gkv = lambda c: gains[:, 16 + l * 2 + c:17 + l * 2 + c]
            cnt = {'ps': 0}

            def nps():
                i = cnt['ps'] % 8
                cnt['ps'] += 1
                return (ps[i], 'ps%d' % i)

            def nps4x():
                i = cnt['ps'] % 4
                cnt['ps'] += 1
                return (ps[i], 'ps%d' % i)

            def featmm(col0, ncols, H, b, tw):
                pt, tok = nps()
                P.mm(pt[0:ncols, 0:tw], [(win[:, c, col0:col0 + ncols], H[:, c, 0:tw]) for c in range(8)], r=['ht%d' % b, 'wres'], w=[tok])
                return (pt, tok)

            def heads_block(mmfn, gcol, rope, dst_dram, t0, tw):
                gsc = gains[0:96, gcol:gcol + 1]
                for h0 in (0, 4):
                    J = range(4)
                    for j in J:
                        mmfn(h0 + j, ps[j], 'ps%d' % j)
                    for j in J:
                        P.op('act', _mk('activation', out=sqh[j][0:96, 0:tw], in_=ps[j][0:96, 0:tw], func=AF.Square), r=['ps%d' % j], w=['sqh%d' % j])
                    for j in J:
                        P.mm(ps[4 + j][0:96, 0:tw], [(ones16[0:96, 0:96], sqh[j][0:96, 0:tw])], r=['sqh%d' % j, 'ones16'], w=['ps%d' % (4 + j)])
                    for j in J:
                        P.op('act', _mk('activation', out=SL[3 * j][0:96, 0:tw], in_=ps[4 + j][0:96, 0:tw], func=AF.Sqrt, bias=epsT[0:96, 0:1], scale=1.0 / 96.0), r=['ps%d' % (4 + j), 'epsT'], w=[ST[3 * j]])
                    for j in J:
                        P.op('dve', _mk('reciprocal', out=SL[3 * j][0:96, 0:tw], in_=SL[3 * j][0:96, 0:tw]), r=[ST[3 * j]], w=[ST[3 * j]])
                    for j in J:
                        P.op('dve', _mk('scalar_tensor_tensor', out=SL[3 * j + 1][0:96, 0:tw], in0=ps[j][0:96, 0:tw], scalar=gsc, in1=SL[3 * j][0:96, 0:tw], op0=ALU.mult, op1=ALU.mult), r=['ps%d' % j, ST[3 * j], 'gains'], w=[ST[3 * j + 1]])
                    if rope:
                        for j in J:
                            P.mm(ps[4 + j][0:96, 0:tw], [(rotF[:, :], SL[3 * j + 1][0:96, 0:tw])], r=[ST[3 * j + 1], 'rot'], w=['ps%d' % (4 + j)])
                        for j in J:
                            P.op('dve', _mk('tensor_tensor', out=SL[3 * j + 2][0:96, 0:tw], in0=SL[3 * j + 1][0:96, 0:tw], in1=csT[0:96, 0:tw], op=ALU.mult), r=[ST[3 * j + 1], 'cs'], w=[ST[3 * j + 2]])
                        for j in J:
                            P.op('dve', _mk('tensor_tensor', out=SL[3 * j + 1][0:96, 0:tw], in0=ps[4 + j][0:96, 0:tw], in1=snT[0:96, 0:tw], op=ALU.mult), r=['ps%d' % (4 + j), 'sn'], w=[ST[3 * j + 1]])
                        for j in J:
                            P.op('dve', _mk('tensor_tensor', out=o16[j][0:96, 0:tw], in0=SL[3 * j + 2][0:96, 0:tw], in1=SL[3 * j + 1][0:96, 0:tw], op=ALU.add), r=[ST[3 * j + 2], ST[3 * j + 1]], w=['o16_%d' % j])
                    else:
                        for j in J:
                            P.op('act', _mk('activation', out=o16[j][0:96, 0:tw], in_=SL[3 * j + 1][0:96, 0:tw], func=AF.Copy), r=[ST[3 * j + 1]], w=['o16_%d' % j])
                    for j in J:
                        P.dma('pool', dst_dram[h0 + j, :, t0:t0 + tw], o16[j][0:96, 0:tw], r=['o16_%d' % j], w=['hd'])

            for ti, (t0, tw, m) in enumerate(TILES):
                b = ti % 2
                H = ht[b]
                P.dma('sp', H[:, :, 0:tw], hT[:, t0:t0 + tw].rearrange('(c p) t -> p c t', p=128), w=['ht%d' % b])
                rope = m == 0
                if rope:
                    P.dma('sp', csT[:, 0:tw], I['cosT'][:, t0:t0 + tw], w=['cs'])
                    P.dma('sp', snT[:, 0:tw], I['sinT'][:, t0:t0 + tw], w=['sn'])
                NB = range(tw // 128)
                Fs = lambda i: SL[3 * i]
                Ks = lambda i: SL[3 * i + 1]
                Es = lambda i: SL[3 * i + 2]
                tF = lambda i: ST[3 * i]
                tK = lambda i: ST[3 * i + 1]
                tE = lambda i: ST[3 * i + 2]
                for i in NB:
                    P.mm(ps[i][:, 0:512], [(H[:, c, 128 * i:128 * i + 128], win[:, c, 256:768]) for c in range(8)], r=['ht%d' % b, 'wres'], w=['ps%d' % i])
                for i in NB:
                    P.op('act', _mk('activation', out=Fs(i), in_=ps[i][:, 0:512], func=AF.Sigmoid), r=['ps%d' % i], w=[tF(i)])
                for i in NB:
                    P.op('dve', _mk('tensor_tensor', out=Fs(i), in0=Fs(i), in1=OMLB, op=ALU.mult), r=[tF(i), 'OMLB'], w=[tF(i)])
                    P.op('dve', _mk('tensor_tensor', out=Fs(i), in0=Fs(i), in1=LBB, op=ALU.add), r=[tF(i), 'LBB'], w=[tF(i)])
                    P.op('dve', _mk('tensor_scalar', out=Ks(i), in0=Fs(i), scalar1=-1.0, scalar2=1.0, op0=ALU.mult, op1=ALU.add), r=[tF(i)], w=[tK(i)])
                    P.op('dve', _mk('tensor_scalar', out=Fs(i), in0=Fs(i), scalar1=1e-06, scalar2=1.0, op0=ALU.max, op1=ALU.min), r=[tF(i)], w=[tF(i)])
                for i in NB:
                    P.op('act', _mk('activation', out=Fs(i), in_=Fs(i), func=AF.Ln), r=[tF(i)], w=[tF(i)])
                for i in NB:
                    lf = Fs(i)
                    P.mm(ps[i][:, 0:256], [(d1f[:], lf[:, 0:256])], r=[tF(i), 'd1f'], w=['ps%d' % i])
                    P.mm(ps[i][:, 256:512], [(d1b[:], lf[:, 256:512])], r=[tF(i), 'd1b'], w=['ps%d' % i])
                    for g in range(4):
                        dd, pr = (g // 2, g % 2)
                        P.mm(ps[4 + g][:, 128 * i:128 * i + 128], [(lf[:, dd * 256 + pr * 128:dd * 256 + pr * 128 + 128], (trf if dd == 0 else trb)[:])], r=[tF(i), 'trf', 'trb'], w=['ps%d' % (4 + g)])
                for i in NB:
                    P.op('act', _mk('activation', out=Es(i), in_=ps[i][:, 0:512], func=AF.Exp), r=['ps%d' % i], w=[tE(i)])
                for i in NB:
                    P.op('dve', _mk('tensor_tensor', out=kh16[i], in0=Ks(i), in1=Es(i), op=ALU.mult), r=[tK(i), tE(i)], w=['kh16_%d' % i])
                    P.dma('pool', KH[t0 + 128 * i:t0 + 128 * i + 128, :], kh16[i], r=['kh16_%d' % i], w=['KHd'])
                for i in NB:
                    P.mm(ps[i][:, 0:256], [(H[:, c, 128 * i:128 * i + 128], win[:, c, 768:1024]) for c in range(8)], r=['ht%d' % b, 'wres'], w=['ps%d' % i])
                for i in NB:
                    P.op('act', _mk('activation', out=vh16[i], in_=ps[i][:, 0:256], func=AF.Copy), r=['ps%d' % i], w=['vh16_%d' % i])
                    P.dma('pool', VH[t0 + 128 * i:t0 + 128 * i + 128, :], vh16[i], r=['vh16_%d' % i], w=['VHd'])
                cnt['ps'] = 0
                for g in range(4):
                    dd, pr = (g // 2, g % 2)
                    pb_, tb_ = (ps[4 + g], 'ps%d' % (4 + g))
                    s0 = 3 * (g % 3)
                    eb, enb, kkf = (SL[s0], SL[s0 + 1], SL[s0 + 2])
                    teb, tenb, tkkf = (ST[s0], ST[s0 + 1], ST[s0 + 2])
                    qs, tqs = (SL[9 + pr], ST[9 + pr])
                    P.op('act', _mk('activation', out=eb[:, 0:tw], in_=pb_[:, 0:tw], func=AF.Exp), r=[tb_], w=[teb])
                    P.op('act', _mk('activation', out=enb[:, 0:tw], in_=pb_[:, 0:tw], func=AF.Exp, scale=-1.0), r=[tb_], w=[tenb])
                    if dd == 0:
                        pq, tq = nps4x()
                        P.mm(pq[:, 0:tw], [(win[:, c, 128 * pr:128 * pr + 128], H[:, c, 0:tw]) for c in range(8)], r=['ht%d' % b, 'wres'], w=[tq])
                        P.op('act', _mk('activation', out=qs[:, 0:tw], in_=pq[:, 0:tw], func=AF.Silu), r=[tq], w=[tqs])
                    qb, tqb = (qk16[2 * (g % 2)], 'qk16_%d' % (2 * (g % 2)))
                    P.op('dve', _mk('tensor_tensor', out=qb[:, 0:tw], in0=qs[:, 0:tw], in1=eb[:, 0:tw], op=ALU.mult), r=[tqs, teb], w=[tqb])
                    P.dma('pool', HQ[dd, 128 * pr:128 * pr + 128, t0:t0 + tw], qb[:, 0:tw], r=[tqb], w=['HQd'])
                    pk, tkk = nps4x()
                    c0_ = 256 + dd * 256 + 128 * pr
                    P.mm(pk[:, 0:tw], [(win[:, c, c0_:c0_ + 128], H[:, c, 0:tw]) for c in range(8)], r=['ht%d' % b, 'wres'], w=[tkk])
                    P.op('act', _mk('activation', out=kkf[:, 0:tw], in_=pk[:, 0:tw], func=AF.Sigmoid, scale=-1.0), r=[tkk], w=[tkkf])
                    kb, tkb = (qk16[2 * (g % 2) + 1], 'qk16_%d' % (2 * (g % 2) + 1))
                    gi = l * 4 + dd * 2 + pr
                    P.op('dve', _mk('scalar_tensor_tensor', out=kb[:, 0:tw], in0=kkf[:, 0:tw], scalar=omlF[:, gi:gi + 1], in1=enb[:, 0:tw], op0=ALU.mult, op1=ALU.mult), r=[tkkf, tenb, 'omlF'], w=[tkb])
                    P.dma('pool', HK[dd, 128 * pr:128 * pr + 128, t0:t0 + tw], kb[:, 0:tw], r=[tkb], w=['HKd'])
                    nch = tw // CH
                    ebv = eb[:, 0:tw].rearrange('p (n s) -> p n s', s=CH)
                    sel = ebv[:, :, CH - 1:CH] if dd == 0 else ebv[:, :, 0:1]
                    P.op('dve', _mk('tensor_copy', ebT[:, 16 * g:16 * g + nch].rearrange('p (n o) -> p n o', o=1), sel), r=[teb], w=['ebT%d' % g])
                    P.dma('pool', EBE[dd, pr, :, t0 // CH:t0 // CH + nch], ebT[:, 16 * g:16 * g + nch], r=['ebT%d' % g], w=['EBEd'])
                cnt['ps'] = 0
                for pr in range(2):
                    pg, tg = featmm(1024 + 128 * pr, 128, H, b, tw)
                    P.op('act', _mk('activation', out=g16[pr][:, 0:tw], in_=pg[:, 0:tw], func=AF.Silu), r=[tg], w=['g16_%d' % pr])
                    P.dma('pool', GT[128 * pr:128 * pr + 128, t0:t0 + tw], g16[pr][:, 0:tw], r=['g16_%d' % pr], w=['GTd'])
                for ch in range(2):
                    pu, tu = featmm(1952 + 128 * ch, 128, H, b, tw)
                    P.op('dve', _mk('tensor_copy', uF[:, ch, 0:tw], pu[:, 0:tw]), r=[tu], w=['uF%d' % ch])
                P.dma('pool', uT[:, t0:t0 + tw].rearrange('(c p) t -> p c t', p=128), uF[:, :, 0:tw], r=['uF0', 'uF1'], w=['uTd'])
                pcs = []
                for c in range(3):
                    pcs.append(featmm(1280 + 128 * c, 128, H, b, tw))
                for c in range(3):
                    pc, tc_ = pcs[c]
                    P.op('act', _mk('activation', out=sq3[:, c, 0:tw], in_=pc[:, 0:tw], func=AF.Square), r=[tc_], w=['sq3_%d' % c])
                for c in range(3):
                    pc, tc_ = pcs[c]
                    P.op('dve', _mk('tensor_copy', cqF[:, c, 0:tw], pc[:, 0:tw]), r=[tc_], w=['cqF%d' % c])
                pss, tss = nps()
                P.mm(pss[:, 0:tw], [(ones16[:], sq3[:, c, 0:tw]) for c in range(3)], r=['sq3_0', 'sq3_1', 'sq3_2', 'ones16'], w=[tss])
                rstd_from(pss, tss, rs, 'rs', tw, 384.0)
                for c in range(3):
                    P.op('dve', _mk('scalar_tensor_tensor', out=cqn[:, c, 0:tw], in0=cqF[:, c, 0:tw], scalar=gq(c), in1=rs[:, 0:tw], op0=ALU.mult, op1=ALU.mult), r=['cqF%d' % c, 'rs', 'gains'], w=['cqn%d' % c])

                def qmm(h, pt, tok, tw=tw):
                    P.mm(pt[0:96, 0:tw], [(wuq[:, c, 96 * h:96 * h + 96], cqn[:, c, 0:tw]) for c in range(3)], r=['cqn0', 'cqn1', 'cqn2', 'wres'], w=[tok])
                heads_block(qmm, 32 + l, rope, QT, t0, tw)
                pcs = []
                for c in range(2):
                    pcs.append(featmm(1664 + 128 * c, 128, H, b, tw))
                for c in range(2):
                    pc, tc_ = pcs[c]
                    P.op('act', _mk('activation', out=sq3[:, c, 0:tw], in_=pc[:, 0:tw], func=AF.Square), r=[tc_], w=['sq3_%d' % c])
                for c in range(2):
                    pc, tc_ = pcs[c]
                    P.op('dve', _mk('tensor_copy', ckvF[:, c, 0:tw], pc[:, 0:tw]), r=[tc_], w=['ckvF%d' % c])
                pss, tss = nps()
                P.mm(pss[:, 0:tw], [(ones16[:], sq3[:, c, 0:tw]) for c in range(2)], r=['sq3_0', 'sq3_1', 'ones16'], w=[tss])
                rstd_from(pss, tss, rs, 'rs', tw, 256.0)
                for c in range(2):
                    P.op('dve', _mk('scalar_tensor_tensor', out=ckvn[:, c, 0:tw], in0=ckvF[:, c, 0:tw], scalar=gkv(c), in1=rs[:, 0:tw], op0=ALU.mult, op1=ALU.mult), r=['ckvF%d' % c, 'rs', 'gains'], w=['ckvn%d' % c])
                pkp, tkp = featmm(1920, 32, H, b, tw)
                P.op('act', _mk('activation', out=kpe16[0:32, 0:tw], in_=pkp[0:32, 0:tw], func=AF.Copy), r=[tkp], w=['kpe16'])

                def kmm(h, pt, tok, tw=tw):
                    P.mm(pt[0:96, 0:tw], [(wkp[:, c, 96 * h:96 * h + 96], ckvn[:, c, 0:tw]) for c in range(2)] + [(sh16[:, :], kpe16[0:32, 0:tw])], r=['ckvn0', 'ckvn1', 'kpe16', 'sh16', 'wres'], w=[tok])
                heads_block(kmm, 36 + l, rope, KT, t0, tw)
                for i in range(tw // 128):
                    bb = i % 2
                    pv, tv = nps()
                    P.mm(pv[:, 0:512], [(ckvn[:, c, 128 * i:128 * i + 128], wvv[:, c, :]) for c in range(2)], r=['ckvn0', 'ckvn1', 'wres'], w=[tv])
                    P.op('act', _mk('activation', out=v16[bb][:, :, 0:64], in_=pv[:, 0:512].rearrange('p (h e) -> p h e', h=8), func=AF.Copy), r=[tv], w=['v16_%d' % bb])
                    P.dma('pool', VV[t0 + 128 * i:t0 + 128 * i + 128, :, :], v16[bb], r=['v16_%d' % bb], w=['VVd'])
            P.barrier()

        def phase_hgrn(l):
            FA.reset()
            BA.reset()
            BW.reset()
            chains = [(dd, pr) for dd in range(2) for pr in range(2)]
            qt = {}
            kt = {}
            kh = {}
            vh = {}
            eb = {}
            osb = {}
            a16 = {}
            for ci, ch in enumerate(chains):
                qt[ch] = BW.take(512)
                kt[ch] = BW.take(512)
                kh[ch] = BW.take(2048, 'p (n c) -> p n c', n=16, parts=32)
                vh[ch] = BW.take(2048, 'p (n c) -> p n c', n=16, parts=32)
                eb[ch] = FA.take(16)
                osb[ch] = FA.take(512)
                a16[ch] = BA.take(64, 'p (h c) -> p h c', h=2, parts=32)
            P.op('dve', _mk('memset', S32[:], 0.0), w=['S32_%d' % i for i in range(4)])
            P.op('pool', _mk('memset', S16[:], 0.0), w=['S16_%d' % i for i in range(4)])
            order = {0: [TILES[16]] + TILES[0:16], 1: [TILES[16]] + TILES[15::-1]}
            for step in range(17):
                for ci, ch in enumerate(chains):
                    dd, pr = ch
                    t0, tw, m = order[dd][step]
                    nch = tw // CH
                    c0 = 'c%d' % ci
                    P.dma('sp', qt[ch][:, 0:tw], HQ[dd, 128 * pr:128 * pr + 128, t0:t0 + tw], w=[c0 + 'q'])
                    P.dma('sp', kt[ch][:, 0:tw], HK[dd, 128 * pr:128 * pr + 128, t0:t0 + tw], w=[c0 + 'k'])
                    P.dma('sp', kh[ch][:, 0:nch, :], KH[t0:t0 + tw, dd * 256 + pr * 128:dd * 256 + pr * 128 + 128].rearrange('(n s) c -> s n c', s=CH), w=[c0 + 'kh'])
                    P.dma('sp', vh[ch][:, 0:nch, :], VH[t0:t0 + tw, pr * 128:pr * 128 + 128].rearrange('(n s) c -> s n c', s=CH), w=[c0 + 'vh'])
                    P.dma('sp', eb[ch][:, 0:nch], EBE[dd, pr, :, t0 // CH:t0 // CH + nch], w=[c0 + 'eb'])
                nchs = order[0][step][1] // CH
                for cidx in range(nchs):
                    info = []
                    for ci, ch in enumerate(chains):
                        dd, pr = ch
                        t0, tw, m = order[dd][step]
                        nch = tw // CH
                        cc = cidx if dd == 0 else nch - 1 - cidx
                        info.append(dict(ci=ci, ch=ch, dd=dd, c0='c%d' % ci, bank=ps[ci], btok='ps%d' % ci, obank=ps[4 + ci], otok='ps%d' % (4 + ci),
                                         msk=maskf if dd == 0 else maskb, s32=S32[:, 64 * ci:64 * ci + 64], s16=S16[:, 64 * ci:64 * ci + 64],
                                         cc=cc, sl=slice(CH * cc, CH * cc + CH)))
                    first = cidx == 0
                    for d in info:
                        ch, cc, sl, c0, bank, btok = (d['ch'], d['cc'], d['sl'], d['c0'], d['bank'], d['btok'])
                        for hh in range(2):
                            pb = 64 * hh
                            P.mm(bank[pb:pb + 64, 0:64], [(kh[ch][0:32, cc, pb:pb + 64], vh[ch][0:32, cc, pb:pb + 64])], r=[c0 + 'kh', c0 + 'vh'], w=[btok + 'U'])
                            P.mm(bank[0:32, 64 + 32 * hh:96 + 32 * hh], [(kt[ch][pb:pb + 64, sl], qt[ch][pb:pb + 64, sl])], r=[c0 + 'k', c0 + 'q'], w=[btok + 'A%d' % hh])
                    for d in info:
                        ch, c0, bank, btok, msk = (d['ch'], d['c0'], d['bank'], d['btok'], d['msk'])
                        for hh in range(2):
                            P.op('dve', _mk('tensor_tensor', out=a16[ch][0:32, hh, :], in0=bank[0:32, 64 + 32 * hh:96 + 32 * hh], in1=msk[:, :], op=ALU.mult), r=[btok + 'A%d' % hh, 'maskf', 'maskb'], w=[c0 + 'a%d' % hh])
                    for d in info:
                        ch, cc, sl, c0, ci, obank, otok, s16 = (d['ch'], d['cc'], d['sl'], d['c0'], d['ci'], d['obank'], d['otok'], d['s16'])
                        for hh in range(2):
                            pb = 64 * hh
                            P.mm(obank[pb:pb + 64, sl], [(s16[pb:pb + 64, :], qt[ch][pb:pb + 64, sl]), (vh[ch][0:32, cc, pb:pb + 64], a16[ch][0:32, hh, :])], r=['S16_%d' % ci, c0 + 'q', c0 + 'vh', c0 + 'a%d' % hh], w=[otok] if first else [otok + 'x'])
                    for d in info:
                        ch, cc, c0, ci, bank, btok, s32 = (d['ch'], d['cc'], d['c0'], d['ci'], d['bank'], d['btok'], d['s32'])
                        P.op('dve', _mk('scalar_tensor_tensor', out=s32, in0=s32, scalar=eb[ch][:, cc:cc + 1], in1=bank[:, 0:64], op0=ALU.mult, op1=ALU.add), r=[btok + 'U', c0 + 'eb', 'S32_%d' % ci], w=['S32_%d' % ci])
                    for d in info:
                        ci, s32, s16 = (d['ci'], d['s32'], d['s16'])
                        P.op('act', _mk('activation', out=s16, in_=s32, func=AF.Copy), r=['S32_%d' % ci], w=['S16_%d' % ci])
                for ci, ch in enumerate(chains):
                    dd, pr = ch
                    t0, tw, m = order[dd][step]
                    c0 = 'c%d' % ci
                    obank = ps[4 + ci]
                    otok = 'ps%d' % (4 + ci)
                    P.op('act', _mk('activation', out=osb[ch][:, 0:tw], in_=obank[:, 0:tw], func=AF.Copy), r=[otok, otok + 'x'], w=[c0 + 'o'])
                    P.dma('pool', OFB[dd, 128 * pr:128 * pr + 128, t0:t0 + tw], osb[ch][:, 0:tw], r=[c0 + 'o'], w=['OFBd'])
            P.barrier()
            FA.reset()
            BA.reset()
            of = [FA.take(512) for _ in range(2)]
            obb = [FA.take(512) for _ in range(2)]
            rs = FA.take(512)
            gt = [BA.take(512) for _ in range(2)]
            sq = BA.take(512)
            mo = [BA.take(512) for _ in range(2)]
            k = 0
            for t0, tw, m in TILES:
                for pr in range(2):
                    b = k % 2
                    k += 1
                    P.dma('sp', of[b][:, 0:tw], OFB[0, 128 * pr:128 * pr + 128, t0:t0 + tw], w=['of%d' % b])
                    P.dma('sp', obb[b][:, 0:tw], OFB[1, 128 * pr:128 * pr + 128, t0:t0 + tw], w=['ob%d' % b])
                    P.dma('sp', gt[b][:, 0:tw], GT[128 * pr:128 * pr + 128, t0:t0 + tw], w=['gt%d' % b])
                    P.op('dve', _mk('tensor_tensor', out=of[b][:, 0:tw], in0=of[b][:, 0:tw], in1=obb[b][:, 0:tw], op=ALU.add), r=['ob%d' % b], w=['of%d' % b])
                    P.op('act', _mk('activation', out=sq[:, 0:tw], in_=of[b][:, 0:tw], func=AF.Square), r=['of%d' % b], w=['sq'])
                    pt, tok = (ps[k % 4], 'ps%d' % (k % 4))
                    P.mm(pt[:, 0:tw], [(bd16[:], sq[:, 0:tw])], r=['sq', 'bd16'], w=[tok])
                    rstd_from(pt, tok, rs, 'rs', tw, 64.0)
                    P.op('dve', _mk('scalar_tensor_tensor', out=of[b][:, 0:tw], in0=of[b][:, 0:tw], scalar=gains[:, l:l + 1], in1=rs[:, 0:tw], op0=ALU.mult, op1=ALU.mult), r=['rs', 'gains'], w=['of%d' % b])
                    P.op('dve', _mk('tensor_tensor', out=mo[b][:, 0:tw], in0=of[b][:, 0:tw], in1=gt[b][:, 0:tw], op=ALU.mult), r=['of%d' % b, 'gt%d' % b], w=['mo%d' % b])
                    P.dma('pool', mixT[128 * pr:128 * pr + 128, t0:t0 + tw], mo[b][:, 0:tw], r=['mo%d' % b], w=['mixd'])
            P.barrier()

        def phase_attn(l, with_ctx):
            FA.reset()
            BA.reset()
            BW.reset()
            ktb = [BW.take(T, parts=96) for _ in range(2)]
            vtb = [BW.take(66 * 65, 'p (k e) -> p k e', e=65) for _ in range(2)]
            qtb = [BA.take(512, parts=96) for _ in range(2)]
            pb16 = [BA.take(512) for _ in range(4)]
            mo = [BA.take(512, parts=64) for _ in range(2)]
            osb = [FA.take(512, parts=64) for _ in range(2)]
            rden = FA.take(512)
            scale = 96.0 ** (-0.5)
            qi = 0
            pi = 0
            for h in range(8):
                hb = h % 2
                P.dma('sp', ktb[hb][:, :], KT[h, :, :], w=['kt%d' % hb])
                P.dma('sp', vtb[hb][:, :, :], VV[:, h, :].rearrange('(k p) e -> p k e', p=128), w=['vt%d' % hb])
                qtiles = [(t0, tw, list(range(66))) for t0, tw, m in TILES if m == 0]
                if with_ctx:
                    qtiles.append((TL, TC, [64, 65]))
                for t0, tw, kcs in qtiles:
                    qb = qi % 2
                    qi += 1
                    P.dma('sp', qtb[qb][:, 0:tw], QT[h, :, t0:t0 + tw], w=['q%d' % qb])
                    po = ps[4 + qb]
                    tpo = 'ps%d' % (4 + qb)
                    for n, kc in enumerate(kcs):
                        sp_ = ps[pi % 4]
                        tsp = 'ps%d' % (pi % 4)
                        pbuf = pb16[pi % 4]
                        tpb = 'pb%d' % (pi % 4)
                        pi += 1
                        P.mm(sp_[:, 0:tw], [(ktb[hb][:, 128 * kc:128 * kc + 128], qtb[qb][:, 0:tw])], r=['kt%d' % hb, 'q%d' % qb], w=[tsp])
                        P.op('act', _mk('activation', out=pbuf[:, 0:tw], in_=sp_[:, 0:tw], func=AF.Exp, scale=scale), r=[tsp], w=[tpb])
                        first = n == 0
                        last = n == len(kcs) - 1
                        P.op('pe', _mk('matmul', po[0:65, 0:tw], vtb[hb][:, kc, :], pbuf[:, 0:tw], start=first, stop=last), r=[tpb, 'vt%d' % hb], w=[tpo] if first else [tpo + 'x'], signal=True)
                    P.op('dve', _mk('reciprocal', out=rden[64:65, 0:tw], in_=po[64:65, 0:tw]), r=[tpo, tpo + 'x'], w=['rden'])
                    P.op('act', _mk('activation', out=osb[qb][0:64, 0:tw], in_=po[0:64, 0:tw], func=AF.Copy), r=[tpo, tpo + 'x'], w=['osb%d' % qb])
                    P.mm(ps[6][0:64, 0:tw], [(onesF[64:65, 0:64], rden[64:65, 0:tw])], r=['rden', 'onesF'], w=['ps6'])
                    P.op('dve', _mk('tensor_tensor', out=mo[qb][0:64, 0:tw], in0=osb[qb][0:64, 0:tw], in1=ps[6][0:64, 0:tw], op=ALU.mult), r=['osb%d' % qb, 'ps6'], w=['mo%d' % qb])
                    P.dma('pool', mixT[256 + 64 * h:256 + 64 * h + 64, t0:t0 + tw], mo[qb][0:64, 0:tw], r=['mo%d' % qb], w=['mixd'])
            P.barrier()

        def phase_pool(l, with_ctx):
            FA.reset()
            BA.reset()
            BW.reset()
            pw = BW.take(256, 'p (c n) -> p c n', c=2)
            stg = FA.take(256, 'p (c n) -> p c n', c=2)
            for ch in range(2):
                P.dma('sp', stg[:, ch, :], I['pwb'][l, ch, :, :], w=['stg'])
            P.op('pool', _mk('tensor_copy', pw, stg), r=['stg'], w=['wres'])
            W = 512 + 16
            ub = [FA.take(2 * W, 'p (c t) -> p c t', c=2) for _ in range(2)]
            a1 = FA.take(2 * W, 'p (c t) -> p c t', c=2)
            a2 = FA.take(2 * W, 'p (c t) -> p c t', c=2)
            a3 = FA.take(W)
            a4 = FA.take(W)
            sm = FA.take(1024, 'p (c t) -> p c t', c=2)
            pl16 = BA.take(1024, 'p (c t) -> p c t', c=2)
            mo = [BA.take(512) for _ in range(4)]
            k = 0
            for ti, (t0, tw, m) in enumerate(TILES):
                if m == 1 and (not with_ctx):
                    continue
                b = ti % 2
                U = ub[b]
                seq0, seq1 = (0, TL) if m == 0 else (TL, T)
                lo = max(t0 - 8, seq0)
                hi = min(t0 + tw + 8, seq1)
                if lo > t0 - 8 or hi < t0 + tw + 8:
                    P.op('pool', _mk('memset', U[:, :, :], 0.0), w=['ub%d' % b])
                P.dma('sp', U[:, :, lo - (t0 - 8):hi - (t0 - 8)], uT[:, lo:hi].rearrange('(c p) t -> p c t', p=128), w=['ub%d' % b])
                n1 = tw + 15
                P.op('dve', _mk('tensor_tensor', out=a1[:, :, 0:n1], in0=U[:, :, 0:n1], in1=U[:, :, 1:n1 + 1], op=ALU.add), r=['ub%d' % b], w=['a1'])
                n2 = tw + 13
                P.op('dve', _mk('tensor_tensor', out=a2[:, :, 0:n2], in0=a1[:, :, 0:n2], in1=a1[:, :, 2:n2 + 2], op=ALU.add), r=['a1'], w=['a2'])
                n3 = tw + 9
                P.op('dve', _mk('tensor_tensor', out=a3[:, 0:n3], in0=a2[:, 1, 0:n3], in1=a2[:, 1, 4:n3 + 4], op=ALU.add), r=['a2'], w=['a3'])
                n4 = tw + 1
                P.op('dve', _mk('tensor_tensor', out=a4[:, 0:n4], in0=a3[:, 0:n4], in1=a3[:, 8:n4 + 8], op=ALU.add), r=['a3'], w=['a4'])
                P.op('dve', _mk('tensor_scalar', out=sm[0:64, 0, 0:tw], in0=a1[0:64, 0, 7:7 + tw], scalar1=0.5, scalar2=None, op0=ALU.mult), r=['a1'], w=['sm0a'])
                P.op('dve', _mk('tensor_scalar', out=sm[64:128, 0, 0:tw], in0=a2[64:128, 0, 6:6 + tw], scalar1=0.25, scalar2=None, op0=ALU.mult), r=['a2'], w=['sm0b'])
                P.op('dve', _mk('tensor_scalar', out=sm[0:64, 1, 0:tw], in0=a3[0:64, 4:4 + tw], scalar1=0.125, scalar2=None, op0=ALU.mult), r=['a3'], w=['sm1a'])
                P.op('dve', _mk('tensor_scalar', out=sm[64:128, 1, 0:tw], in0=a4[64:128, 0:tw], scalar1=0.0625, scalar2=None, op0=ALU.mult), r=['a4'], w=['sm1b'])
                smt = ['sm0a', 'sm0b', 'sm1a', 'sm1b']
                if t0 == seq0:
                    P.op('dve', _mk('tensor_tensor', out=sm[:, :, 0:8], in0=sm[:, :, 0:8], in1=corr[:, 0:32].rearrange('p (c t) -> p c t', c=2)[:, :, 0:8], op=ALU.mult), r=smt + ['corr'], w=smt)
                if t0 + tw == seq1:
                    P.op('dve', _mk('tensor_tensor', out=sm[:, :, tw - 8:tw], in0=sm[:, :, tw - 8:tw], in1=corr[:, 0:32].rearrange('p (c t) -> p c t', c=2)[:, :, 8:16], op=ALU.mult), r=smt + ['corr'], w=smt)
                P.op('dve', _mk('tensor_tensor', out=pl16[:, :, 0:tw], in0=sm[:, :, 0:tw], in1=U[:, :, 8:8 + tw], op=ALU.subtract), r=smt + ['ub%d' % b], w=['pl16'])
                for ch in range(2):
                    pt, tok = (ps[k % 4], 'ps%d' % (k % 4))
                    mb = mo[k % 4]
                    mtok = 'mo%d' % (k % 4)
                    k += 1
                    P.mm(pt[:, 0:tw], [(pw[:, ch, :], pl16[:, ch, 0:tw])], r=['pl16', 'wres'], w=[tok])
                    P.op('act', _mk('activation', out=mb[:, 0:tw], in_=pt[:, 0:tw], func=AF.Identity, scale=gains[:, 24 + 2 * l + ch:25 + 2 * l + ch]), r=[tok, 'gains'], w=[mtok])
                    P.dma('pool', mixT[768 + 128 * ch:768 + 128 * ch + 128, t0:t0 + tw], mb[:, 0:tw], r=[mtok], w=['mixd'])
            P.barrier()
        steps = [load_consts, phase_mod, phase_in_transpose]
        for l in range(n_layers):
            last = l == NL - 1
            steps += [
                (lambda l=l: phase_norm(l, 0)),
                (lambda l=l: phase_ffn(l, 0, I['f1i'], I['f1o'])),
                (lambda l=l: phase_norm(l, 1)),
                (lambda l=l: phase_inproj(l)),
                (lambda l=l: phase_hgrn(l)),
                (lambda l=l, last=last: phase_attn(l, not last)),
                (lambda l=l, last=last: phase_pool(l, not last)),
                (lambda l=l: phase_outproj(l)),
                (lambda l=l: phase_norm(l, 2)),
                (lambda l=l: phase_ffn(l, 2, I['f2i'], I['f2o'])),
            ]
        steps.append(phase_out_transpose)
        for si, fn in enumerate(steps):
            if upto is not None and si >= upto:
                break
            try:
                fn()
            except _Stop:
                break
        P.barrier()
        P.emit()
    return nc

def host_constants():
    c = {}
    c['ident'] = np.eye(128, dtype=np.float32)
    s = np.arange(128)[:, None]
    t = np.arange(128)[None, :]
    same = s // CH == t // CH
    c['d1f'] = (same & (s > t)).astype(np.float32)
    c['d1b'] = (same & (s < t)).astype(np.float32)
    c['trf'] = (same & (s <= t)).astype(np.float32)
    c['trb'] = (same & (s >= t)).astype(np.float32)
    s2 = np.arange(CH)[:, None]
    t2 = np.arange(CH)[None, :]
    c['maskf'] = (s2 <= t2).astype(np.float32)
    c['maskb'] = (s2 >= t2).astype(np.float32)
    c['bd64'] = (s // 64 == t // 64).astype(np.float32)
    rot = np.zeros((96, 96), np.float32)
    for i in range(16):
        rot[80 + i, 64 + i] = -1.0
        rot[64 + i, 80 + i] = 1.0
    c['rot'] = rot
    sh = np.zeros((32, 96), np.float32)
    sh[np.arange(32), 64 + np.arange(32)] = 1.0
    c['shift'] = sh
    pos = np.arange(TL)
    row = (pos // 64).astype(np.float32)
    col = (pos % 64).astype(np.float32)
    inv = (np.float32(10000.0) ** (-np.arange(8, dtype=np.float32) / np.float32(8))).astype(np.float32)
    ang = np.concatenate([row[:, None] * inv[None, :], col[:, None] * inv[None, :]], axis=1).astype(np.float32)
    cs = np.cos(ang).astype(np.float32).T
    sn = np.sin(ang).astype(np.float32).T
    cosT = np.ones((96, TL), np.float32)
    sinT = np.zeros((96, TL), np.float32)
    cosT[64:80] = cs
    cosT[80:96] = cs
    sinT[64:80] = sn
    sinT[80:96] = sn
    c['cosT'] = cosT
    c['sinT'] = sinT
    corr = np.ones((128, 2, 16), np.float32)
    for g, w in enumerate((2, 4, 8, 16)):
        ch, p0 = (g // 2, g % 2 * 64)
        for i in range(8):
            lo = max(i - w // 2, 0)
            hi = i + w - 1 - w // 2
            corr[p0:p0 + 64, ch, i] = w / float(hi - lo + 1)
            d = 7 - i
            hi2 = min(w - 1 - w // 2, d)
            cnt = hi2 + w // 2 + 1
            corr[p0:p0 + 64, ch, 8 + i] = w / float(cnt)
    c['corr'] = corr.reshape(128, 32)
    return c
_CACHE = {}

def prep_inputs(inputs, b):
    f = lambda a: np.ascontiguousarray(np.asarray(a, dtype=np.float32))
    m = {}
    m['x_in'] = f(inputs['x'][b])
    m['ctx_in'] = f(inputs['ctx'][b])
    cv = np.stack([np.asarray(inputs['c'][b]), np.asarray(inputs['c_ctx'])], 0)
    m['cvT'] = f(cv.reshape(2, 8, 128).transpose(2, 1, 0).reshape(128, 16))
    m['w_mod'] = f(inputs['w_mod'])
    m['b_mod'] = f(inputs['b_mod'])
    m['f1i'] = f(inputs['ffn1_w_in'])
    m['f1o'] = f(inputs['ffn1_w_out'])
    m['f2i'] = f(inputs['ffn2_w_in'])
    m['f2o'] = f(inputs['ffn2_w_out'])
    m['w_in'] = f(inputs['w_in'])
    m['w_out'] = f(inputs['w_out'])
    lb = np.asarray(inputs['hg_lb_logits'], np.float32)
    m['lblB'] = f(np.broadcast_to(lb.reshape(1, NL * 512), (128, NL * 512)))
    m['lblF'] = f(lb.reshape(NL, 2, 2, 128).transpose(3, 0, 1, 2).reshape(128, 16))
    m['g_hg'] = f(np.tile(np.asarray(inputs['hg_out_gain'], np.float32), (1, 2)).T)
    m['g_qa'] = f(np.asarray(inputs['mla_q_a_gain'], np.float32).reshape(NL, 3, 128).transpose(2, 0, 1).reshape(128, NL * 3))
    m['g_kva'] = f(np.asarray(inputs['mla_kv_a_gain'], np.float32).reshape(NL, 2, 128).transpose(2, 0, 1).reshape(128, NL * 2))
    m['g_q'] = f(np.asarray(inputs['mla_q_gain'], np.float32).T)
    m['g_k'] = f(np.asarray(inputs['mla_k_gain'], np.float32).T)
    m['g_ps'] = f(np.asarray(inputs['pool_scale'], np.float32).reshape(NL, 2, 128).transpose(2, 0, 1).reshape(128, NL * 2))
    m['w_uq'] = f(inputs['mla_w_uq'])
    wukv = np.asarray(inputs['mla_w_ukv'], np.float32).reshape(NL, 256, 8, 128)
    wk = np.zeros((NL, 256, 8, 96), np.float32)
    wk[..., 0:64] = wukv[..., 0:64]
    m['wk_pad'] = f(wk.reshape(NL, 256, 768))
    m['wv'] = f(wukv[..., 64:128].reshape(NL, 256, 512))
    pw = np.asarray(inputs['pool_w'], np.float32)
    pwb = np.zeros((NL, 2, 128, 128), np.float32)
    for ch in range(2):
        pwb[:, ch, 0:64, 0:64] = pw[:, 2 * ch]
        pwb[:, ch, 64:128, 64:128] = pw[:, 2 * ch + 1]
    m['pwb'] = pwb
    return m

def kernel(**inputs):
    if 'nc' not in _CACHE:
        _CACHE['nc'] = build()
        _CACHE['consts'] = host_constants()
    nc = _CACHE['nc']
    in_maps = []
    per_b = [prep_inputs(inputs, b) for b in range(4)]
    for core in range(8):
        mm = dict(per_b[core % 4])
        mm.update(_CACHE['consts'])
        in_maps.append(mm)
    res = run_bass_kernel_spmd(nc, in_maps, core_ids=list(range(8)))
    out = np.stack([np.asarray(res.results[b]['out'], np.float32) for b in range(4)], 0)
    return out
```

```python
import numpy as np
import ml_dtypes
from contextlib import ExitStack
import concourse.bass as bass
import concourse.mybir as mybir
from concourse.bass_utils import run_bass_kernel_spmd
F32 = mybir.dt.float32
BF16 = mybir.dt.bfloat16
AF = mybir.ActivationFunctionType
ALU = mybir.AluOpType
D = 1024
TL = 8192
TC = 256
T = TL + TC
DFF = 2816
NL = 4
EPS = 1e-06
INW = 2208
CH = 32
NCHUNK = T // CH
ENGS = ('pe', 'act', 'dve', 'pool', 'sp')


def _mk(method, *args, **kwargs):
    return lambda e: getattr(e, method)(*args, **kwargs)


class Ev:
    __slots__ = ('sem', 'val')

    def __init__(self, sem, val):
        self.sem = sem
        self.val = val

class Prog:

    def __init__(self, nc, n_dma_sems=12):
        self.nc = nc
        self.ops = {e: [] for e in ENGS}
        self.cnt = {e: 0 for e in ENGS}
        self.known = {e: {} for e in ENGS}
        self.last_w = {}
        self.readers = {}
        self.n_dma_sems = n_dma_sems
        self.dma_uses = {}
        self.dma_rr = {e: 0 for e in ENGS}
        self.pending = {e: [] for e in ENGS}

    def _need(self, eng, ev, waits):
        if ev is None:
            return
        if ev.val is None:
            if ev.sem == ('eng', 'pe') and eng == 'pe':
                return
            raise RuntimeError('wait on unresolved event')
        k = self.known[eng]
        if k.get(ev.sem, 0) >= ev.val:
            return
        if ev.sem == ('eng', 'pe') and eng == 'pe':
            return
        k[ev.sem] = ev.val
        waits[ev.sem] = max(waits.get(ev.sem, 0), ev.val)

    def _deps(self, eng, r, w):
        waits = {}
        for t in r:
            self._need(eng, self.last_w.get(t), waits)
        for t in w:
            self._need(eng, self.last_w.get(t), waits)
            for ev in self.readers.get(t, ()):
                self._need(eng, ev, waits)
        return waits

    def _commit(self, ev, r, w):
        for t in r:
            self.readers.setdefault(t, []).append(ev)
        for t in w:
            self.last_w[t] = ev
            self.readers[t] = []

    def op(self, eng, fn, r=(), w=(), signal=True):
        w = list(w) + ['bank' + t[2] for t in list(r) + list(w) if t.startswith('ps') and t[2:3].isdigit()]
        waits = self._deps(eng, r, w)
        if signal:
            self.cnt[eng] += 1
            ev = Ev(('eng', eng), self.cnt[eng])
            for p in self.pending[eng]:
                p.val = ev.val
            self.pending[eng] = []
        else:
            ev = Ev(('eng', eng), None)
            self.pending[eng].append(ev)
        self.ops[eng].append((fn, waits, ('eng', eng) if signal else None, 1))
        self._commit(ev, r, w)
        return ev

    def mm(self, out, pairs, r=(), w=()):
        n = len(pairs)

        def mk(i, lhsT, rhs):
            return _mk('matmul', out, lhsT, rhs, start=i == 0, stop=i == n - 1)
        ev = None
        for i, (lhsT, rhs) in enumerate(pairs):
            ev = self.op('pe', mk(i, lhsT, rhs), r=r if i == 0 else (), w=w if i == 0 else (), signal=i == n - 1)
        return ev

    def dma(self, eng, out, in_, r=(), w=(), **kw):
        waits = self._deps(eng, r, w)
        idx = self.dma_rr[eng]
        self.dma_rr[eng] = (idx + 1) % self.n_dma_sems
        key = ('dma', eng, idx)
        uses = self.dma_uses.get(key, 0)
        if uses > 0:
            k = self.known[eng]
            if k.get(key, 0) < 16 * uses:
                k[key] = 16 * uses
                waits[key] = max(waits.get(key, 0), 16 * uses)
        self.dma_uses[key] = uses + 1
        ev = Ev(key, 16 * (uses + 1))
        self.ops[eng].append((_mk('dma_start', out=out, in_=in_, **kw), waits, key, 16))
        self._commit(ev, r, w)
        return ev

    def barrier(self, final=False):
        for e in ENGS:
            waits = {}
            k = self.known[e]
            for e2 in ENGS:
                v = self.cnt[e2]
                key = ('eng', e2)
                if v > 0 and k.get(key, 0) < v:
                    k[key] = v
                    waits[key] = v
            for key, uses in self.dma_uses.items():
                v = 16 * uses
                if k.get(key, 0) < v:
                    k[key] = v
                    waits[key] = v
            self.ops[e].append((None, waits, None, 0))
        self.last_w.clear()
        self.readers.clear()

    def emit(self):
        nc = self.nc
        handles = {'pe': 'tensor', 'act': 'scalar', 'dve': 'vector', 'pool': 'gpsimd', 'sp': 'sync'}
        with ExitStack() as st:
            sems = {}
            for e in ENGS:
                sems['eng', e] = st.enter_context(nc.semaphore('s_' + e))
            for key in self.dma_uses:
                sems[key] = st.enter_context(nc.semaphore('d_%s_%d' % (key[1], key[2])))
            block = st.enter_context(nc.Block())

            def run(e):

                def body(h):
                    for fn, waits, sig, inc in self.ops[e]:
                        for s, v in waits.items():
                            h.wait_ge(sems[s], v)
                        if fn is not None:
                            ins = fn(h)
                            if sig is not None:
                                ins.then_inc(sems[sig], inc)
                return body
            for e in ENGS:
                getattr(block, handles[e])(run(e))

class Arena:

    def __init__(self, t, n):
        self.t = t
        self.n = n
        self.off = 0

    def reset(self):
        self.off = 0

    def take(self, size, pat=None, parts=128, **kw):
        assert self.off + size <= self.n, (self.off, size, self.n)
        v = self.t[0:parts, self.off:self.off + size]
        self.off += size
        if pat:
            v = v.rearrange(pat, **kw)
        return v
TILES = [(i * 512, 512, 0) for i in range(TL // 512)] + [(TL, TC, 1)]

import os
STOP = int(os.environ.get('KSTOP', '99'))
STOP2 = int(os.environ.get('KSTOP2', '0'))


class _Stop(Exception):
    pass


def build(n_layers=NL, debug=(), upto=None):
    nc = bass.Bass('TRN2', target_bir_lowering=False)

    def din(name, shape, dt=F32):
        return nc.dram_tensor(name, list(shape), dt, kind='ExternalInput').ap()

    def dscr(name, shape, dt=F32):
        if name in debug:
            return nc.dram_tensor(name, list(shape), dt, kind='ExternalOutput').ap()
        return nc.dram_tensor(name, list(shape), dt).ap()
    I = {}
    for name, shape in [('x_in', (TL, D)), ('ctx_in', (TC, D)), ('cvT', (128, 16)), ('w_mod', (NL, D, 9 * D)), ('b_mod', (NL, 9 * D)), ('f1i', (NL, D, 2 * DFF)), ('f1o', (NL, DFF, D)), ('f2i', (NL, D, 2 * DFF)), ('f2o', (NL, DFF, D)), ('w_in', (NL, D, INW)), ('w_out', (NL, D, D)), ('lblB', (128, NL * 512)), ('lblF', (128, 16)), ('g_hg', (128, NL)), ('g_qa', (128, NL * 3)), ('g_kva', (128, NL * 2)), ('g_q', (96, NL)), ('g_k', (96, NL)), ('g_ps', (128, NL * 2)), ('w_uq', (NL, 384, 768)), ('wk_pad', (NL, 256, 768)), ('wv', (NL, 256, 512)), ('pwb', (NL, 2, 128, 128)), ('ident', (128, 128)), ('d1f', (128, 128)), ('d1b', (128, 128)), ('trf', (128, 128)), ('trb', (128, 128)), ('maskf', (32, 32)), ('maskb', (32, 32)), ('bd64', (128, 128)), ('rot', (96, 96)), ('shift', (32, 96)), ('cosT', (96, TL)), ('sinT', (96, TL)), ('corr', (128, 32))]:
        I[name] = din(name, shape)
    out = nc.dram_tensor('out', [TL, D], F32, kind='ExternalOutput').ap()
    xT = dscr('xT', (D, T))
    hT = dscr('hT', (D, T), BF16)
    mixT = dscr('mixT', (D, T), BF16)
    QT = dscr('QT', (8, 96, T), BF16)
    KT = dscr('KT', (8, 96, T), BF16)
    VV = dscr('VV', (T, 8, 65), BF16)
    uT = dscr('uT', (256, T))
    KH = dscr('KH', (T, 512), BF16)
    VH = dscr('VH', (T, 256), BF16)
    HQ = dscr('HQ', (2, 256, T), BF16)
    HK = dscr('HK', (2, 256, T), BF16)
    EBE = dscr('EBE', (2, 2, 128, NCHUNK))
    GT = dscr('GT', (256, T), BF16)
    OFB = dscr('OFB', (2, 256, T))
    MODD = dscr('MODD', (128, NL * 144))
    with ExitStack() as st:

        def sb(n, s, d=F32):
            return st.enter_context(nc.sbuf_tensor(n, list(s), d))
        NBW, NBA, NFA = (36000, 20480, 13312)
        BWt = sb('BW', (128, NBW), BF16)
        BAt = sb('BA', (128, NBA), BF16)
        FAt = sb('FA', (128, NFA), F32)
        BW, BA, FA = (Arena(BWt, NBW), Arena(BAt, NBA), Arena(FAt, NFA))
        ps = [st.enter_context(nc.psum_tensor('ps%d' % i, [128, 512], F32)) for i in range(8)]
        MOD = sb('MOD', (128, NL * 144))
        SC1 = sb('SC1', (128, NL * 48))
        GHT = sb('GHT', (128, NL * 48))
        cact = sb('cact', (128, 16))
        identF = sb('identF', (128, 128))
        d1f = sb('d1fS', (128, 128))
        d1b = sb('d1bS', (128, 128))
        trf = sb('trfS', (128, 128))
        trb = sb('trbS', (128, 128))
        maskf = sb('maskfS', (32, 32))
        maskb = sb('maskbS', (32, 32))
        bdF = sb('bdF', (128, 128))
        bd16 = sb('bd16', (128, 128), BF16)
        ones16 = sb('ones16', (128, 128), BF16)
        onesF = sb('onesF', (128, 128))
        rotF = sb('rotF', (96, 96))
        shF = sb('shF', (32, 96))
        sh16 = sb('sh16', (32, 96), BF16)
        corr = sb('corrS', (128, 32))
        epsT = sb('epsT', (128, 1))
        gains = sb('gains', (128, 64))
        lbF = sb('lbF', (128, 16))
        omlF = sb('omlF', (128, 16))
        lbtmp = sb('lbtmp', (128, 16))
        S32 = sb('S32', (128, 4 * 64))
        S16 = sb('S16', (128, 4 * 64), BF16)
        P = Prog(nc)
        ckc = [0]

        def ck():
            ckc[0] += 1
            if ckc[0] == STOP2:
                P.barrier()
                raise _Stop()

        def mod_ap(tile, l, s, c, m, n=48):
            i = ((l * (n // 16) + s) * 8 + c) * 2 + m
            return tile[:, i:i + 1]

        def load_consts():
            for dst, name in [(identF, 'ident'), (d1f, 'd1f'), (d1b, 'd1b'), (trf, 'trf'), (trb, 'trb'), (maskf, 'maskf'), (maskb, 'maskb'), (bdF, 'bd64'), (rotF, 'rot'), (shF, 'shift'), (corr, 'corr'), (cact, 'cvT'), (lbF, 'lblF')]:
                P.dma('sp', dst[:], I[name][:, :], w=[name])
            P.dma('sp', gains[:, 0:4], I['g_hg'][:, :], w=['gains'])
            P.dma('sp', gains[:, 4:16], I['g_qa'][:, :], w=['gains'])
            P.dma('sp', gains[:, 16:24], I['g_kva'][:, :], w=['gains'])
            P.dma('sp', gains[:, 24:32], I['g_ps'][:, :], w=['gains'])
            P.dma('sp', gains[0:96, 32:36], I['g_q'][:, :], w=['gains'])
            P.dma('sp', gains[0:96, 36:40], I['g_k'][:, :], w=['gains'])
            P.op('dve', _mk('memset', onesF[:], 1.0), w=['onesF'])
            P.op('dve', _mk('memset', epsT[:], EPS), w=['epsT'])
            P.op('pool', _mk('memset', ones16[:], 1.0), w=['ones16'])
            P.op('pool', _mk('tensor_copy', bd16[:], bdF[:]), r=['bd64'], w=['bd16'])
            P.op('pool', _mk('tensor_copy', sh16[:], shF[:]), r=['shift'], w=['sh16'])
            P.op('act', _mk('activation', out=cact[:], in_=cact[:], func=AF.Silu), r=['cvT'], w=['cvT'])
            L = lambda l: lbF[:, 4 * l:4 * l + 4]
            mx = lbtmp[:, 0:4]
            sm = lbtmp[:, 4:8]
            rc = lbtmp[:, 8:12]
            P.op('dve', _mk('tensor_tensor', out=mx, in0=L(0), in1=L(1), op=ALU.max), r=['lblF'], w=['lbt'])
            P.op('dve', _mk('tensor_tensor', out=mx, in0=mx, in1=L(2), op=ALU.max), r=['lbt'], w=['lbt'])
            P.op('dve', _mk('tensor_tensor', out=mx, in0=mx, in1=L(3), op=ALU.max), r=['lbt'], w=['lbt'])
            for l in range(4):
                P.op('dve', _mk('tensor_tensor', out=L(l), in0=L(l), in1=mx, op=ALU.subtract), r=['lbt', 'lblF'], w=['lblF'])
            P.op('act', _mk('activation', out=lbF[:], in_=lbF[:], func=AF.Exp), r=['lblF'], w=['lblF'])
            P.op('dve', _mk('tensor_tensor', out=sm, in0=L(0), in1=L(1), op=ALU.add), r=['lblF'], w=['lbt'])
            P.op('dve', _mk('tensor_tensor', out=sm, in0=sm, in1=L(2), op=ALU.add), r=['lbt'], w=['lbt'])
            P.op('dve', _mk('tensor_tensor', out=sm, in0=sm, in1=L(3), op=ALU.add), r=['lbt'], w=['lbt'])
            P.op('dve', _mk('reciprocal', out=rc, in_=sm), r=['lbt'], w=['lbt'])
            for l in range(4):
                P.op('dve', _mk('tensor_tensor', out=L(l), in0=L(l), in1=rc, op=ALU.mult), r=['lbt', 'lblF'], w=['lblF'])
            P.op('dve', _mk('memset', L(0), 0.0), r=['lblF'], w=['lblF'])
            P.op('dve', _mk('tensor_tensor', out=L(2), in0=L(2), in1=L(1), op=ALU.add), r=['lblF'], w=['lblF'])
            P.op('dve', _mk('tensor_tensor', out=L(3), in0=L(3), in1=L(2), op=ALU.add), r=['lblF'], w=['lblF'])
            P.op('dve', _mk('tensor_scalar', out=omlF[:], in0=lbF[:], scalar1=-1.0, scalar2=1.0, op0=ALU.mult, op1=ALU.add), r=['lblF'], w=['omlF'])
            P.barrier()

        def phase_mod():
            FA.reset()
            stg = [FA.take(4096, 'p (c n) -> p c n', c=8) for _ in range(2)]
            brow = [FA.take(512, parts=1) for _ in range(2)]
            k = 0
            for l in range(n_layers):
                for nb in range(18):
                    b = k % 2
                    P.dma('sp', stg[b], I['w_mod'][l, :, nb * 512:(nb + 1) * 512].rearrange('(c p) n -> p c n', p=128), w=['stg%d' % b])
                    P.dma('pool', brow[b], I['b_mod'][l:l + 1, nb * 512:(nb + 1) * 512], w=['brow%d' % b])
                    pt = ps[k % 4]
                    for fc in range(4):
                        pairs = [(stg[b][:, c, 128 * fc:128 * fc + 128], cact[:, 2 * c:2 * c + 2]) for c in range(8)]
                        pairs.append((brow[b][0:1, 128 * fc:128 * fc + 128], onesF[0:1, 0:2]))
                        P.mm(pt[:, 2 * fc:2 * fc + 2], pairs, r=['stg%d' % b, 'brow%d' % b, 'cvT', 'onesF'], w=['ps%d' % (k % 4)])
                    g0 = l * 144 + nb * 8
                    P.op('dve', _mk('tensor_copy', MOD[:, g0:g0 + 8], pt[:, 0:8]), r=['ps%d' % (k % 4)], w=['MOD'])
                    k += 1
            for l in range(n_layers):
                for s in range(3):
                    src = MOD[:, l * 144 + (3 * s + 1) * 16:l * 144 + (3 * s + 2) * 16]
                    dst = SC1[:, (l * 3 + s) * 16:(l * 3 + s + 1) * 16]
                    P.op('dve', _mk('tensor_scalar', out=dst, in0=src, scalar1=1.0, scalar2=None, op0=ALU.add), r=['MOD'], w=['SC1'])
                    srcg = MOD[:, l * 144 + (3 * s + 2) * 16:l * 144 + (3 * s + 3) * 16]
                    dstg = GHT[:, (l * 3 + s) * 16:(l * 3 + s + 1) * 16]
                    fac = 1.0 if s == 1 else 0.5
                    P.op('dve', _mk('tensor_scalar', out=dstg, in0=srcg, scalar1=fac, scalar2=None, op0=ALU.mult), r=['MOD'], w=['GHT'])
            if 'MODD' in debug:
                P.dma('sp', MODD[:, :], MOD[:], r=['MOD'], w=['MODD'])
            P.barrier()

        def shift_ap(l, s, c, m):
            i = l * 144 + 3 * s * 16 + c * 2 + m
            return MOD[:, i:i + 1]

        def phase_in_transpose():
            FA.reset()
            xb = [FA.take(1024) for _ in range(4)]
            xt = FA.take(4096, 'p (c t) -> p c t', c=8)
            for t0, tw, m in TILES:
                nb = tw // 128
                for i in range(nb):
                    src = I['x_in'][t0 + 128 * i:t0 + 128 * i + 128, :] if m == 0 else I['ctx_in'][128 * i:128 * i + 128, :]
                    P.dma('sp', xb[i], src, w=['xb%d' % i])
                for c in range(8):
                    for i in range(nb):
                        P.op('pe', _mk('transpose', ps[c][:, 128 * i:128 * i + 128], xb[i][:, 128 * c:128 * c + 128], identF[:]), r=['xb%d' % i, 'ident'], w=['ps%d' % c])
                    if c % 2 == 0:
                        P.op('act', _mk('activation', out=xt[:, c, 0:tw], in_=ps[c][:, 0:tw], func=AF.Copy), r=['ps%d' % c], w=['xt%d' % c])
                    else:
                        P.op('dve', _mk('tensor_copy', xt[:, c, 0:tw], ps[c][:, 0:tw]), r=['ps%d' % c], w=['xt%d' % c])
                P.dma('pool', xT[:, t0:t0 + tw].rearrange('(c p) t -> p c t', p=128), xt[:, :, 0:tw], r=['xt%d' % c for c in range(8)], w=['xT%d' % t0])
            P.barrier()

        def phase_out_transpose():
            FA.reset()
            xt = [FA.take(4096, 'p (c t) -> p c t', c=8) for _ in range(2)]
            ob = [FA.take(1024) for _ in range(4)]
            for ti, (t0, tw, m) in enumerate(TILES):
                if m == 1:
                    continue
                b = ti % 2
                P.dma('sp', xt[b][:, :, 0:tw], xT[:, t0:t0 + tw].rearrange('(c p) t -> p c t', p=128), w=['xt%d' % b])
                for i in range(tw // 128):
                    for c in range(8):
                        pt = ps[(i * 8 + c) % 8]
                        P.op('pe', _mk('transpose', pt[:, 0:128], xt[b][:, c, 128 * i:128 * i + 128], identF[:]), r=['xt%d' % b, 'ident'], w=['ps%d' % ((i * 8 + c) % 8)])
                        if c % 2 == 0:
                            P.op('act', _mk('activation', out=ob[i][:, 128 * c:128 * c + 128], in_=pt[:, 0:128], func=AF.Copy), r=['ps%d' % ((i * 8 + c) % 8)], w=['ob%d_%d' % (i, c)])
                        else:
                            P.op('dve', _mk('tensor_copy', ob[i][:, 128 * c:128 * c + 128], pt[:, 0:128]), r=['ps%d' % ((i * 8 + c) % 8)], w=['ob%d_%d' % (i, c)])
                    P.dma('pool', out[t0 + 128 * i:t0 + 128 * i + 128, :], ob[i], r=['ob%d_%d' % (i, c) for c in range(8)], w=['out%d_%d' % (t0, i)])
            P.barrier()

        def phase_norm(l, s):
            FA.reset()
            BA.reset()
            xt = [FA.take(4096, 'p (c t) -> p c t', c=8) for _ in range(2)]
            tmp = [FA.take(512) for _ in range(2)]
            rstd = FA.take(512)
            sq = BA.take(4096, 'p (c t) -> p c t', c=8)
            h = [BA.take(4096, 'p (c t) -> p c t', c=8) for _ in range(2)]
            for ti, (t0, tw, m) in enumerate(TILES):
                b = ti % 2
                X = xt[b]
                P.dma('sp', X[:, :, 0:tw], xT[:, t0:t0 + tw].rearrange('(c p) t -> p c t', p=128), w=['xt%d' % b])
                P.op('act', _mk('activation', out=sq[:, :, 0:tw], in_=X[:, :, 0:tw], func=AF.Square), r=['xt%d' % b], w=['sq'])
                P.mm(ps[0][:, 0:tw], [(ones16[:], sq[:, c, 0:tw]) for c in range(8)], r=['sq', 'ones16'], w=['ps0'])
                P.op('act', _mk('activation', out=rstd[:, 0:tw], in_=ps[0][:, 0:tw], func=AF.Sqrt, bias=epsT[:, 0:1], scale=1.0 / D), r=['ps0', 'epsT'], w=['rstd'])
                P.op('dve', _mk('reciprocal', out=rstd[:, 0:tw], in_=rstd[:, 0:tw]), r=['rstd'], w=['rstd'])
                for c in range(8):
                    tb = tmp[c % 2]
                    P.op('dve', _mk('scalar_tensor_tensor', out=tb[:, 0:tw], in0=X[:, c, 0:tw], scalar=mod_ap(SC1, l, s, c, m), in1=rstd[:, 0:tw], op0=ALU.mult, op1=ALU.mult), r=['xt%d' % b, 'rstd', 'SC1'], w=['tmp%d' % (c % 2)])
                    P.op('act', _mk('activation', out=h[b][:, c, 0:tw], in_=tb[:, 0:tw], func=AF.Identity, bias=shift_ap(l, s, c, m), scale=1.0), r=['tmp%d' % (c % 2), 'MOD'], w=['h%d_%d' % (b, c)])
                P.dma('pool', hT[:, t0:t0 + tw].rearrange('(c p) t -> p c t', p=128), h[b][:, :, 0:tw], r=['h%d_%d' % (b, c) for c in range(8)], w=['hT%d' % t0])
            P.barrier()

        def residual_update(pt, ptok, X, xtok, j, tw, l, s, m):
            P.op('dve', _mk('scalar_tensor_tensor', out=X[:, j, 0:tw], in0=pt[:, 0:tw], scalar=mod_ap(GHT, l, s, j, m), in1=X[:, j, 0:tw], op0=ALU.mult, op1=ALU.add), r=[ptok, 'GHT'], w=[xtok + '_%d' % j])

        def phase_ffn(l, s, wi, wo):
            NH = 11
            for half in range(2):
                FA.reset()
                BA.reset()
                BW.reset()
                wg = BW.take(8 * 1408, 'p (c n) -> p c n', c=8)
                wu = BW.take(8 * 1408, 'p (c n) -> p c n', c=8)
                wob = BW.take(NH * 1024, 'p (c n) -> p c n', c=NH)
                stg = [FA.take(1408) for _ in range(2)]
                k = 0
                f0 = half * 1408
                for c in range(8):
                    for dst, col0 in ((wg, f0), (wu, DFF + f0)):
                        bb = k % 2
                        P.dma('sp', stg[bb], wi[l, 128 * c:128 * c + 128, col0:col0 + 1408], w=['stg%d' % bb])
                        P.op('pool', _mk('tensor_copy', dst[:, c, :], stg[bb][:, 0:1408]), r=['stg%d' % bb], w=['wres'])
                        k += 1
                for i in range(NH):
                    bb = k % 2
                    P.dma('sp', stg[bb][:, 0:1024], wo[l, f0 + 128 * i:f0 + 128 * i + 128, :], w=['stg%d' % bb])
                    P.op('pool', _mk('tensor_copy', wob[:, i, :], stg[bb][:, 0:1024]), r=['stg%d' % bb], w=['wres'])
                    k += 1
                xt = [FA.take(4096, 'p (c t) -> p c t', c=8) for _ in range(2)]
                sg = [FA.take(512) for _ in range(2)]
                ht = [BA.take(4096, 'p (c t) -> p c t', c=8) for _ in range(2)]
                act = BA.take(NH * 512, 'p (c t) -> p c t', c=NH)
                for ti, (t0, tw, m) in enumerate(TILES):
                    b = ti % 2
                    H = ht[b]
                    X = xt[b]
                    P.dma('sp', H[:, :, 0:tw], hT[:, t0:t0 + tw].rearrange('(c p) t -> p c t', p=128), w=['ht%d' % b])
                    P.dma('sp', X[:, :, 0:tw], xT[:, t0:t0 + tw].rearrange('(c p) t -> p c t', p=128), r=['xT%d' % t0], w=['xt%d_%d' % (b, j) for j in range(8)])
                    for i in range(NH):
                        pg = ps[2 * i % 4]
                        pu = ps[(2 * i + 1) % 4]
                        tg = 'ps%d' % (2 * i % 4)
                        tu = 'ps%d' % ((2 * i + 1) % 4)
                        P.mm(pg[:, 0:tw], [(wg[:, c, 128 * i:128 * i + 128], H[:, c, 0:tw]) for c in range(8)], r=['ht%d' % b, 'wres'], w=[tg])
                        P.mm(pu[:, 0:tw], [(wu[:, c, 128 * i:128 * i + 128], H[:, c, 0:tw]) for c in range(8)], r=['ht%d' % b, 'wres'], w=[tu])
                        sgb = sg[i % 2]
                        P.op('act', _mk('activation', out=sgb[:, 0:tw], in_=pg[:, 0:tw], func=AF.Silu), r=[tg], w=['sg%d' % (i % 2)])
                        P.op('dve', _mk('tensor_tensor', out=act[:, i, 0:tw], in0=sgb[:, 0:tw], in1=pu[:, 0:tw], op=ALU.mult), r=['sg%d' % (i % 2), tu], w=['act%d' % i])
                    for j in range(8):
                        py = ps[4 + j % 2]
                        ty = 'ps%d' % (4 + j % 2)
                        P.mm(py[:, 0:tw], [(wob[:, i, 128 * j:128 * j + 128], act[:, i, 0:tw]) for i in range(NH)], r=['act%d' % i for i in range(NH)] + ['wres'], w=[ty])
                        residual_update(py, ty, X, 'xt%d' % b, j, tw, l, s, m)
                    P.dma('pool', xT[:, t0:t0 + tw].rearrange('(c p) t -> p c t', p=128), X[:, :, 0:tw], r=['xt%d_%d' % (b, j) for j in range(8)], w=['xT%d' % t0])
                P.barrier()

        def phase_outproj(l):
            FA.reset()
            BA.reset()
            BW.reset()
            wo = BW.take(8 * 1024, 'p (c n) -> p c n', c=8)
            stg = [FA.take(1024) for _ in range(2)]
            for c in range(8):
                bb = c % 2
                P.dma('sp', stg[bb], I['w_out'][l, 128 * c:128 * c + 128, :], w=['stg%d' % bb])
                P.op('pool', _mk('tensor_copy', wo[:, c, :], stg[bb][:]), r=['stg%d' % bb], w=['wres'])
            xt = [FA.take(4096, 'p (c t) -> p c t', c=8) for _ in range(2)]
            mt = [BA.take(4096, 'p (c t) -> p c t', c=8) for _ in range(2)]
            for ti, (t0, tw, m) in enumerate(TILES):
                b = ti % 2
                M = mt[b]
                X = xt[b]
                P.dma('sp', M[:, :, 0:tw], mixT[:, t0:t0 + tw].rearrange('(c p) t -> p c t', p=128), w=['mt%d' % b])
                P.dma('sp', X[:, :, 0:tw], xT[:, t0:t0 + tw].rearrange('(c p) t -> p c t', p=128), w=['xt%d_%d' % (b, j) for j in range(8)])
                for j in range(8):
                    py = ps[j % 4]
                    ty = 'ps%d' % (j % 4)
                    P.mm(py[:, 0:tw], [(wo[:, c, 128 * j:128 * j + 128], M[:, c, 0:tw]) for c in range(8)], r=['mt%d' % b, 'wres'], w=[ty])
                    residual_update(py, ty, X, 'xt%d' % b, j, tw, l, 1, m)
                P.dma('pool', xT[:, t0:t0 + tw].rearrange('(c p) t -> p c t', p=128), X[:, :, 0:tw], r=['xt%d_%d' % (b, j) for j in range(8)], w=['xT%d' % t0])
            P.barrier()

        def rstd_from(pt, ptok, dst, dtok, tw, n, parts=128):
            P.op('act', _mk('activation', out=dst[0:parts, 0:tw], in_=pt[0:parts, 0:tw], func=AF.Sqrt, bias=epsT[0:parts, 0:1], scale=1.0 / n), r=[ptok, 'epsT'], w=[dtok])
            P.op('dve', _mk('reciprocal', out=dst[0:parts, 0:tw], in_=dst[0:parts, 0:tw]), r=[dtok], w=[dtok])

        def phase_inproj(l):
            FA.reset()
            BA.reset()
            BW.reset()
            win = BW.take(8 * INW, 'p (c n) -> p c n', c=8)
            wuq = BW.take(3 * 768, 'p (c n) -> p c n', c=3)
            wkp = BW.take(2 * 768, 'p (c n) -> p c n', c=2)
            wvv = BW.take(2 * 512, 'p (c n) -> p c n', c=2)
            stg = [FA.take(INW) for _ in range(2)]
            k = 0
            for c in range(8):
                bb = k % 2
                P.dma('sp', stg[bb], I['w_in'][l, 128 * c:128 * c + 128, :], w=['stg%d' % bb])
                P.op('pool', _mk('tensor_copy', win[:, c, :], stg[bb][:]), r=['stg%d' % bb], w=['wres'])
                k += 1
            for dst, name, ncc, n in ((wuq, 'w_uq', 3, 768), (wkp, 'wk_pad', 2, 768), (wvv, 'wv', 2, 512)):
                for c in range(ncc):
                    bb = k % 2
                    P.dma('sp', stg[bb][:, 0:n], I[name][l, 128 * c:128 * c + 128, :], w=['stg%d' % bb])
                    P.op('pool', _mk('tensor_copy', dst[:, c, :], stg[bb][:, 0:n]), r=['stg%d' % bb], w=['wres'])
                    k += 1
            P.barrier()
            FA.reset()
            LBB = FA.take(512)
            OMLB = FA.take(512)
            lbl = FA.take(2048)
            tmpA = FA.take(512)
            tmpB = FA.take(512)
            P.dma('sp', lbl, I['lblB'][:, :], w=['lbl'])
            Lr = lambda i: lbl[:, 512 * i:512 * i + 512]
            P.op('dve', _mk('tensor_tensor', out=tmpA, in0=Lr(0), in1=Lr(1), op=ALU.max), r=['lbl'], w=['tA'])
            P.op('dve', _mk('tensor_tensor', out=tmpA, in0=tmpA, in1=Lr(2), op=ALU.max), r=['tA'], w=['tA'])
            P.op('dve', _mk('tensor_tensor', out=tmpA, in0=tmpA, in1=Lr(3), op=ALU.max), r=['tA'], w=['tA'])
            for i in range(4):
                P.op('dve', _mk('tensor_tensor', out=Lr(i), in0=Lr(i), in1=tmpA, op=ALU.subtract), r=['tA', 'lbl'], w=['lbl'])
            P.op('act', _mk('activation', out=lbl, in_=lbl, func=AF.Exp), r=['lbl'], w=['lbl'])
            P.op('dve', _mk('tensor_tensor', out=tmpB, in0=Lr(0), in1=Lr(1), op=ALU.add), r=['lbl'], w=['tB'])
            P.op('dve', _mk('tensor_tensor', out=tmpB, in0=tmpB, in1=Lr(2), op=ALU.add), r=['tB'], w=['tB'])
            P.op('dve', _mk('tensor_tensor', out=tmpB, in0=tmpB, in1=Lr(3), op=ALU.add), r=['tB'], w=['tB'])
            P.op('dve', _mk('reciprocal', out=tmpB, in_=tmpB), r=['tB'], w=['tB'])
            if l == 0:
                P.op('dve', _mk('memset', LBB, 0.0), w=['LBB'])
            else:
                P.op('dve', _mk('tensor_copy', LBB, Lr(1)), r=['lbl'], w=['LBB'])
                for i in range(2, l + 1):
                    P.op('dve', _mk('tensor_tensor', out=LBB, in0=LBB, in1=Lr(i), op=ALU.add), r=['lbl', 'LBB'], w=['LBB'])
                P.op('dve', _mk('tensor_tensor', out=LBB, in0=LBB, in1=tmpB, op=ALU.mult), r=['tB', 'LBB'], w=['LBB'])
            P.op('dve', _mk('tensor_scalar', out=OMLB, in0=LBB, scalar1=-1.0, scalar2=1.0, op0=ALU.mult, op1=ALU.add), r=['LBB'], w=['OMLB'])
            P.barrier()
            FA.off = 1024
            if STOP == 0:
                P.barrier()
                return
            SL = [FA.take(512) for _ in range(12)]
            ST = ['S%d' % i for i in range(12)]
            cqF = FA.take(1536, 'p (c t) -> p c t', c=3)
            ckvF = FA.take(1024, 'p (c t) -> p c t', c=2)
            rs = FA.take(512)
            csT = FA.take(512, parts=96)
            snT = FA.take(512, parts=96)
            uF = FA.take(1024, 'p (c t) -> p c t', c=2)
            ebT = FA.take(64)
            ht = [BW.take(4096, 'p (c t) -> p c t', c=8) for _ in range(2)]
            kh16 = [BA.take(512) for _ in range(4)]
            vh16 = [BA.take(256) for _ in range(4)]
            qk16 = [BA.take(512) for _ in range(4)]
            g16 = [BA.take(512) for _ in range(2)]
            sq3 = BA.take(1536, 'p (c t) -> p c t', c=3)
            cqn = BA.take(1536, 'p (c t) -> p c t', c=3)
            ckvn = BA.take(1024, 'p (c t) -> p c t', c=2)
            kpe16 = BA.take(512, parts=32)
            sqh = [BA.take(512) for _ in range(4)]
            o16 = [BA.take(512) for _ in range(4)]
            v16 = [BA.take(520, 'p (h e) -> p h e', h=8) for _ in range(2)]
            for b in range(2):
                P.op('pool', _mk('memset', v16[b][:, :, 64:65], 1.0), w=['v16_%d' % b])
            gq = lambda c: gains[:, 4 + l * 3 + c:5 + l * 3 + c]
            gkv = lambda c: gains[:, 16 + l * 2 + c:17 + l * 2 + c]
            cnt = {'ps': 0}

            def nps():
                i = cnt['ps'] % 8
                cnt['ps'] += 1
                return (ps[i], 'ps%d' % i)

            def nps4x():
                i = cnt['ps'] % 4
                cnt['ps'] += 1
                return (ps[i], 'ps%d' % i)

            def featmm(col0, ncols, H, b, tw):
                pt, tok = nps()
                P.mm(pt[0:ncols, 0:tw], [(win[:, c, col0:col0 + ncols], H[:, c, 0:tw]) for c in range(8)], r=['ht%d' % b, 'wres'], w=[tok])
                return (pt, tok)

            def heads_block(mmfn, gcol, rope, dst_dram, t0, tw):
                gsc = gains[0:96, gcol:gcol + 1]
                for h0 in (0, 4):
                    J = range(4)
                    for j in J:
                        mmfn(h0 + j, ps[j], 'ps%d' % j)
                    for j in J:
                        P.op('act', _mk('activation', out=sqh[j][0:96, 0:tw], in_=ps[j][0:96, 0:tw], func=AF.Square), r=['ps%d' % j], w=['sqh%d' % j])
                    for j in J:
                        P.mm(ps[4 + j][0:96, 0:tw], [(ones16[0:96, 0:96], sqh[j][0:96, 0:tw])], r=['sqh%d' % j, 'ones16'], w=['ps%d' % (4 + j)])
                    for j in J:
                        P.op('act', _mk('activation', out=SL[3 * j][0:96, 0:tw], in_=ps[4 + j][0:96, 0:tw], func=AF.Sqrt, bias=epsT[0:96, 0:1], scale=1.0 / 96.0), r=['ps%d' % (4 + j), 'epsT'], w=[ST[3 * j]])
                    for j in J:
                        P.op('dve', _mk('reciprocal', out=SL[3 * j][0:96, 0:tw], in_=SL[3 * j][0:96, 0:tw]), r=[ST[3 * j]], w=[ST[3 * j]])
                    for j in J:
                        P.op('dve', _mk('scalar_tensor_tensor', out=SL[3 * j + 1][0:96, 0:tw], in0=ps[j][0:96, 0:tw], scalar=gsc, in1=SL[3 * j][0:96, 0:tw], op0=ALU.mult, op1=ALU.mult), r=['ps%d' % j, ST[3 * j], 'gains'], w=[ST[3 * j + 1]])
                    if rope:
                        for j in J:
                            P.mm(ps[4 + j][0:96, 0:tw], [(rotF[:, :], SL[3 * j + 1][0:96, 0:tw])], r=[ST[3 * j + 1], 'rot'], w=['ps%d' % (4 + j)])
                        for j in J:
                            P.op('dve', _mk('tensor_tensor', out=SL[3 * j + 2][0:96, 0:tw], in0=SL[3 * j + 1][0:96, 0:tw], in1=csT[0:96, 0:tw], op=ALU.mult), r=[ST[3 * j + 1], 'cs'], w=[ST[3 * j + 2]])
                        for j in J:
                            P.op('dve', _mk('tensor_tensor', out=SL[3 * j + 1][0:96, 0:tw], in0=ps[4 + j][0:96, 0:tw], in1=snT[0:96, 0:tw], op=ALU.mult), r=['ps%d' % (4 + j), 'sn'], w=[ST[3 * j + 1]])
                        for j in J:
                            P.op('dve', _mk('tensor_tensor', out=o16[j][0:96, 0:tw], in0=SL[3 * j + 2][0:96, 0:tw], in1=SL[3 * j + 1][0:96, 0:tw], op=ALU.add), r=[ST[3 * j + 2], ST[3 * j + 1]], w=['o16_%d' % j])
                    else:
                        for j in J:
                            P.op('act', _mk('activation', out=o16[j][0:96, 0:tw], in_=SL[3 * j + 1][0:96, 0:tw], func=AF.Copy), r=[ST[3 * j + 1]], w=['o16_%d' % j])
                    for j in J:
                        P.dma('pool', dst_dram[h0 + j, :, t0:t0 + tw], o16[j][0:96, 0:tw], r=['o16_%d' % j], w=['hd'])

            for ti, (t0, tw, m) in enumerate(TILES):
                b = ti % 2
                H = ht[b]
                P.dma('sp', H[:, :, 0:tw], hT[:, t0:t0 + tw].rearrange('(c p) t -> p c t', p=128), w=['ht%d' % b])
                rope = m == 0
                if rope:
                    P.dma('sp', csT[:, 0:tw], I['cosT'][:, t0:t0 + tw], w=['cs'])
                    P.dma('sp', snT[:, 0:tw], I['sinT'][:, t0:t0 + tw], w=['sn'])
                NB = range(tw // 128)
                Fs = lambda i: SL[3 * i]
                Ks = lambda i: SL[3 * i + 1]
                Es = lambda i: SL[3 * i + 2]
                tF = lambda i: ST[3 * i]
                tK = lambda i: ST[3 * i + 1]
                tE = lambda i: ST[3 * i + 2]
                for i in NB:
                    P.mm(ps[i][:, 0:512], [(H[:, c, 128 * i:128 * i + 128], win[:, c, 256:768]) for c in range(8)], r=['ht%d' % b, 'wres'], w=['ps%d' % i])
                for i in NB:
                    P.op('act', _mk('activation', out=Fs(i), in_=ps[i][:, 0:512], func=AF.Sigmoid), r=['ps%d' % i], w=[tF(i)])
                for i in NB:
                    P.op('dve', _mk('tensor_tensor', out=Fs(i), in0=Fs(i), in1=OMLB, op=ALU.mult), r=[tF(i), 'OMLB'], w=[tF(i)])
                    P.op('dve', _mk('tensor_tensor', out=Fs(i), in0=Fs(i), in1=LBB, op=ALU.add), r=[tF(i), 'LBB'], w=[tF(i)])
                    P.op('dve', _mk('tensor_scalar', out=Ks(i), in0=Fs(i), scalar1=-1.0, scalar2=1.0, op0=ALU.mult, op1=ALU.add), r=[tF(i)], w=[tK(i)])
                    P.op('dve', _mk('tensor_scalar', out=Fs(i), in0=Fs(i), scalar1=1e-06, scalar2=1.0, op0=ALU.max, op1=ALU.min), r=[tF(i)], w=[tF(i)])
                for i in NB:
                    P.op('act', _mk('activation', out=Fs(i), in_=Fs(i), func=AF.Ln), r=[tF(i)], w=[tF(i)])
                for i in NB:
                    lf = Fs(i)
                    P.mm(ps[i][:, 0:256], [(d1f[:], lf[:, 0:256])], r=[tF(i), 'd1f'], w=['ps%d' % i])
                    P.mm(ps[i][:, 256:512], [(d1b[:], lf[:, 256:512])], r=[tF(i), 'd1b'], w=['ps%d' % i])
                    for g in range(4):
                        dd, pr = (g // 2, g % 2)
                        P.mm(ps[4 + g][:, 128 * i:128 * i + 128], [(lf[:, dd * 256 + pr * 128:dd * 256 + pr * 128 + 128], (trf if dd == 0 else trb)[:])], r=[tF(i), 'trf', 'trb'], w=['ps%d' % (4 + g)])
                for i in NB:
                    P.op('act', _mk('activation', out=Es(i), in_=ps[i][:, 0:512], func=AF.Exp), r=['ps%d' % i], w=[tE(i)])
                for i in NB:
                    P.op('dve', _mk('tensor_tensor', out=kh16[i], in0=Ks(i), in1=Es(i), op=ALU.mult), r=[tK(i), tE(i)], w=['kh16_%d' % i])
                    P.dma('pool', KH[t0 + 128 * i:t0 + 128 * i + 128, :], kh16[i], r=['kh16_%d' % i], w=['KHd'])
                for i in NB:
                    P.mm(ps[i][:, 0:256], [(H[:, c, 128 * i:128 * i + 128], win[:, c, 768:1024]) for c in range(8)], r=['ht%d' % b, 'wres'], w=['ps%d' % i])
                for i in NB:
                    P.op('act', _mk('activation', out=vh16[i], in_=ps[i][:, 0:256], func=AF.Copy), r=['ps%d' % i], w=['vh16_%d' % i])
                    P.dma('pool', VH[t0 + 128 * i:t0 + 128 * i + 128, :], vh16[i], r=['vh16_%d' % i], w=['VHd'])
                cnt['ps'] = 0
                for g in range(4):
                    dd, pr = (g // 2, g % 2)
                    pb_, tb_ = (ps[4 + g], 'ps%d' % (4 + g))
                    s0 = 3 * (g % 3)
                    eb, enb, kkf = (SL[s0], SL[s0 + 1], SL[s0 + 2])
                    teb, tenb, tkkf = (ST[s0], ST[s0 + 1], ST[s0 + 2])
                    qs, tqs = (SL[9 + pr], ST[9 + pr])
                    P.op('act', _mk('activation', out=eb[:, 0:tw], in_=pb_[:, 0:tw], func=AF.Exp), r=[tb_], w=[teb])
                    P.op('act', _mk('activation', out=enb[:, 0:tw], in_=pb_[:, 0:tw], func=AF.Exp, scale=-1.0), r=[tb_], w=[tenb])
                    if dd == 0:
                        pq, tq = nps4x()
                        P.mm(pq[:, 0:tw], [(win[:, c, 128 * pr:128 * pr + 128], H[:, c, 0:tw]) for c in range(8)], r=['ht%d' % b, 'wres'], w=[tq])
                        P.op('act', _mk('activation', out=qs[:, 0:tw], in_=pq[:, 0:tw], func=AF.Silu), r=[tq], w=[tqs])
                    qb, tqb = (qk16[2 * (g % 2)], 'qk16_%d' % (2 * (g % 2)))
                    P.op('dve', _mk('tensor_tensor', out=qb[:, 0:tw], in0=qs[:, 0:tw], in1=eb[:, 0:tw], op=ALU.mult), r=[tqs, teb], w=[tqb])
                    P.dma('pool', HQ[dd, 128 * pr:128 * pr + 128, t0:t0 + tw], qb[:, 0:tw], r=[tqb], w=['HQd'])
                    pk, tkk = nps4x()
                    c0_ = 256 + dd * 256 + 128 * pr
                    P.mm(pk[:, 0:tw], [(win[:, c, c0_:c0_ + 128], H[:, c, 0:tw]) for c in range(8)], r=['ht%d' % b, 'wres'], w=[tkk])
                    P.op('act', _mk('activation', out=kkf[:, 0:tw], in_=pk[:, 0:tw], func=AF.Sigmoid, scale=-1.0), r=[tkk], w=[tkkf])
                    kb, tkb = (qk16[2 * (g % 2) + 1], 'qk16_%d' % (2 * (g % 2) + 1))
                    gi = l * 4 + dd * 2 + pr
                    P.op('dve', _mk('scalar_tensor_tensor', out=kb[:, 0:tw], in0=kkf[:, 0:tw], scalar=omlF[:, gi:gi + 1], in1=enb[:, 0:tw], op0=ALU.mult, op1=ALU.mult), r=[tkkf, tenb, 'omlF'], w=[tkb])
                    P.dma('pool', HK[dd, 128 * pr:128 * pr + 128, t0:t0 + tw], kb[:, 0:tw], r=[tkb], w=['HKd'])
                    nch = tw // CH
                    ebv = eb[:, 0:tw].rearrange('p (n s) -> p n s', s=CH)
                    sel = ebv[:, :, CH - 1:CH] if dd == 0 else ebv[:, :, 0:1]
                    P.op('dve', _mk('tensor_copy', ebT[:, 16 * g:16 * g + nch].rearrange('p (n o) -> p n o', o=1), sel), r=[teb], w=['ebT%d' % g])
                    P.dma('pool', EBE[dd, pr, :, t0 // CH:t0 // CH + nch], ebT[:, 16 * g:16 * g + nch], r=['ebT%d' % g], w=['EBEd'])
                cnt['ps'] = 0
                for pr in range(2):
                    pg, tg = featmm(1024 + 128 * pr, 128, H, b, tw)
                    P.op('act', _mk('activation', out=g16[pr][:, 0:tw], in_=pg[:, 0:tw], func=AF.Silu), r=[tg], w=['g16_%d' % pr])
                    P.dma('pool', GT[128 * pr:128 * pr + 128, t0:t0 + tw], g16[pr][:, 0:tw], r=['g16_%d' % pr], w=['GTd'])
                for ch in range(2):
                    pu, tu = featmm(1952 + 128 * ch, 128, H, b, tw)
                    P.op('dve', _mk('tensor_copy', uF[:, ch, 0:tw], pu[:, 0:tw]), r=[tu], w=['uF%d' % ch])
                P.dma('pool', uT[:, t0:t0 + tw].rearrange('(c p) t -> p c t', p=128), uF[:, :, 0:tw], r=['uF0', 'uF1'], w=['uTd'])
                pcs = []
                for c in range(3):
                    pcs.append(featmm(1280 + 128 * c, 128, H, b, tw))
                for c in range(3):
                    pc, tc_ = pcs[c]
                    P.op('act', _mk('activation', out=sq3[:, c, 0:tw], in_=pc[:, 0:tw], func=AF.Square), r=[tc_], w=['sq3_%d' % c])
                for c in range(3):
                    pc, tc_ = pcs[c]
                    P.op('dve', _mk('tensor_copy', cqF[:, c, 0:tw], pc[:, 0:tw]), r=[tc_], w=['cqF%d' % c])
                pss, tss = nps()
                P.mm(pss[:, 0:tw], [(ones16[:], sq3[:, c, 0:tw]) for c in range(3)], r=['sq3_0', 'sq3_1', 'sq3_2', 'ones16'], w=[tss])
                rstd_from(pss, tss, rs, 'rs', tw, 384.0)
                for c in range(3):
                    P.op('dve', _mk('scalar_tensor_tensor', out=cqn[:, c, 0:tw], in0=cqF[:, c, 0:tw], scalar=gq(c), in1=rs[:, 0:tw], op0=ALU.mult, op1=ALU.mult), r=['cqF%d' % c, 'rs', 'gains'], w=['cqn%d' % c])

                def qmm(h, pt, tok, tw=tw):
                    P.mm(pt[0:96, 0:tw], [(wuq[:, c, 96 * h:96 * h + 96], cqn[:, c, 0:tw]) for c in range(3)], r=['cqn0', 'cqn1', 'cqn2', 'wres'], w=[tok])
                heads_block(qmm, 32 + l, rope, QT, t0, tw)
                pcs = []
                for c in range(2):
                    pcs.append(featmm(1664 + 128 * c, 128, H, b, tw))
                for c in range(2):
                    pc, tc_ = pcs[c]
                    P.op('act', _mk('activation', out=sq3[:, c, 0:tw], in_=pc[:, 0:tw], func=AF.Square), r=[tc_], w=['sq3_%d' % c])
                for c in range(2):
                    pc, tc_ = pcs[c]
                    P.op('dve', _mk('tensor_copy', ckvF[:, c, 0:tw], pc[:, 0:tw]), r=[tc_], w=['ckvF%d' % c])
                pss, tss = nps()
                P.mm(pss[:, 0:tw], [(ones16[:], sq3[:, c, 0:tw]) for c in range(2)], r=['sq3_0', 'sq3_1', 'ones16'], w=[tss])
                rstd_from(pss, tss, rs, 'rs', tw, 256.0)
                for c in range(2):
                    P.op('dve', _mk('scalar_tensor_tensor', out=ckvn[:, c, 0:tw], in0=ckvF[:, c, 0:tw], scalar=gkv(c), in1=rs[:, 0:tw], op0=ALU.mult, op1=ALU.mult), r=['ckvF%d' % c, 'rs', 'gains'], w=['ckvn%d' % c])
                pkp, tkp = featmm(1920, 32, H, b, tw)
                P.op('act', _mk('activation', out=kpe16[0:32, 0:tw], in_=pkp[0:32, 0:tw], func=AF.Copy), r=[tkp], w=['kpe16'])

                def kmm(h, pt, tok, tw=tw):
                    P.mm(pt[0:96, 0:tw], [(wkp[:, c, 96 * h:96 * h + 96], ckvn[:, c, 0:tw]) for c in range(2)] + [(sh16[:, :], kpe16[0:32, 0:tw])], r=['ckvn0', 'ckvn1', 'kpe16', 'sh16', 'wres'], w=[tok])
                heads_block(kmm, 36 + l, rope, KT, t0, tw)
                for i in range(tw // 128):
                    bb = i % 2
                    pv, tv = nps()
                    P.mm(pv[:, 0:512], [(ckvn[:, c, 128 * i:128 * i + 128], wvv[:, c, :]) for c in range(2)], r=['ckvn0', 'ckvn1', 'wres'], w=[tv])
                    P.op('act', _mk('activation', out=v16[bb][:, :, 0:64], in_=pv[:, 0:512].rearrange('p (h e) -> p h e', h=8), func=AF.Copy), r=[tv], w=['v16_%d' % bb])
                    P.dma('pool', VV[t0 + 128 * i:t0 + 128 * i + 128, :, :], v16[bb], r=['v16_%d' % bb], w=['VVd'])
            P.barrier()

        def phase_hgrn(l):
            FA.reset()
            BA.reset()
            BW.reset()
            chains = [(dd, pr) for dd in range(2) for pr in range(2)]
            qt = {}
            kt = {}
            kh = {}
            vh = {}
            eb = {}
            osb = {}
            a16 = {}
            for ci, ch in enumerate(chains):
                qt[ch] = BW.take(512)
                kt[ch] = BW.take(512)
                kh[ch] = BW.take(2048, 'p (n c) -> p n c', n=16, parts=32)
                vh[ch] = BW.take(2048, 'p (n c) -> p n c', n=16, parts=32)
                eb[ch] = FA.take(16)
                osb[ch] = FA.take(512)
                a16[ch] = BA.take(64, 'p (h c) -> p h c', h=2, parts=32)
            P.op('dve', _mk('memset', S32[:], 0.0), w=['S32_%d' % i for i in range(4)])
            P.op('pool', _mk('memset', S16[:], 0.0), w=['S16_%d' % i for i in range(4)])
            order = {0: [TILES[16]] + TILES[0:16], 1: [TILES[16]] + TILES[15::-1]}
            for step in range(17):
                for ci, ch in enumerate(chains):
                    dd, pr = ch
                    t0, tw, m = order[dd][step]
                    nch = tw // CH
                    c0 = 'c%d' % ci
                    P.dma('sp', qt[ch][:, 0:tw], HQ[dd, 128 * pr:128 * pr + 128, t0:t0 + tw], w=[c0 + 'q'])
                    P.dma('sp', kt[ch][:, 0:tw], HK[dd, 128 * pr:128 * pr + 128, t0:t0 + tw], w=[c0 + 'k'])
                    P.dma('sp', kh[ch][:, 0:nch, :], KH[t0:t0 + tw, dd * 256 + pr * 128:dd * 256 + pr * 128 + 128].rearrange('(n s) c -> s n c', s=CH), w=[c0 + 'kh'])
                    P.dma('sp', vh[ch][:, 0:nch, :], VH[t0:t0 + tw, pr * 128:pr * 128 + 128].rearrange('(n s) c -> s n c', s=CH), w=[c0 + 'vh'])
                    P.dma('sp', eb[ch][:, 0:nch], EBE[dd, pr, :, t0 // CH:t0 // CH + nch], w=[c0 + 'eb'])
                nchs = order[0][step][1] // CH
                for cidx in range(nchs):
                    info = []
                    for ci, ch in enumerate(chains):
                        dd, pr = ch
                        t0, tw, m = order[dd][step]
                        nch = tw // CH
                        cc = cidx if dd == 0 else nch - 1 - cidx
                        info.append(dict(ci=ci, ch=ch, dd=dd, c0='c%d' % ci, bank=ps[ci], btok='ps%d' % ci, obank=ps[4 + ci], otok='ps%d' % (4 + ci),
                                         msk=maskf if dd == 0 else maskb, s32=S32[:, 64 * ci:64 * ci + 64], s16=S16[:, 64 * ci:64 * ci + 64],
                                         cc=cc, sl=slice(CH * cc, CH * cc + CH)))
                    first = cidx == 0
                    for d in info:
                        ch, cc, sl, c0, bank, btok = (d['ch'], d['cc'], d['sl'], d['c0'], d['bank'], d['btok'])
                        for hh in range(2):
                            pb = 64 * hh
                            P.mm(bank[pb:pb + 64, 0:64], [(kh[ch][0:32, cc, pb:pb + 64], vh[ch][0:32, cc, pb:pb + 64])], r=[c0 + 'kh', c0 + 'vh'], w=[btok + 'U'])
                            P.mm(bank[0:32, 64 + 32 * hh:96 + 32 * hh], [(kt[ch][pb:pb + 64, sl], qt[ch][pb:pb + 64, sl])], r=[c0 + 'k', c0 + 'q'], w=[btok + 'A%d' % hh])
                    for d in info:
                        ch, c0, bank, btok, msk = (d['ch'], d['c0'], d['bank'], d['btok'], d['msk'])
                        for hh in range(2):
                            P.op('dve', _mk('tensor_tensor', out=a16[ch][0:32, hh, :], in0=bank[0:32, 64 + 32 * hh:96 + 32 * hh], in1=msk[:, :], op=ALU.mult), r=[btok + 'A%d' % hh, 'maskf', 'maskb'], w=[c0 + 'a%d' % hh])
                    for d in info:
                        ch, cc, sl, c0, ci, obank, otok, s16 = (d['ch'], d['cc'], d['sl'], d['c0'], d['ci'], d['obank'], d['otok'], d['s16'])
                        for hh in range(2):
                            pb = 64 * hh
                            P.mm(obank[pb:pb + 64, sl], [(s16[pb:pb + 64, :], qt[ch][pb:pb + 64, sl]), (vh[ch][0:32, cc, pb:pb + 64], a16[ch][0:32, hh, :])], r=['S16_%d' % ci, c0 + 'q', c0 + 'vh', c0 + 'a%d' % hh], w=[otok] if first else [otok + 'x'])
                    for d in info:
                        ch, cc, c0, ci, bank, btok, s32 = (d['ch'], d['cc'], d['c0'], d['ci'], d['bank'], d['btok'], d['s32'])
                        P.op('dve', _mk('scalar_tensor_tensor', out=s32, in0=s32, scalar=eb[ch][:, cc:cc + 1], in1=bank[:, 0:64], op0=ALU.mult, op1=ALU.add), r=[btok + 'U', c0 + 'eb', 'S32_%d' % ci], w=['S32_%d' % ci])
                    for d in info:
                        ci, s32, s16 = (d['ci'], d['s32'], d['s16'])
                        P.op('act', _mk('activation', out=s16, in_=s32, func=AF.Copy), r=['S32_%d' % ci], w=['S16_%d' % ci])
                for ci, ch in enumerate(chains):
                    dd, pr = ch
                    t0, tw, m = order[dd][step]
                    c0 = 'c%d' % ci
                    obank = ps[4 + ci]
                    otok = 'ps%d' % (4 + ci)
                    P.op('act', _mk('activation', out=osb[ch][:, 0:tw], in_=obank[:, 0:tw], func=AF.Copy), r=[otok, otok + 'x'], w=[c0 + 'o'])
                    P.dma('pool', OFB[dd, 128 * pr:128 * pr + 128, t0:t0 + tw], osb[ch][:, 0:tw], r=[c0 + 'o'], w=['OFBd'])
            P.barrier()
            FA.reset()
            BA.reset()
            of = [FA.take(512) for _ in range(2)]
            obb = [FA.take(512) for _ in range(2)]
            rs = FA.take(512)
            gt = [BA.take(512) for _ in range(2)]
            sq = BA.take(512)
            mo = [BA.take(512) for _ in range(2)]
            k = 0
            for t0, tw, m in TILES:
                for pr in range(2):
                    b = k % 2
                    k += 1
                    P.dma('sp', of[b][:, 0:tw], OFB[0, 128 * pr:128 * pr + 128, t0:t0 + tw], w=['of%d' % b])
                    P.dma('sp', obb[b][:, 0:tw], OFB[1, 128 * pr:128 * pr + 128, t0:t0 + tw], w=['ob%d' % b])
                    P.dma('sp', gt[b][:, 0:tw], GT[128 * pr:128 * pr + 128, t0:t0 + tw], w=['gt%d' % b])
                    P.op('dve', _mk('tensor_tensor', out=of[b][:, 0:tw], in0=of[b][:, 0:tw], in1=obb[b][:, 0:tw], op=ALU.add), r=['ob%d' % b], w=['of%d' % b])
                    P.op('act', _mk('activation', out=sq[:, 0:tw], in_=of[b][:, 0:tw], func=AF.Square), r=['of%d' % b], w=['sq'])
                    pt, tok = (ps[k % 4], 'ps%d' % (k % 4))
                    P.mm(pt[:, 0:tw], [(bd16[:], sq[:, 0:tw])], r=['sq', 'bd16'], w=[tok])
                    rstd_from(pt, tok, rs, 'rs', tw, 64.0)
                    P.op('dve', _mk('scalar_tensor_tensor', out=of[b][:, 0:tw], in0=of[b][:, 0:tw], scalar=gains[:, l:l + 1], in1=rs[:, 0:tw], op0=ALU.mult, op1=ALU.mult), r=['rs', 'gains'], w=['of%d' % b])
                    P.op('dve', _mk('tensor_tensor', out=mo[b][:, 0:tw], in0=of[b][:, 0:tw], in1=gt[b][:, 0:tw], op=ALU.mult), r=['of%d' % b, 'gt%d' % b], w=['mo%d' % b])
                    P.dma('pool', mixT[128 * pr:128 * pr + 128, t0:t0 + tw], mo[b][:, 0:tw], r=['mo%d' % b], w=['mixd'])
            P.barrier()

        def phase_attn(l, with_ctx):
            FA.reset()
            BA.reset()
            BW.reset()
            ktb = [BW.take(T, parts=96) for _ in range(2)]
            vtb = [BW.take(66 * 65, 'p (k e) -> p k e', e=65) for _ in range(2)]
            qtb = [BA.take(512, parts=96) for _ in range(2)]
            pb16 = [BA.take(512) for _ in range(4)]
            mo = [BA.take(512, parts=64) for _ in range(2)]
            osb = [FA.take(512, parts=64) for _ in range(2)]
            rden = FA.take(512)
            scale = 96.0 ** (-0.5)
            jobs = []
            for h in range(8):
                qtiles = [(t0, tw, list(range(66))) for t0, tw, m in TILES if m == 0]
                if with_ctx:
                    qtiles.append((TL, TC, [64, 65]))
                for t0, tw, kcs in qtiles:
                    jobs.append(dict(h=h, hb=h % 2, t0=t0, tw=tw, kcs=kcs, qb=len(jobs) % 2, first_of_head=(t0 == 0)))
            items = []
            for ji, jb in enumerate(jobs):
                for n, kc in enumerate(jb['kcs']):
                    items.append((ji, n, kc))

            def load_head(h):
                hb = h % 2
                P.dma('sp', ktb[hb][:, :], KT[h, :, :], w=['kt%d' % hb])
                P.dma('sp', vtb[hb][:, :, :], VV[:, h, :].rearrange('(k p) e -> p k e', p=128), w=['vt%d' % hb])

            def load_q(ji):
                jb = jobs[ji]
                P.dma('sp', qtb[jb['qb']][:, 0:jb['tw']], QT[jb['h'], :, jb['t0']:jb['t0'] + jb['tw']], w=['q%d' % jb['qb']])

            def emit_qk(idx):
                ji, n, kc = items[idx]
                jb = jobs[ji]
                tw, hb, qb = (jb['tw'], jb['hb'], jb['qb'])
                if n == 0 and ji + 1 < len(jobs):
                    load_q(ji + 1)
                bank = idx % 4
                P.mm(ps[bank][:, 0:tw], [(ktb[hb][:, 128 * kc:128 * kc + 128], qtb[qb][:, 0:tw])], r=['kt%d' % hb, 'q%d' % qb], w=['ps%d' % bank])

            def emit_pv(idx):
                ji, n, kc = items[idx]
                jb = jobs[ji]
                tw, hb, qb, h, t0 = (jb['tw'], jb['hb'], jb['qb'], jb['h'], jb['t0'])
                bank = idx % 4
                pbuf = pb16[bank]
                tpb = 'pb%d' % bank
                po = ps[4 + qb]
                tpo = 'ps%d' % (4 + qb)
                P.op('act', _mk('activation', out=pbuf[:, 0:tw], in_=ps[bank][:, 0:tw], func=AF.Exp, scale=scale), r=['ps%d' % bank], w=[tpb])
                first = n == 0
                last = n == len(jb['kcs']) - 1
                P.op('pe', _mk('matmul', po[0:65, 0:tw], vtb[hb][:, kc, :], pbuf[:, 0:tw], start=first, stop=last), r=[tpb, 'vt%d' % hb], w=[tpo] if first else [tpo + 'x'], signal=True)
                if first and jb['first_of_head'] and h + 1 < 8:
                    load_head(h + 1)
                if last:
                    P.op('dve', _mk('reciprocal', out=rden[64:65, 0:tw], in_=po[64:65, 0:tw]), r=[tpo, tpo + 'x'], w=['rden'])
                    P.op('act', _mk('activation', out=osb[qb][0:64, 0:tw], in_=po[0:64, 0:tw], func=AF.Copy), r=[tpo, tpo + 'x'], w=['osb%d' % qb])
                    P.mm(ps[6][0:64, 0:tw], [(onesF[64:65, 0:64], rden[64:65, 0:tw])], r=['rden', 'onesF'], w=['ps6'])
                    P.op('dve', _mk('tensor_tensor', out=mo[qb][0:64, 0:tw], in0=osb[qb][0:64, 0:tw], in1=ps[6][0:64, 0:tw], op=ALU.mult), r=['osb%d' % qb, 'ps6'], w=['mo%d' % qb])
                    P.dma('pool', mixT[256 + 64 * h:256 + 64 * h + 64, t0:t0 + tw], mo[qb][0:64, 0:tw], r=['mo%d' % qb], w=['mixd'])
            load_head(0)
            load_q(0)
            LA = 3
            for i in range(len(items) + LA):
                if i < len(items):
                    emit_qk(i)
                if i >= LA:
                    emit_pv(i - LA)
            P.barrier()

        def phase_pool(l, with_ctx):
            FA.reset()
            BA.reset()
            BW.reset()
            pw = BW.take(256, 'p (c n) -> p c n', c=2)
            stg = FA.take(256, 'p (c n) -> p c n', c=2)
            for ch in range(2):
                P.dma('sp', stg[:, ch, :], I['pwb'][l, ch, :, :], w=['stg'])
            P.op('pool', _mk('tensor_copy', pw, stg), r=['stg'], w=['wres'])
            W = 512 + 16
            ub = [FA.take(2 * W, 'p (c t) -> p c t', c=2) for _ in range(2)]
            a1 = FA.take(2 * W, 'p (c t) -> p c t', c=2)
            a2 = FA.take(2 * W, 'p (c t) -> p c t', c=2)
            a3 = FA.take(W)
            a4 = FA.take(W)
            sm = FA.take(1024, 'p (c t) -> p c t', c=2)
            pl16 = BA.take(1024, 'p (c t) -> p c t', c=2)
            mo = [BA.take(512) for _ in range(4)]
            k = 0
            for ti, (t0, tw, m) in enumerate(TILES):
                if m == 1 and (not with_ctx):
                    continue
                b = ti % 2
                U = ub[b]
                seq0, seq1 = (0, TL) if m == 0 else (TL, T)
                lo = max(t0 - 8, seq0)
                hi = min(t0 + tw + 8, seq1)
                if lo > t0 - 8 or hi < t0 + tw + 8:
                    P.op('pool', _mk('memset', U[:, :, :], 0.0), w=['ub%d' % b])
                P.dma('sp', U[:, :, lo - (t0 - 8):hi - (t0 - 8)], uT[:, lo:hi].rearrange('(c p) t -> p c t', p=128), w=['ub%d' % b])
                n1 = tw + 15
                P.op('dve', _mk('tensor_tensor', out=a1[:, :, 0:n1], in0=U[:, :, 0:n1], in1=U[:, :, 1:n1 + 1], op=ALU.add), r=['ub%d' % b], w=['a1'])
                n2 = tw + 13
                P.op('dve', _mk('tensor_tensor', out=a2[:, :, 0:n2], in0=a1[:, :, 0:n2], in1=a1[:, :, 2:n2 + 2], op=ALU.add), r=['a1'], w=['a2'])
                n3 = tw + 9
                P.op('dve', _mk('tensor_tensor', out=a3[:, 0:n3], in0=a2[:, 1, 0:n3], in1=a2[:, 1, 4:n3 + 4], op=ALU.add), r=['a2'], w=['a3'])
                n4 = tw + 1
                P.op('dve', _mk('tensor_tensor', out=a4[:, 0:n4], in0=a3[:, 0:n4], in1=a3[:, 8:n4 + 8], op=ALU.add), r=['a3'], w=['a4'])
                P.op('dve', _mk('tensor_scalar', out=sm[0:64, 0, 0:tw], in0=a1[0:64, 0, 7:7 + tw], scalar1=0.5, scalar2=None, op0=ALU.mult), r=['a1'], w=['sm0a'])
                P.op('dve', _mk('tensor_scalar', out=sm[64:128, 0, 0:tw], in0=a2[64:128, 0, 6:6 + tw], scalar1=0.25, scalar2=None, op0=ALU.mult), r=['a2'], w=['sm0b'])
                P.op('dve', _mk('tensor_scalar', out=sm[0:64, 1, 0:tw], in0=a3[0:64, 4:4 + tw], scalar1=0.125, scalar2=None, op0=ALU.mult), r=['a3'], w=['sm1a'])
                P.op('dve', _mk('tensor_scalar', out=sm[64:128, 1, 0:tw], in0=a4[64:128, 0:tw], scalar1=0.0625, scalar2=None, op0=ALU.mult), r=['a4'], w=['sm1b'])
                smt = ['sm0a', 'sm0b', 'sm1a', 'sm1b']
                if t0 == seq0:
                    P.op('dve', _mk('tensor_tensor', out=sm[:, :, 0:8], in0=sm[:, :, 0:8], in1=corr[:, 0:32].rearrange('p (c t) -> p c t', c=2)[:, :, 0:8], op=ALU.mult), r=smt + ['corr'], w=smt)
                if t0 + tw == seq1:
                    P.op('dve', _mk('tensor_tensor', out=sm[:, :, tw - 8:tw], in0=sm[:, :, tw - 8:tw], in1=corr[:, 0:32].rearrange('p (c t) -> p c t', c=2)[:, :, 8:16], op=ALU.mult), r=smt + ['corr'], w=smt)
                P.op('dve', _mk('tensor_tensor', out=pl16[:, :, 0:tw], in0=sm[:, :, 0:tw], in1=U[:, :, 8:8 + tw], op=ALU.subtract), r=smt + ['ub%d' % b], w=['pl16'])
                for ch in range(2):
                    pt, tok = (ps[k % 4], 'ps%d' % (k % 4))
                    mb = mo[k % 4]
                    mtok = 'mo%d' % (k % 4)
                    k += 1
                    P.mm(pt[:, 0:tw], [(pw[:, ch, :], pl16[:, ch, 0:tw])], r=['pl16', 'wres'], w=[tok])
                    P.op('act', _mk('activation', out=mb[:, 0:tw], in_=pt[:, 0:tw], func=AF.Identity, scale=gains[:, 24 + 2 * l + ch:25 + 2 * l + ch]), r=[tok, 'gains'], w=[mtok])
                    P.dma('pool', mixT[768 + 128 * ch:768 + 128 * ch + 128, t0:t0 + tw], mb[:, 0:tw], r=[mtok], w=['mixd'])
            P.barrier()
        steps = [load_consts, phase_mod, phase_in_transpose]
        for l in range(n_layers):
            last = l == NL - 1
            steps += [
                (lambda l=l: phase_norm(l, 0)),
                (lambda l=l: phase_ffn(l, 0, I['f1i'], I['f1o'])),
                (lambda l=l: phase_norm(l, 1)),
                (lambda l=l: phase_inproj(l)),
                (lambda l=l: phase_hgrn(l)),
                (lambda l=l, last=last: phase_attn(l, not last)),
                (lambda l=l, last=last: phase_pool(l, not last)),
                (lambda l=l: phase_outproj(l)),
                (lambda l=l: phase_norm(l, 2)),
                (lambda l=l: phase_ffn(l, 2, I['f2i'], I['f2o'])),
            ]
        steps.append(phase_out_transpose)
        for si, fn in enumerate(steps):
            if upto is not None and si >= upto:
                break
            try:
                fn()
            except _Stop:
                break
        P.barrier()
        P.emit()
    return nc

def host_constants():
    c = {}
    c['ident'] = np.eye(128, dtype=np.float32)
    s = np.arange(128)[:, None]
    t = np.arange(128)[None, :]
    same = s // CH == t // CH
    c['d1f'] = (same & (s > t)).astype(np.float32)
    c['d1b'] = (same & (s < t)).astype(np.float32)
    c['trf'] = (same & (s <= t)).astype(np.float32)
    c['trb'] = (same & (s >= t)).astype(np.float32)
    s2 = np.arange(CH)[:, None]
    t2 = np.arange(CH)[None, :]
    c['maskf'] = (s2 <= t2).astype(np.float32)
    c['maskb'] = (s2 >= t2).astype(np.float32)
    c['bd64'] = (s // 64 == t // 64).astype(np.float32)
    rot = np.zeros((96, 96), np.float32)
    for i in range(16):
        rot[80 + i, 64 + i] = -1.0
        rot[64 + i, 80 + i] = 1.0
    c['rot'] = rot
    sh = np.zeros((32, 96), np.float32)
    sh[np.arange(32), 64 + np.arange(32)] = 1.0
    c['shift'] = sh
    pos = np.arange(TL)
    row = (pos // 64).astype(np.float32)
    col = (pos % 64).astype(np.float32)
    inv = (np.float32(10000.0) ** (-np.arange(8, dtype=np.float32) / np.float32(8))).astype(np.float32)
    ang = np.concatenate([row[:, None] * inv[None, :], col[:, None] * inv[None, :]], axis=1).astype(np.float32)
    cs = np.cos(ang).astype(np.float32).T
    sn = np.sin(ang).astype(np.float32).T
    cosT = np.ones((96, TL), np.float32)
    sinT = np.zeros((96, TL), np.float32)
    cosT[64:80] = cs
    cosT[80:96] = cs
    sinT[64:80] = sn
    sinT[80:96] = sn
    c['cosT'] = cosT
    c['sinT'] = sinT
    corr = np.ones((128, 2, 16), np.float32)
    for g, w in enumerate((2, 4, 8, 16)):
        ch, p0 = (g // 2, g % 2 * 64)
        for i in range(8):
            lo = max(i - w // 2, 0)
            hi = i + w - 1 - w // 2
            corr[p0:p0 + 64, ch, i] = w / float(hi - lo + 1)
            d = 7 - i
            hi2 = min(w - 1 - w // 2, d)
            cnt = hi2 + w // 2 + 1
            corr[p0:p0 + 64, ch, 8 + i] = w / float(cnt)
    c['corr'] = corr.reshape(128, 32)
    return c
_CACHE = {}

def prep_inputs(inputs, b):
    f = lambda a: np.ascontiguousarray(np.asarray(a, dtype=np.float32))
    m = {}
    m['x_in'] = f(inputs['x'][b])
    m['ctx_in'] = f(inputs['ctx'][b])
    cv = np.stack([np.asarray(inputs['c'][b]), np.asarray(inputs['c_ctx'])], 0)
    m['cvT'] = f(cv.reshape(2, 8, 128).transpose(2, 1, 0).reshape(128, 16))
    m['w_mod'] = f(inputs['w_mod'])
    m['b_mod'] = f(inputs['b_mod'])
    m['f1i'] = f(inputs['ffn1_w_in'])
    m['f1o'] = f(inputs['ffn1_w_out'])
    m['f2i'] = f(inputs['ffn2_w_in'])
    m['f2o'] = f(inputs['ffn2_w_out'])
    m['w_in'] = f(inputs['w_in'])
    m['w_out'] = f(inputs['w_out'])
    lb = np.asarray(inputs['hg_lb_logits'], np.float32)
    m['lblB'] = f(np.broadcast_to(lb.reshape(1, NL * 512), (128, NL * 512)))
    m['lblF'] = f(lb.reshape(NL, 2, 2, 128).transpose(3, 0, 1, 2).reshape(128, 16))
    m['g_hg'] = f(np.tile(np.asarray(inputs['hg_out_gain'], np.float32), (1, 2)).T)
    m['g_qa'] = f(np.asarray(inputs['mla_q_a_gain'], np.float32).reshape(NL, 3, 128).transpose(2, 0, 1).reshape(128, NL * 3))
    m['g_kva'] = f(np.asarray(inputs['mla_kv_a_gain'], np.float32).reshape(NL, 2, 128).transpose(2, 0, 1).reshape(128, NL * 2))
    m['g_q'] = f(np.asarray(inputs['mla_q_gain'], np.float32).T)
    m['g_k'] = f(np.asarray(inputs['mla_k_gain'], np.float32).T)
    m['g_ps'] = f(np.asarray(inputs['pool_scale'], np.float32).reshape(NL, 2, 128).transpose(2, 0, 1).reshape(128, NL * 2))
    m['w_uq'] = f(inputs['mla_w_uq'])
    wukv = np.asarray(inputs['mla_w_ukv'], np.float32).reshape(NL, 256, 8, 128)
    wk = np.zeros((NL, 256, 8, 96), np.float32)
    wk[..., 0:64] = wukv[..., 0:64]
    m['wk_pad'] = f(wk.reshape(NL, 256, 768))
    m['wv'] = f(wukv[..., 64:128].reshape(NL, 256, 512))
    pw = np.asarray(inputs['pool_w'], np.float32)
    pwb = np.zeros((NL, 2, 128, 128), np.float32)
    for ch in range(2):
        pwb[:, ch, 0:64, 0:64] = pw[:, 2 * ch]
        pwb[:, ch, 64:128, 64:128] = pw[:, 2 * ch + 1]
    m['pwb'] = pwb
    return m

def kernel(**inputs):
    if 'nc' not in _CACHE:
        _CACHE['nc'] = build()
        _CACHE['consts'] = host_constants()
    nc = _CACHE['nc']
    in_maps = []
    per_b = [prep_inputs(inputs, b) for b in range(4)]
    for core in range(8):
        mm = dict(per_b[core % 4])
        mm.update(_CACHE['consts'])
        in_maps.append(mm)
    res = run_bass_kernel_spmd(nc, in_maps, core_ids=list(range(8)))
    out = np.stack([np.asarray(res.results[b]['out'], np.float32) for b in range(4)], 0)
    return out
```

```python
import numpy as np
import ml_dtypes
from contextlib import ExitStack
import concourse.bass as bass
import concourse.mybir as mybir
from concourse.bass_utils import run_bass_kernel_spmd
F32 = mybir.dt.float32
BF16 = mybir.dt.bfloat16
AF = mybir.ActivationFunctionType
ALU = mybir.AluOpType
D = 1024
TL = 8192
TC = 256
T = TL + TC
DFF = 2816
NL = 4
EPS = 1e-06
INW = 2208
CH = 32
NCHUNK = T // CH
ENGS = ('pe', 'act', 'dve', 'pool', 'sp')


def _mk(method, *args, **kwargs):
    return lambda e: getattr(e, method)(*args, **kwargs)


class Ev:
    __slots__ = ('sem', 'val')

    def __init__(self, sem, val):
        self.sem = sem
        self.val = val

class Prog:

    def __init__(self, nc, n_dma_sems=12):
        self.nc = nc
        self.ops = {e: [] for e in ENGS}
        self.cnt = {e: 0 for e in ENGS}
        self.known = {e: {} for e in ENGS}
        self.last_w = {}
        self.readers = {}
        self.n_dma_sems = n_dma_sems
        self.dma_uses = {}
        self.dma_rr = {e: 0 for e in ENGS}
        self.pending = {e: [] for e in ENGS}

    def _need(self, eng, ev, waits):
        if ev is None:
            return
        if ev.val is None:
            if ev.sem == ('eng', 'pe') and eng == 'pe':
                return
            raise RuntimeError('wait on unresolved event')
        k = self.known[eng]
        if k.get(ev.sem, 0) >= ev.val:
            return
        if ev.sem == ('eng', 'pe') and eng == 'pe':
            return
        k[ev.sem] = ev.val
        waits[ev.sem] = max(waits.get(ev.sem, 0), ev.val)

    def _deps(self, eng, r, w):
        waits = {}
        for t in r:
            self._need(eng, self.last_w.get(t), waits)
        for t in w:
            self._need(eng, self.last_w.get(t), waits)
            for ev in self.readers.get(t, ()):
                self._need(eng, ev, waits)
        return waits

    def _commit(self, ev, r, w):
        for t in r:
            self.readers.setdefault(t, []).append(ev)
        for t in w:
            self.last_w[t] = ev
            self.readers[t] = []

    def op(self, eng, fn, r=(), w=(), signal=True):
        w = list(w) + ['bank' + t[2] for t in list(r) + list(w) if t.startswith('ps') and t[2:3].isdigit()]
        waits = self._deps(eng, r, w)
        if signal:
            self.cnt[eng] += 1
            ev = Ev(('eng', eng), self.cnt[eng])
            for p in self.pending[eng]:
                p.val = ev.val
            self.pending[eng] = []
        else:
            ev = Ev(('eng', eng), None)
            self.pending[eng].append(ev)
        self.ops[eng].append((fn, waits, ('eng', eng) if signal else None, 1))
        self._commit(ev, r, w)
        return ev

    def mm(self, out, pairs, r=(), w=()):
        n = len(pairs)

        def mk(i, lhsT, rhs):
            return _mk('matmul', out, lhsT, rhs, start=i == 0, stop=i == n - 1)
        ev = None
        for i, (lhsT, rhs) in enumerate(pairs):
            ev = self.op('pe', mk(i, lhsT, rhs), r=r if i == 0 else (), w=w if i == 0 else (), signal=i == n - 1)
        return ev

    def dma(self, eng, out, in_, r=(), w=(), **kw):
        waits = self._deps(eng, r, w)
        idx = self.dma_rr[eng]
        self.dma_rr[eng] = (idx + 1) % self.n_dma_sems
        key = ('dma', eng, idx)
        uses = self.dma_uses.get(key, 0)
        if uses > 0:
            k = self.known[eng]
            if k.get(key, 0) < 16 * uses:
                k[key] = 16 * uses
                waits[key] = max(waits.get(key, 0), 16 * uses)
        self.dma_uses[key] = uses + 1
        ev = Ev(key, 16 * (uses + 1))
        self.ops[eng].append((_mk('dma_start', out=out, in_=in_, **kw), waits, key, 16))
        self._commit(ev, r, w)
        return ev

    def barrier(self, final=False):
        for e in ENGS:
            waits = {}
            k = self.known[e]
            for e2 in ENGS:
                v = self.cnt[e2]
                key = ('eng', e2)
                if v > 0 and k.get(key, 0) < v:
                    k[key] = v
                    waits[key] = v
            for key, uses in self.dma_uses.items():
                v = 16 * uses
                if k.get(key, 0) < v:
                    k[key] = v
                    waits[key] = v
            self.ops[e].append((None, waits, None, 0))
        self.last_w.clear()
        self.readers.clear()

    def emit(self):
        nc = self.nc
        handles = {'pe': 'tensor', 'act': 'scalar', 'dve': 'vector', 'pool': 'gpsimd', 'sp': 'sync'}
        with ExitStack() as st:
            sems = {}
            for e in ENGS:
                sems['eng', e] = st.enter_context(nc.semaphore('s_' + e))
            for key in self.dma_uses:
                sems[key] = st.enter_context(nc.semaphore('d_%s_%d' % (key[1], key[2])))
            block = st.enter_context(nc.Block())

            def run(e):

                def body(h):
                    for fn, waits, sig, inc in self.ops[e]:
                        for s, v in waits.items():
                            h.wait_ge(sems[s], v)
                        if fn is not None:
                            ins = fn(h)
                            if sig is not None:
                                ins.then_inc(sems[sig], inc)
                return body
            for e in ENGS:
                getattr(block, handles[e])(run(e))

class Arena:

    def __init__(self, t, n):
        self.t = t
        self.n = n
        self.off = 0

    def reset(self):
        self.off = 0

    def take(self, size, pat=None, parts=128, **kw):
        assert self.off + size <= self.n, (self.off, size, self.n)
        v = self.t[0:parts, self.off:self.off + size]
        self.off += size
        if pat:
            v = v.rearrange(pat, **kw)
        return v
TILES = [(i * 512, 512, 0) for i in range(TL // 512)] + [(TL, TC, 1)]

import os
STOP = int(os.environ.get('KSTOP', '99'))
STOP2 = int(os.environ.get('KSTOP2', '0'))


class _Stop(Exception):
    pass


def build(n_layers=NL, debug=(), upto=None):
    nc = bass.Bass('TRN2', target_bir_lowering=False)

    def din(name, shape, dt=F32):
        return nc.dram_tensor(name, list(shape), dt, kind='ExternalInput').ap()

    def dscr(name, shape, dt=F32):
        if name in debug:
            return nc.dram_tensor(name, list(shape), dt, kind='ExternalOutput').ap()
        return nc.dram_tensor(name, list(shape), dt).ap()
    I = {}
    for name, shape in [('x_in', (TL, D)), ('ctx_in', (TC, D)), ('cvT', (128, 16)), ('w_mod', (NL, D, 9 * D)), ('b_mod', (NL, 9 * D)), ('f1i', (NL, D, 2 * DFF)), ('f1o', (NL, DFF, D)), ('f2i', (NL, D, 2 * DFF)), ('f2o', (NL, DFF, D)), ('w_in', (NL, D, INW)), ('w_out', (NL, D, D)), ('lblB', (128, NL * 512)), ('lblF', (128, 16)), ('g_hg', (128, NL)), ('g_qa', (128, NL * 3)), ('g_kva', (128, NL * 2)), ('g_q', (96, NL)), ('g_k', (96, NL)), ('g_ps', (128, NL * 2)), ('w_uq', (NL, 384, 768)), ('wk_pad', (NL, 256, 768)), ('wv', (NL, 256, 512)), ('pwb', (NL, 2, 128, 128)), ('ident', (128, 128)), ('d1f', (128, 128)), ('d1b', (128, 128)), ('trf', (128, 128)), ('trb', (128, 128)), ('maskf', (32, 32)), ('maskb', (32, 32)), ('bd64', (128, 128)), ('rot', (96, 96)), ('shift', (32, 96)), ('cosT', (96, TL)), ('sinT', (96, TL)), ('corr', (128, 32))]:
        I[name] = din(name, shape)
    out = nc.dram_tensor('out', [TL, D], F32, kind='ExternalOutput').ap()
    xT = dscr('xT', (D, T))
    hT = dscr('hT', (D, T), BF16)
    mixT = dscr('mixT', (D, T), BF16)
    QT = dscr('QT', (8, 96, T), BF16)
    KT = dscr('KT', (8, 96, T), BF16)
    VV = dscr('VV', (T, 8, 65), BF16)
    uT = dscr('uT', (256, T))
    KH = dscr('KH', (T, 512), BF16)
    VH = dscr('VH', (T, 256), BF16)
    HQ = dscr('HQ', (2, 256, T), BF16)
    HK = dscr('HK', (2, 256, T), BF16)
    EBE = dscr('EBE', (2, 2, 128, NCHUNK))
    GT = dscr('GT', (256, T), BF16)
    OFB = dscr('OFB', (2, 256, T))
    MODD = dscr('MODD', (128, NL * 144))
    with ExitStack() as st:

        def sb(n, s, d=F32):
            return st.enter_context(nc.sbuf_tensor(n, list(s), d))
        NBW, NBA, NFA = (36000, 20480, 13312)
        BWt = sb('BW', (128, NBW), BF16)
        BAt = sb('BA', (128, NBA), BF16)
        FAt = sb('FA', (128, NFA), F32)
        BW, BA, FA = (Arena(BWt, NBW), Arena(BAt, NBA), Arena(FAt, NFA))
        ps = [st.enter_context(nc.psum_tensor('ps%d' % i, [128, 512], F32)) for i in range(8)]
        MOD = sb('MOD', (128, NL * 144))
        SC1 = sb('SC1', (128, NL * 48))
        GHT = sb('GHT', (128, NL * 48))
        cact = sb('cact', (128, 16))
        identF = sb('identF', (128, 128))
        d1f = sb('d1fS', (128, 128))
        d1b = sb('d1bS', (128, 128))
        trf = sb('trfS', (128, 128))
        trb = sb('trbS', (128, 128))
        maskf = sb('maskfS', (32, 32))
        maskb = sb('maskbS', (32, 32))
        bdF = sb('bdF', (128, 128))
        bd16 = sb('bd16', (128, 128), BF16)
        ones16 = sb('ones16', (128, 128), BF16)
        onesF = sb('onesF', (128, 128))
        rotF = sb('rotF', (96, 96))
        shF = sb('shF', (32, 96))
        sh16 = sb('sh16', (32, 96), BF16)
        corr = sb('corrS', (128, 32))
        epsT = sb('epsT', (128, 1))
        gains = sb('gains', (128, 64))
        lbF = sb('lbF', (128, 16))
        omlF = sb('omlF', (128, 16))
        lbtmp = sb('lbtmp', (128, 16))
        S32 = sb('S32', (128, 4 * 64))
        S16 = sb('S16', (128, 4 * 64), BF16)
        P = Prog(nc)
        ckc = [0]

        def ck():
            ckc[0] += 1
            if ckc[0] == STOP2:
                P.barrier()
                raise _Stop()

        def mod_ap(tile, l, s, c, m, n=48):
            i = ((l * (n // 16) + s) * 8 + c) * 2 + m
            return tile[:, i:i + 1]

        def load_consts():
            for dst, name in [(identF, 'ident'), (d1f, 'd1f'), (d1b, 'd1b'), (trf, 'trf'), (trb, 'trb'), (maskf, 'maskf'), (maskb, 'maskb'), (bdF, 'bd64'), (rotF, 'rot'), (shF, 'shift'), (corr, 'corr'), (cact, 'cvT'), (lbF, 'lblF')]:
                P.dma('sp', dst[:], I[name][:, :], w=[name])
            P.dma('sp', gains[:, 0:4], I['g_hg'][:, :], w=['gains'])
            P.dma('sp', gains[:, 4:16], I['g_qa'][:, :], w=['gains'])
            P.dma('sp', gains[:, 16:24], I['g_kva'][:, :], w=['gains'])
            P.dma('sp', gains[:, 24:32], I['g_ps'][:, :], w=['gains'])
            P.dma('sp', gains[0:96, 32:36], I['g_q'][:, :], w=['gains'])
            P.dma('sp', gains[0:96, 36:40], I['g_k'][:, :], w=['gains'])
            P.op('dve', _mk('memset', onesF[:], 1.0), w=['onesF'])
            P.op('dve', _mk('memset', epsT[:], EPS), w=['epsT'])
            P.op('pool', _mk('memset', ones16[:], 1.0), w=['ones16'])
            P.op('pool', _mk('tensor_copy', bd16[:], bdF[:]), r=['bd64'], w=['bd16'])
            P.op('pool', _mk('tensor_copy', sh16[:], shF[:]), r=['shift'], w=['sh16'])
            P.op('act', _mk('activation', out=cact[:], in_=cact[:], func=AF.Silu), r=['cvT'], w=['cvT'])
            L = lambda l: lbF[:, 4 * l:4 * l + 4]
            mx = lbtmp[:, 0:4]
            sm = lbtmp[:, 4:8]
            rc = lbtmp[:, 8:12]
            P.op('dve', _mk('tensor_tensor', out=mx, in0=L(0), in1=L(1), op=ALU.max), r=['lblF'], w=['lbt'])
            P.op('dve', _mk('tensor_tensor', out=mx, in0=mx, in1=L(2), op=ALU.max), r=['lbt'], w=['lbt'])
            P.op('dve', _mk('tensor_tensor', out=mx, in0=mx, in1=L(3), op=ALU.max), r=['lbt'], w=['lbt'])
            for l in range(4):
                P.op('dve', _mk('tensor_tensor', out=L(l), in0=L(l), in1=mx, op=ALU.subtract), r=['lbt', 'lblF'], w=['lblF'])
            P.op('act', _mk('activation', out=lbF[:], in_=lbF[:], func=AF.Exp), r=['lblF'], w=['lblF'])
            P.op('dve', _mk('tensor_tensor', out=sm, in0=L(0), in1=L(1), op=ALU.add), r=['lblF'], w=['lbt'])
            P.op('dve', _mk('tensor_tensor', out=sm, in0=sm, in1=L(2), op=ALU.add), r=['lbt'], w=['lbt'])
            P.op('dve', _mk('tensor_tensor', out=sm, in0=sm, in1=L(3), op=ALU.add), r=['lbt'], w=['lbt'])
            P.op('dve', _mk('reciprocal', out=rc, in_=sm), r=['lbt'], w=['lbt'])
            for l in range(4):
                P.op('dve', _mk('tensor_tensor', out=L(l), in0=L(l), in1=rc, op=ALU.mult), r=['lbt', 'lblF'], w=['lblF'])
            P.op('dve', _mk('memset', L(0), 0.0), r=['lblF'], w=['lblF'])
            P.op('dve', _mk('tensor_tensor', out=L(2), in0=L(2), in1=L(1), op=ALU.add), r=['lblF'], w=['lblF'])
            P.op('dve', _mk('tensor_tensor', out=L(3), in0=L(3), in1=L(2), op=ALU.add), r=['lblF'], w=['lblF'])
            P.op('dve', _mk('tensor_scalar', out=omlF[:], in0=lbF[:], scalar1=-1.0, scalar2=1.0, op0=ALU.mult, op1=ALU.add), r=['lblF'], w=['omlF'])
            P.barrier()

        def phase_mod():
            FA.reset()
            stg = [FA.take(4096, 'p (c n) -> p c n', c=8) for _ in range(2)]
            brow = [FA.take(512, parts=1) for _ in range(2)]
            k = 0
            for l in range(n_layers):
                for nb in range(18):
                    b = k % 2
                    P.dma('sp', stg[b], I['w_mod'][l, :, nb * 512:(nb + 1) * 512].rearrange('(c p) n -> p c n', p=128), w=['stg%d' % b])
                    P.dma('pool', brow[b], I['b_mod'][l:l + 1, nb * 512:(nb + 1) * 512], w=['brow%d' % b])
                    pt = ps[k % 4]
                    for fc in range(4):
                        pairs = [(stg[b][:, c, 128 * fc:128 * fc + 128], cact[:, 2 * c:2 * c + 2]) for c in range(8)]
                        pairs.append((brow[b][0:1, 128 * fc:128 * fc + 128], onesF[0:1, 0:2]))
                        P.mm(pt[:, 2 * fc:2 * fc + 2], pairs, r=['stg%d' % b, 'brow%d' % b, 'cvT', 'onesF'], w=['ps%d' % (k % 4)])
                    g0 = l * 144 + nb * 8
                    P.op('dve', _mk('tensor_copy', MOD[:, g0:g0 + 8], pt[:, 0:8]), r=['ps%d' % (k % 4)], w=['MOD'])
                    k += 1
            for l in range(n_layers):
                for s in range(3):
                    src = MOD[:, l * 144 + (3 * s + 1) * 16:l * 144 + (3 * s + 2) * 16]
                    dst = SC1[:, (l * 3 + s) * 16:(l * 3 + s + 1) * 16]
                    P.op('dve', _mk('tensor_scalar', out=dst, in0=src, scalar1=1.0, scalar2=None, op0=ALU.add), r=['MOD'], w=['SC1'])
                    srcg = MOD[:, l * 144 + (3 * s + 2) * 16:l * 144 + (3 * s + 3) * 16]
                    dstg = GHT[:, (l * 3 + s) * 16:(l * 3 + s + 1) * 16]
                    fac = 1.0 if s == 1 else 0.5
                    P.op('dve', _mk('tensor_scalar', out=dstg, in0=srcg, scalar1=fac, scalar2=None, op0=ALU.mult), r=['MOD'], w=['GHT'])
            if 'MODD' in debug:
                P.dma('sp', MODD[:, :], MOD[:], r=['MOD'], w=['MODD'])
            P.barrier()

        def shift_ap(l, s, c, m):
            i = l * 144 + 3 * s * 16 + c * 2 + m
            return MOD[:, i:i + 1]

        def phase_in_transpose():
            FA.reset()
            xb = [FA.take(1024) for _ in range(4)]
            xt = FA.take(4096, 'p (c t) -> p c t', c=8)
            for t0, tw, m in TILES:
                nb = tw // 128
                for i in range(nb):
                    src = I['x_in'][t0 + 128 * i:t0 + 128 * i + 128, :] if m == 0 else I['ctx_in'][128 * i:128 * i + 128, :]
                    P.dma('sp', xb[i], src, w=['xb%d' % i])
                for c in range(8):
                    for i in range(nb):
                        P.op('pe', _mk('transpose', ps[c][:, 128 * i:128 * i + 128], xb[i][:, 128 * c:128 * c + 128], identF[:]), r=['xb%d' % i, 'ident'], w=['ps%d' % c])
                    if c % 2 == 0:
                        P.op('act', _mk('activation', out=xt[:, c, 0:tw], in_=ps[c][:, 0:tw], func=AF.Copy), r=['ps%d' % c], w=['xt%d' % c])
                    else:
                        P.op('dve', _mk('tensor_copy', xt[:, c, 0:tw], ps[c][:, 0:tw]), r=['ps%d' % c], w=['xt%d' % c])
                P.dma('pool', xT[:, t0:t0 + tw].rearrange('(c p) t -> p c t', p=128), xt[:, :, 0:tw], r=['xt%d' % c for c in range(8)], w=['xT%d' % t0])
            P.barrier()

        def phase_out_transpose():
            FA.reset()
            xt = [FA.take(4096, 'p (c t) -> p c t', c=8) for _ in range(2)]
            ob = [FA.take(1024) for _ in range(4)]
            for ti, (t0, tw, m) in enumerate(TILES):
                if m == 1:
                    continue
                b = ti % 2
                P.dma('sp', xt[b][:, :, 0:tw], xT[:, t0:t0 + tw].rearrange('(c p) t -> p c t', p=128), w=['xt%d' % b])
                for i in range(tw // 128):
                    for c in range(8):
                        pt = ps[(i * 8 + c) % 8]
                        P.op('pe', _mk('transpose', pt[:, 0:128], xt[b][:, c, 128 * i:128 * i + 128], identF[:]), r=['xt%d' % b, 'ident'], w=['ps%d' % ((i * 8 + c) % 8)])
                        if c % 2 == 0:
                            P.op('act', _mk('activation', out=ob[i][:, 128 * c:128 * c + 128], in_=pt[:, 0:128], func=AF.Copy), r=['ps%d' % ((i * 8 + c) % 8)], w=['ob%d_%d' % (i, c)])
                        else:
                            P.op('dve', _mk('tensor_copy', ob[i][:, 128 * c:128 * c + 128], pt[:, 0:128]), r=['ps%d' % ((i * 8 + c) % 8)], w=['ob%d_%d' % (i, c)])
                    P.dma('pool', out[t0 + 128 * i:t0 + 128 * i + 128, :], ob[i], r=['ob%d_%d' % (i, c) for c in range(8)], w=['out%d_%d' % (t0, i)])
            P.barrier()

        def phase_norm(l, s):
            FA.reset()
            BA.reset()
            xt = [FA.take(4096, 'p (c t) -> p c t', c=8) for _ in range(2)]
            tmp = [FA.take(512) for _ in range(2)]
            rstd = FA.take(512)
            sq = BA.take(4096, 'p (c t) -> p c t', c=8)
            h = [BA.take(4096, 'p (c t) -> p c t', c=8) for _ in range(2)]
            for ti, (t0, tw, m) in enumerate(TILES):
                b = ti % 2
                X = xt[b]
                P.dma('sp', X[:, :, 0:tw], xT[:, t0:t0 + tw].rearrange('(c p) t -> p c t', p=128), w=['xt%d' % b])
                P.op('act', _mk('activation', out=sq[:, :, 0:tw], in_=X[:, :, 0:tw], func=AF.Square), r=['xt%d' % b], w=['sq'])
                P.mm(ps[0][:, 0:tw], [(ones16[:], sq[:, c, 0:tw]) for c in range(8)], r=['sq', 'ones16'], w=['ps0'])
                P.op('act', _mk('activation', out=rstd[:, 0:tw], in_=ps[0][:, 0:tw], func=AF.Sqrt, bias=epsT[:, 0:1], scale=1.0 / D), r=['ps0', 'epsT'], w=['rstd'])
                P.op('dve', _mk('reciprocal', out=rstd[:, 0:tw], in_=rstd[:, 0:tw]), r=['rstd'], w=['rstd'])
                for c in range(8):
                    tb = tmp[c % 2]
                    P.op('dve', _mk('scalar_tensor_tensor', out=tb[:, 0:tw], in0=X[:, c, 0:tw], scalar=mod_ap(SC1, l, s, c, m), in1=rstd[:, 0:tw], op0=ALU.mult, op1=ALU.mult), r=['xt%d' % b, 'rstd', 'SC1'], w=['tmp%d' % (c % 2)])
                    P.op('act', _mk('activation', out=h[b][:, c, 0:tw], in_=tb[:, 0:tw], func=AF.Identity, bias=shift_ap(l, s, c, m), scale=1.0), r=['tmp%d' % (c % 2), 'MOD'], w=['h%d_%d' % (b, c)])
                P.dma('pool', hT[:, t0:t0 + tw].rearrange('(c p) t -> p c t', p=128), h[b][:, :, 0:tw], r=['h%d_%d' % (b, c) for c in range(8)], w=['hT%d' % t0])
            P.barrier()

        def residual_update(pt, ptok, X, xtok, j, tw, l, s, m):
            P.op('dve', _mk('scalar_tensor_tensor', out=X[:, j, 0:tw], in0=pt[:, 0:tw], scalar=mod_ap(GHT, l, s, j, m), in1=X[:, j, 0:tw], op0=ALU.mult, op1=ALU.add), r=[ptok, 'GHT'], w=[xtok + '_%d' % j])

        def phase_ffn(l, s, wi, wo):
            NH = 11
            for half in range(2):
                FA.reset()
                BA.reset()
                BW.reset()
                wg = BW.take(8 * 1408, 'p (c n) -> p c n', c=8)
                wu = BW.take(8 * 1408, 'p (c n) -> p c n', c=8)
                wob = BW.take(NH * 1024, 'p (c n) -> p c n', c=NH)
                stg = [FA.take(1408) for _ in range(2)]
                k = 0
                f0 = half * 1408
                for c in range(8):
                    for dst, col0 in ((wg, f0), (wu, DFF + f0)):
                        bb = k % 2
                        P.dma('sp', stg[bb], wi[l, 128 * c:128 * c + 128, col0:col0 + 1408], w=['stg%d' % bb])
                        P.op('pool', _mk('tensor_copy', dst[:, c, :], stg[bb][:, 0:1408]), r=['stg%d' % bb], w=['wres'])
                        k += 1
                for i in range(NH):
                    bb = k % 2
                    P.dma('sp', stg[bb][:, 0:1024], wo[l, f0 + 128 * i:f0 + 128 * i + 128, :], w=['stg%d' % bb])
                    P.op('pool', _mk('tensor_copy', wob[:, i, :], stg[bb][:, 0:1024]), r=['stg%d' % bb], w=['wres'])
                    k += 1
                xt = [FA.take(4096, 'p (c t) -> p c t', c=8) for _ in range(2)]
                sg = [FA.take(512) for _ in range(2)]
                ht = [BA.take(4096, 'p (c t) -> p c t', c=8) for _ in range(2)]
                act = BA.take(NH * 512, 'p (c t) -> p c t', c=NH)
                for ti, (t0, tw, m) in enumerate(TILES):
                    b = ti % 2
                    H = ht[b]
                    X = xt[b]
                    P.dma('sp', H[:, :, 0:tw], hT[:, t0:t0 + tw].rearrange('(c p) t -> p c t', p=128), w=['ht%d' % b])
                    P.dma('sp', X[:, :, 0:tw], xT[:, t0:t0 + tw].rearrange('(c p) t -> p c t', p=128), r=['xT%d' % t0], w=['xt%d_%d' % (b, j) for j in range(8)])
                    for i in range(NH):
                        pg = ps[2 * i % 4]
                        pu = ps[(2 * i + 1) % 4]
                        tg = 'ps%d' % (2 * i % 4)
                        tu = 'ps%d' % ((2 * i + 1) % 4)
                        P.mm(pg[:, 0:tw], [(wg[:, c, 128 * i:128 * i + 128], H[:, c, 0:tw]) for c in range(8)], r=['ht%d' % b, 'wres'], w=[tg])
                        P.mm(pu[:, 0:tw], [(wu[:, c, 128 * i:128 * i + 128], H[:, c, 0:tw]) for c in range(8)], r=['ht%d' % b, 'wres'], w=[tu])
                        sgb = sg[i % 2]
                        P.op('act', _mk('activation', out=sgb[:, 0:tw], in_=pg[:, 0:tw], func=AF.Silu), r=[tg], w=['sg%d' % (i % 2)])
                        P.op('dve', _mk('tensor_tensor', out=act[:, i, 0:tw], in0=sgb[:, 0:tw], in1=pu[:, 0:tw], op=ALU.mult), r=['sg%d' % (i % 2), tu], w=['act%d' % i])
                    for j in range(8):
                        py = ps[4 + j % 2]
                        ty = 'ps%d' % (4 + j % 2)
                        P.mm(py[:, 0:tw], [(wob[:, i, 128 * j:128 * j + 128], act[:, i, 0:tw]) for i in range(NH)], r=['act%d' % i for i in range(NH)] + ['wres'], w=[ty])
                        residual_update(py, ty, X, 'xt%d' % b, j, tw, l, s, m)
                    P.dma('pool', xT[:, t0:t0 + tw].rearrange('(c p) t -> p c t', p=128), X[:, :, 0:tw], r=['xt%d_%d' % (b, j) for j in range(8)], w=['xT%d' % t0])
                P.barrier()

        def phase_outproj(l):
            FA.reset()
            BA.reset()
            BW.reset()
            wo = BW.take(8 * 1024, 'p (c n) -> p c n', c=8)
            stg = [FA.take(1024) for _ in range(2)]
            for c in range(8):
                bb = c % 2
                P.dma('sp', stg[bb], I['w_out'][l, 128 * c:128 * c + 128, :], w=['stg%d' % bb])
                P.op('pool', _mk('tensor_copy', wo[:, c, :], stg[bb][:]), r=['stg%d' % bb], w=['wres'])
            xt = [FA.take(4096, 'p (c t) -> p c t', c=8) for _ in range(2)]
            mt = [BA.take(4096, 'p (c t) -> p c t', c=8) for _ in range(2)]
            for ti, (t0, tw, m) in enumerate(TILES):
                b = ti % 2
                M = mt[b]
                X = xt[b]
                P.dma('sp', M[:, :, 0:tw], mixT[:, t0:t0 + tw].rearrange('(c p) t -> p c t', p=128), w=['mt%d' % b])
                P.dma('sp', X[:, :, 0:tw], xT[:, t0:t0 + tw].rearrange('(c p) t -> p c t', p=128), w=['xt%d_%d' % (b, j) for j in range(8)])
                for j in range(8):
                    py = ps[j % 4]
                    ty = 'ps%d' % (j % 4)
                    P.mm(py[:, 0:tw], [(wo[:, c, 128 * j:128 * j + 128], M[:, c, 0:tw]) for c in range(8)], r=['mt%d' % b, 'wres'], w=[ty])
                    residual_update(py, ty, X, 'xt%d' % b, j, tw, l, 1, m)
                P.dma('pool', xT[:, t0:t0 + tw].rearrange('(c p) t -> p c t', p=128), X[:, :, 0:tw], r=['xt%d_%d' % (b, j) for j in range(8)], w=['xT%d' % t0])
            P.barrier()

        def rstd_from(pt, ptok, dst, dtok, tw, n, parts=128):
            P.op('act', _mk('activation', out=dst[0:parts, 0:tw], in_=pt[0:parts, 0:tw], func=AF.Sqrt, bias=epsT[0:parts, 0:1], scale=1.0 / n), r=[ptok, 'epsT'], w=[dtok])
            P.op('dve', _mk('reciprocal', out=dst[0:parts, 0:tw], in_=dst[0:parts, 0:tw]), r=[dtok], w=[dtok])

        def phase_inproj(l):
            FA.reset()
            BA.reset()
            BW.reset()
            win = BW.take(8 * INW, 'p (c n) -> p c n', c=8)
            wuq = BW.take(3 * 768, 'p (c n) -> p c n', c=3)
            wkp = BW.take(2 * 768, 'p (c n) -> p c n', c=2)
            wvv = BW.take(2 * 512, 'p (c n) -> p c n', c=2)
            stg = [FA.take(INW) for _ in range(2)]
            k = 0
            for c in range(8):
                bb = k % 2
                P.dma('sp', stg[bb], I['w_in'][l, 128 * c:128 * c + 128, :], w=['stg%d' % bb])
                P.op('pool', _mk('tensor_copy', win[:, c, :], stg[bb][:]), r=['stg%d' % bb], w=['wres'])
                k += 1
            for dst, name, ncc, n in ((wuq, 'w_uq', 3, 768), (wkp, 'wk_pad', 2, 768), (wvv, 'wv', 2, 512)):
                for c in range(ncc):
                    bb = k % 2
                    P.dma('sp', stg[bb][:, 0:n], I[name][l, 128 * c:128 * c + 128, :], w=['stg%d' % bb])
                    P.op('pool', _mk('tensor_copy', dst[:, c, :], stg[bb][:, 0:n]), r=['stg%d' % bb], w=['wres'])
                    k += 1
            P.barrier()
            FA.reset()
            LBB = FA.take(512)
            OMLB = FA.take(512)
            lbl = FA.take(2048)
            tmpA = FA.take(512)
            tmpB = FA.take(512)
            P.dma('sp', lbl, I['lblB'][:, :], w=['lbl'])
            Lr = lambda i: lbl[:, 512 * i:512 * i + 512]
            P.op('dve', _mk('tensor_tensor', out=tmpA, in0=Lr(0), in1=Lr(1), op=ALU.max), r=['lbl'], w=['tA'])
            P.op('dve', _mk('tensor_tensor', out=tmpA, in0=tmpA, in1=Lr(2), op=ALU.max), r=['tA'], w=['tA'])
            P.op('dve', _mk('tensor_tensor', out=tmpA, in0=tmpA, in1=Lr(3), op=ALU.max), r=['tA'], w=['tA'])
            for i in range(4):
                P.op('dve', _mk('tensor_tensor', out=Lr(i), in0=Lr(i), in1=tmpA, op=ALU.subtract), r=['tA', 'lbl'], w=['lbl'])
            P.op('act', _mk('activation', out=lbl, in_=lbl, func=AF.Exp), r=['lbl'], w=['lbl'])
            P.op('dve', _mk('tensor_tensor', out=tmpB, in0=Lr(0), in1=Lr(1), op=ALU.add), r=['lbl'], w=['tB'])
            P.op('dve', _mk('tensor_tensor', out=tmpB, in0=tmpB, in1=Lr(2), op=ALU.add), r=['tB'], w=['tB'])
            P.op('dve', _mk('tensor_tensor', out=tmpB, in0=tmpB, in1=Lr(3), op=ALU.add), r=['tB'], w=['tB'])
            P.op('dve', _mk('reciprocal', out=tmpB, in_=tmpB), r=['tB'], w=['tB'])
            if l == 0:
                P.op('dve', _mk('memset', LBB, 0.0), w=['LBB'])
            else:
                P.op('dve', _mk('tensor_copy', LBB, Lr(1)), r=['lbl'], w=['LBB'])
                for i in range(2, l + 1):
                    P.op('dve', _mk('tensor_tensor', out=LBB, in0=LBB, in1=Lr(i), op=ALU.add), r=['lbl', 'LBB'], w=['LBB'])
                P.op('dve', _mk('tensor_tensor', out=LBB, in0=LBB, in1=tmpB, op=ALU.mult), r=['tB', 'LBB'], w=['LBB'])
            P.op('dve', _mk('tensor_scalar', out=OMLB, in0=LBB, scalar1=-1.0, scalar2=1.0, op0=ALU.mult, op1=ALU.add), r=['LBB'], w=['OMLB'])
            P.barrier()
            FA.off = 1024
            if STOP == 0:
                P.barrier()
                return
            SL = [FA.take(512) for _ in range(12)]
            ST = ['S%d' % i for i in range(12)]
            cqF = FA.take(1536, 'p (c t) -> p c t', c=3)
            ckvF = FA.take(1024, 'p (c t) -> p c t', c=2)
            rs = FA.take(512)
            csT = FA.take(512, parts=96)
            snT = FA.take(512, parts=96)
            uF = FA.take(1024, 'p (c t) -> p c t', c=2)
            ebT = FA.take(64)
            ht = [BW.take(4096, 'p (c t) -> p c t', c=8) for _ in range(2)]
            kh16 = [BA.take(512) for _ in range(4)]
            vh16 = [BA.take(256) for _ in range(4)]
            qk16 = [BA.take(512) for _ in range(4)]
            g16 = [BA.take(512) for _ in range(2)]
            sq3 = BA.take(1536, 'p (c t) -> p c t', c=3)
            cqn = BA.take(1536, 'p (c t) -> p c t', c=3)
            ckvn = BA.take(1024, 'p (c t) -> p c t', c=2)
            kpe16 = BA.take(512, parts=32)
            sqh = [BA.take(512) for _ in range(4)]
            o16 = [BA.take(512) for _ in range(4)]
            v16 = [BA.take(520, 'p (h e) -> p h e', h=8) for _ in range(2)]
            for b in range(2):
                P.op('pool', _mk('memset', v16[b][:, :, 64:65], 1.0), w=['v16_%d' % b])
            gq = lambda c: gains[:, 4 + l * 3 + c:5 + l * 3 + c]
            gkv = lambda c: gains[:, 16 + l * 2 + c:17 + l * 2 + c]
            cnt = {'ps': 0}

            def nps():
                i = cnt['ps'] % 8
                cnt['ps'] += 1
                return (ps[i], 'ps%d' % i)

            def nps4x():
                i = cnt['ps'] % 4
                cnt['ps'] += 1
                return (ps[i], 'ps%d' % i)

            def featmm(col0, ncols, H, b, tw):
                pt, tok = nps()
                P.mm(pt[0:ncols, 0:tw], [(win[:, c, col0:col0 + ncols], H[:, c, 0:tw]) for c in range(8)], r=['ht%d' % b, 'wres'], w=[tok])
                return (pt, tok)

            def heads_block(mmfn, gcol, rope, dst_dram, t0, tw):
                gsc = gains[0:96, gcol:gcol + 1]
                for h0 in (0, 4):
                    J = range(4)
                    for j in J:
                        mmfn(h0 + j, ps[j], 'ps%d' % j)
                    for j in J:
                        P.op('act', _mk('activation', out=sqh[j][0:96, 0:tw], in_=ps[j][0:96, 0:tw], func=AF.Square), r=['ps%d' % j], w=['sqh%d' % j])
                    for j in J:
                        P.mm(ps[4 + j][0:96, 0:tw], [(ones16[0:96, 0:96], sqh[j][0:96, 0:tw])], r=['sqh%d' % j, 'ones16'], w=['ps%d' % (4 + j)])
                    for j in J:
                        P.op('act', _mk('activation', out=SL[3 * j][0:96, 0:tw], in_=ps[4 + j][0:96, 0:tw], func=AF.Sqrt, bias=epsT[0:96, 0:1], scale=1.0 / 96.0), r=['ps%d' % (4 + j), 'epsT'], w=[ST[3 * j]])
                    for j in J:
                        P.op('dve', _mk('reciprocal', out=SL[3 * j][0:96, 0:tw], in_=SL[3 * j][0:96, 0:tw]), r=[ST[3 * j]], w=[ST[3 * j]])
                    for j in J:
                        P.op('dve', _mk('scalar_tensor_tensor', out=SL[3 * j + 1][0:96, 0:tw], in0=ps[j][0:96, 0:tw], scalar=gsc, in1=SL[3 * j][0:96, 0:tw], op0=ALU.mult, op1=ALU.mult), r=['ps%d' % j, ST[3 * j], 'gains'], w=[ST[3 * j + 1]])
                    if rope:
                        for j in J:
                            P.mm(ps[4 + j][0:96, 0:tw], [(rotF[:, :], SL[3 * j + 1][0:96, 0:tw])], r=[ST[3 * j + 1], 'rot'], w=['ps%d' % (4 + j)])
                        for j in J:
                            P.op('dve', _mk('tensor_tensor', out=SL[3 * j + 2][0:96, 0:tw], in0=SL[3 * j + 1][0:96, 0:tw], in1=csT[0:96, 0:tw], op=ALU.mult), r=[ST[3 * j + 1], 'cs'], w=[ST[3 * j + 2]])
                        for j in J:
                            P.op('dve', _mk('tensor_tensor', out=SL[3 * j + 1][0:96, 0:tw], in0=ps[4 + j][0:96, 0:tw], in1=snT[0:96, 0:tw], op=ALU.mult), r=['ps%d' % (4 + j), 'sn'], w=[ST[3 * j + 1]])
                        for j in J:
                            P.op('dve', _mk('tensor_tensor', out=o16[j][0:96, 0:tw], in0=SL[3 * j + 2][0:96, 0:tw], in1=SL[3 * j + 1][0:96, 0:tw], op=ALU.add), r=[ST[3 * j + 2], ST[3 * j + 1]], w=['o16_%d' % j])
                    else:
                        for j in J:
                            P.op('act', _mk('activation', out=o16[j][0:96, 0:tw], in_=SL[3 * j + 1][0:96, 0:tw], func=AF.Copy), r=[ST[3 * j + 1]], w=['o16_%d' % j])
                    for j in J:
                        P.dma('sp', dst_dram[h0 + j, :, t0:t0 + tw], o16[j][0:96, 0:tw], r=['o16_%d' % j], w=['hd'])

            for ti, (t0, tw, m) in enumerate(TILES):
                b = ti % 2
                H = ht[b]
                if ti == 0:
                    P.dma('sp', H[:, :, 0:tw], hT[:, t0:t0 + tw].rearrange('(c p) t -> p c t', p=128), w=['ht%d' % b])
                if ti + 1 < len(TILES):
                    nt0, ntw, _nm = TILES[ti + 1]
                    P.dma('sp', ht[1 - b][:, :, 0:ntw], hT[:, nt0:nt0 + ntw].rearrange('(c p) t -> p c t', p=128), w=['ht%d' % (1 - b)])
                rope = m == 0
                if rope:
                    P.dma('sp', csT[:, 0:tw], I['cosT'][:, t0:t0 + tw], w=['cs'])
                    P.dma('sp', snT[:, 0:tw], I['sinT'][:, t0:t0 + tw], w=['sn'])
                NB = range(tw // 128)
                Fs = lambda i: SL[3 * i]
                Ks = lambda i: SL[3 * i + 1]
                Es = lambda i: SL[3 * i + 2]
                tF = lambda i: ST[3 * i]
                tK = lambda i: ST[3 * i + 1]
                tE = lambda i: ST[3 * i + 2]
                for i in NB:
                    P.mm(ps[i][:, 0:512], [(H[:, c, 128 * i:128 * i + 128], win[:, c, 256:768]) for c in range(8)], r=['ht%d' % b, 'wres'], w=['ps%d' % i])
                for i in NB:
                    P.op('act', _mk('activation', out=Fs(i), in_=ps[i][:, 0:512], func=AF.Sigmoid), r=['ps%d' % i], w=[tF(i)])
                for i in NB:
                    P.op('dve', _mk('tensor_tensor', out=Fs(i), in0=Fs(i), in1=OMLB, op=ALU.mult), r=[tF(i), 'OMLB'], w=[tF(i)])
                    P.op('dve', _mk('tensor_tensor', out=Fs(i), in0=Fs(i), in1=LBB, op=ALU.add), r=[tF(i), 'LBB'], w=[tF(i)])
                    P.op('dve', _mk('tensor_scalar', out=Ks(i), in0=Fs(i), scalar1=-1.0, scalar2=1.0, op0=ALU.mult, op1=ALU.add), r=[tF(i)], w=[tK(i)])
                    P.op('dve', _mk('tensor_scalar', out=Fs(i), in0=Fs(i), scalar1=1e-06, scalar2=1.0, op0=ALU.max, op1=ALU.min), r=[tF(i)], w=[tF(i)])
                for i in NB:
                    P.op('act', _mk('activation', out=Fs(i), in_=Fs(i), func=AF.Ln), r=[tF(i)], w=[tF(i)])
                for i in NB:
                    lf = Fs(i)
                    P.mm(ps[i][:, 0:256], [(d1f[:], lf[:, 0:256])], r=[tF(i), 'd1f'], w=['ps%d' % i])
                    P.mm(ps[i][:, 256:512], [(d1b[:], lf[:, 256:512])], r=[tF(i), 'd1b'], w=['ps%d' % i])
                    for g in range(4):
                        dd, pr = (g // 2, g % 2)
                        P.mm(ps[4 + g][:, 128 * i:128 * i + 128], [(lf[:, dd * 256 + pr * 128:dd * 256 + pr * 128 + 128], (trf if dd == 0 else trb)[:])], r=[tF(i), 'trf', 'trb'], w=['ps%d' % (4 + g)])
                for i in NB:
                    P.op('act', _mk('activation', out=Es(i), in_=ps[i][:, 0:512], func=AF.Exp), r=['ps%d' % i], w=[tE(i)])
                for i in NB:
                    P.op('dve', _mk('tensor_tensor', out=kh16[i], in0=Ks(i), in1=Es(i), op=ALU.mult), r=[tK(i), tE(i)], w=['kh16_%d' % i])
                    P.dma('sp', KH[t0 + 128 * i:t0 + 128 * i + 128, :], kh16[i], r=['kh16_%d' % i], w=['KHd'])
                for i in NB:
                    P.mm(ps[i][:, 0:256], [(H[:, c, 128 * i:128 * i + 128], win[:, c, 768:1024]) for c in range(8)], r=['ht%d' % b, 'wres'], w=['ps%d' % i])
                for i in NB:
                    P.op('act', _mk('activation', out=vh16[i], in_=ps[i][:, 0:256], func=AF.Copy), r=['ps%d' % i], w=['vh16_%d' % i])
                    P.dma('sp', VH[t0 + 128 * i:t0 + 128 * i + 128, :], vh16[i], r=['vh16_%d' % i], w=['VHd'])
                cnt['ps'] = 0
                for g in range(4):
                    dd, pr = (g // 2, g % 2)
                    pb_, tb_ = (ps[4 + g], 'ps%d' % (4 + g))
                    s0 = 3 * (g % 3)
                    eb, enb, kkf = (SL[s0], SL[s0 + 1], SL[s0 + 2])
                    teb, tenb, tkkf = (ST[s0], ST[s0 + 1], ST[s0 + 2])
                    qs, tqs = (SL[9 + pr], ST[9 + pr])
                    P.op('act', _mk('activation', out=eb[:, 0:tw], in_=pb_[:, 0:tw], func=AF.Exp), r=[tb_], w=[teb])
                    P.op('act', _mk('activation', out=enb[:, 0:tw], in_=pb_[:, 0:tw], func=AF.Exp, scale=-1.0), r=[tb_], w=[tenb])
                    if dd == 0:
                        pq, tq = nps4x()
                        P.mm(pq[:, 0:tw], [(win[:, c, 128 * pr:128 * pr + 128], H[:, c, 0:tw]) for c in range(8)], r=['ht%d' % b, 'wres'], w=[tq])
                        P.op('act', _mk('activation', out=qs[:, 0:tw], in_=pq[:, 0:tw], func=AF.Silu), r=[tq], w=[tqs])
                    qb, tqb = (qk16[2 * (g % 2)], 'qk16_%d' % (2 * (g % 2)))
                    P.op('dve', _mk('tensor_tensor', out=qb[:, 0:tw], in0=qs[:, 0:tw], in1=eb[:, 0:tw], op=ALU.mult), r=[tqs, teb], w=[tqb])
                    P.dma('sp', HQ[dd, 128 * pr:128 * pr + 128, t0:t0 + tw], qb[:, 0:tw], r=[tqb], w=['HQd'])
                    pk, tkk = nps4x()
                    c0_ = 256 + dd * 256 + 128 * pr
                    P.mm(pk[:, 0:tw], [(win[:, c, c0_:c0_ + 128], H[:, c, 0:tw]) for c in range(8)], r=['ht%d' % b, 'wres'], w=[tkk])
                    P.op('act', _mk('activation', out=kkf[:, 0:tw], in_=pk[:, 0:tw], func=AF.Sigmoid, scale=-1.0), r=[tkk], w=[tkkf])
                    kb, tkb = (qk16[2 * (g % 2) + 1], 'qk16_%d' % (2 * (g % 2) + 1))
                    gi = l * 4 + dd * 2 + pr
                    P.op('dve', _mk('scalar_tensor_tensor', out=kb[:, 0:tw], in0=kkf[:, 0:tw], scalar=omlF[:, gi:gi + 1], in1=enb[:, 0:tw], op0=ALU.mult, op1=ALU.mult), r=[tkkf, tenb, 'omlF'], w=[tkb])
                    P.dma('sp', HK[dd, 128 * pr:128 * pr + 128, t0:t0 + tw], kb[:, 0:tw], r=[tkb], w=['HKd'])
                    nch = tw // CH
                    ebv = eb[:, 0:tw].rearrange('p (n s) -> p n s', s=CH)
                    sel = ebv[:, :, CH - 1:CH] if dd == 0 else ebv[:, :, 0:1]
                    P.op('dve', _mk('tensor_copy', ebT[:, 16 * g:16 * g + nch].rearrange('p (n o) -> p n o', o=1), sel), r=[teb], w=['ebT%d' % g])
                    P.dma('sp', EBE[dd, pr, :, t0 // CH:t0 // CH + nch], ebT[:, 16 * g:16 * g + nch], r=['ebT%d' % g], w=['EBEd'])
                cnt['ps'] = 0
                for pr in range(2):
                    pg, tg = featmm(1024 + 128 * pr, 128, H, b, tw)
                    P.op('act', _mk('activation', out=g16[pr][:, 0:tw], in_=pg[:, 0:tw], func=AF.Silu), r=[tg], w=['g16_%d' % pr])
                    P.dma('sp', GT[128 * pr:128 * pr + 128, t0:t0 + tw], g16[pr][:, 0:tw], r=['g16_%d' % pr], w=['GTd'])
                for ch in range(2):
                    pu, tu = featmm(1952 + 128 * ch, 128, H, b, tw)
                    P.op('dve', _mk('tensor_copy', uF[:, ch, 0:tw], pu[:, 0:tw]), r=[tu], w=['uF%d' % ch])
                P.dma('sp', uT[:, t0:t0 + tw].rearrange('(c p) t -> p c t', p=128), uF[:, :, 0:tw], r=['uF0', 'uF1'], w=['uTd'])
                pcs = []
                for c in range(3):
                    pcs.append(featmm(1280 + 128 * c, 128, H, b, tw))
                for c in range(3):
                    pc, tc_ = pcs[c]
                    P.op('act', _mk('activation', out=sq3[:, c, 0:tw], in_=pc[:, 0:tw], func=AF.Square), r=[tc_], w=['sq3_%d' % c])
                for c in range(3):
                    pc, tc_ = pcs[c]
                    P.op('dve', _mk('tensor_copy', cqF[:, c, 0:tw], pc[:, 0:tw]), r=[tc_], w=['cqF%d' % c])
                pss, tss = nps()
                P.mm(pss[:, 0:tw], [(ones16[:], sq3[:, c, 0:tw]) for c in range(3)], r=['sq3_0', 'sq3_1', 'sq3_2', 'ones16'], w=[tss])
                rstd_from(pss, tss, rs, 'rs', tw, 384.0)
                for c in range(3):
                    P.op('dve', _mk('scalar_tensor_tensor', out=cqn[:, c, 0:tw], in0=cqF[:, c, 0:tw], scalar=gq(c), in1=rs[:, 0:tw], op0=ALU.mult, op1=ALU.mult), r=['cqF%d' % c, 'rs', 'gains'], w=['cqn%d' % c])

                def qmm(h, pt, tok, tw=tw):
                    P.mm(pt[0:96, 0:tw], [(wuq[:, c, 96 * h:96 * h + 96], cqn[:, c, 0:tw]) for c in range(3)], r=['cqn0', 'cqn1', 'cqn2', 'wres'], w=[tok])
                heads_block(qmm, 32 + l, rope, QT, t0, tw)
                pcs = []
                for c in range(2):
                    pcs.append(featmm(1664 + 128 * c, 128, H, b, tw))
                for c in range(2):
                    pc, tc_ = pcs[c]
                    P.op('act', _mk('activation', out=sq3[:, c, 0:tw], in_=pc[:, 0:tw], func=AF.Square), r=[tc_], w=['sq3_%d' % c])
                for c in range(2):
                    pc, tc_ = pcs[c]
                    P.op('dve', _mk('tensor_copy', ckvF[:, c, 0:tw], pc[:, 0:tw]), r=[tc_], w=['ckvF%d' % c])
                pss, tss = nps()
                P.mm(pss[:, 0:tw], [(ones16[:], sq3[:, c, 0:tw]) for c in range(2)], r=['sq3_0', 'sq3_1', 'ones16'], w=[tss])
                rstd_from(pss, tss, rs, 'rs', tw, 256.0)
                for c in range(2):
                    P.op('dve', _mk('scalar_tensor_tensor', out=ckvn[:, c, 0:tw], in0=ckvF[:, c, 0:tw], scalar=gkv(c), in1=rs[:, 0:tw], op0=ALU.mult, op1=ALU.mult), r=['ckvF%d' % c, 'rs', 'gains'], w=['ckvn%d' % c])
                pkp, tkp = featmm(1920, 32, H, b, tw)
                P.op('act', _mk('activation', out=kpe16[0:32, 0:tw], in_=pkp[0:32, 0:tw], func=AF.Copy), r=[tkp], w=['kpe16'])

                def kmm(h, pt, tok, tw=tw):
                    P.mm(pt[0:96, 0:tw], [(wkp[:, c, 96 * h:96 * h + 96], ckvn[:, c, 0:tw]) for c in range(2)] + [(sh16[:, :], kpe16[0:32, 0:tw])], r=['ckvn0', 'ckvn1', 'kpe16', 'sh16', 'wres'], w=[tok])
                heads_block(kmm, 36 + l, rope, KT, t0, tw)
                for i in range(tw // 128):
                    bb = i % 2
                    pv, tv = nps()
                    P.mm(pv[:, 0:512], [(ckvn[:, c, 128 * i:128 * i + 128], wvv[:, c, :]) for c in range(2)], r=['ckvn0', 'ckvn1', 'wres'], w=[tv])
                    P.op('act', _mk('activation', out=v16[bb][:, :, 0:64], in_=pv[:, 0:512].rearrange('p (h e) -> p h e', h=8), func=AF.Copy), r=[tv], w=['v16_%d' % bb])
                    P.dma('sp', VV[t0 + 128 * i:t0 + 128 * i + 128, :, :], v16[bb], r=['v16_%d' % bb], w=['VVd'])
            P.barrier()

        def phase_hgrn(l):
            FA.reset()
            BA.reset()
            BW.reset()
            chains = [(dd, pr) for dd in range(2) for pr in range(2)]
            qt = {}
            kt = {}
            kh = {}
            vh = {}
            eb = {}
            osb = {}
            a16 = {}
            for ci, ch in enumerate(chains):
                qt[ch] = BW.take(512)
                kt[ch] = BW.take(512)
                kh[ch] = BW.take(2048, 'p (n c) -> p n c', n=16, parts=32)
                vh[ch] = BW.take(2048, 'p (n c) -> p n c', n=16, parts=32)
                eb[ch] = FA.take(16)
                osb[ch] = FA.take(512)
                a16[ch] = BA.take(64, 'p (h c) -> p h c', h=2, parts=32)
            P.op('dve', _mk('memset', S32[:], 0.0), w=['S32_%d' % i for i in range(4)])
            P.op('pool', _mk('memset', S16[:], 0.0), w=['S16_%d' % i for i in range(4)])
            order = {0: [TILES[16]] + TILES[0:16], 1: [TILES[16]] + TILES[15::-1]}
            for step in range(17):
                for ci, ch in enumerate(chains):
                    dd, pr = ch
                    t0, tw, m = order[dd][step]
                    nch = tw // CH
                    c0 = 'c%d' % ci
                    P.dma('sp', qt[ch][:, 0:tw], HQ[dd, 128 * pr:128 * pr + 128, t0:t0 + tw], w=[c0 + 'q'])
                    P.dma('sp', kt[ch][:, 0:tw], HK[dd, 128 * pr:128 * pr + 128, t0:t0 + tw], w=[c0 + 'k'])
                    P.dma('sp', kh[ch][:, 0:nch, :], KH[t0:t0 + tw, dd * 256 + pr * 128:dd * 256 + pr * 128 + 128].rearrange('(n s) c -> s n c', s=CH), w=[c0 + 'kh'])
                    P.dma('sp', vh[ch][:, 0:nch, :], VH[t0:t0 + tw, pr * 128:pr * 128 + 128].rearrange('(n s) c -> s n c', s=CH), w=[c0 + 'vh'])
                    P.dma('sp', eb[ch][:, 0:nch], EBE[dd, pr, :, t0 // CH:t0 // CH + nch], w=[c0 + 'eb'])
                nchs = order[0][step][1] // CH
                for cidx in range(nchs):
                    info = []
                    for ci, ch in enumerate(chains):
                        dd, pr = ch
                        t0, tw, m = order[dd][step]
                        nch = tw // CH
                        cc = cidx if dd == 0 else nch - 1 - cidx
                        info.append(dict(ci=ci, ch=ch, dd=dd, c0='c%d' % ci, bank=ps[ci], btok='ps%d' % ci, obank=ps[4 + ci], otok='ps%d' % (4 + ci),
                                         msk=maskf if dd == 0 else maskb, s32=S32[:, 64 * ci:64 * ci + 64], s16=S16[:, 64 * ci:64 * ci + 64],
                                         cc=cc, sl=slice(CH * cc, CH * cc + CH)))
                    first = cidx == 0
                    for d in info:
                        ch, cc, sl, c0, bank, btok = (d['ch'], d['cc'], d['sl'], d['c0'], d['bank'], d['btok'])
                        for hh in range(2):
                            pb = 64 * hh
                            P.mm(bank[pb:pb + 64, 0:64], [(kh[ch][0:32, cc, pb:pb + 64], vh[ch][0:32, cc, pb:pb + 64])], r=[c0 + 'kh', c0 + 'vh'], w=[btok + 'U'])
                            P.mm(bank[0:32, 64 + 32 * hh:96 + 32 * hh], [(kt[ch][pb:pb + 64, sl], qt[ch][pb:pb + 64, sl])], r=[c0 + 'k', c0 + 'q'], w=[btok + 'A%d' % hh])
                    for d in info:
                        ch, c0, bank, btok, msk = (d['ch'], d['c0'], d['bank'], d['btok'], d['msk'])
                        for hh in range(2):
                            P.op('dve', _mk('tensor_tensor', out=a16[ch][0:32, hh, :], in0=bank[0:32, 64 + 32 * hh:96 + 32 * hh], in1=msk[:, :], op=ALU.mult), r=[btok + 'A%d' % hh, 'maskf', 'maskb'], w=[c0 + 'a%d' % hh])
                    for d in info:
                        ch, cc, sl, c0, ci, obank, otok, s16 = (d['ch'], d['cc'], d['sl'], d['c0'], d['ci'], d['obank'], d['otok'], d['s16'])
                        for hh in range(2):
                            pb = 64 * hh
                            P.mm(obank[pb:pb + 64, sl], [(s16[pb:pb + 64, :], qt[ch][pb:pb + 64, sl]), (vh[ch][0:32, cc, pb:pb + 64], a16[ch][0:32, hh, :])], r=['S16_%d' % ci, c0 + 'q', c0 + 'vh', c0 + 'a%d' % hh], w=[otok] if first else [otok + 'x'])
                    for d in info:
                        ch, cc, c0, ci, bank, btok, s32 = (d['ch'], d['cc'], d['c0'], d['ci'], d['bank'], d['btok'], d['s32'])
                        P.op('dve', _mk('scalar_tensor_tensor', out=s32, in0=s32, scalar=eb[ch][:, cc:cc + 1], in1=bank[:, 0:64], op0=ALU.mult, op1=ALU.add), r=[btok + 'U', c0 + 'eb', 'S32_%d' % ci], w=['S32_%d' % ci])
                    for d in info:
                        ci, s32, s16 = (d['ci'], d['s32'], d['s16'])
                        P.op('act', _mk('activation', out=s16, in_=s32, func=AF.Copy), r=['S32_%d' % ci], w=['S16_%d' % ci])
                for ci, ch in enumerate(chains):
                    dd, pr = ch
                    t0, tw, m = order[dd][step]
                    c0 = 'c%d' % ci
                    obank = ps[4 + ci]
                    otok = 'ps%d' % (4 + ci)
                    P.op('act', _mk('activation', out=osb[ch][:, 0:tw], in_=obank[:, 0:tw], func=AF.Copy), r=[otok, otok + 'x'], w=[c0 + 'o'])
                    P.dma('pool', OFB[dd, 128 * pr:128 * pr + 128, t0:t0 + tw], osb[ch][:, 0:tw], r=[c0 + 'o'], w=['OFBd'])
            P.barrier()
            FA.reset()
            BA.reset()
            of = [FA.take(512) for _ in range(2)]
            obb = [FA.take(512) for _ in range(2)]
            rs = FA.take(512)
            gt = [BA.take(512) for _ in range(2)]
            sq = BA.take(512)
            mo = [BA.take(512) for _ in range(2)]
            k = 0
            for t0, tw, m in TILES:
                for pr in range(2):
                    b = k % 2
                    k += 1
                    P.dma('sp', of[b][:, 0:tw], OFB[0, 128 * pr:128 * pr + 128, t0:t0 + tw], w=['of%d' % b])
                    P.dma('sp', obb[b][:, 0:tw], OFB[1, 128 * pr:128 * pr + 128, t0:t0 + tw], w=['ob%d' % b])
                    P.dma('sp', gt[b][:, 0:tw], GT[128 * pr:128 * pr + 128, t0:t0 + tw], w=['gt%d' % b])
                    P.op('dve', _mk('tensor_tensor', out=of[b][:, 0:tw], in0=of[b][:, 0:tw], in1=obb[b][:, 0:tw], op=ALU.add), r=['ob%d' % b], w=['of%d' % b])
                    P.op('act', _mk('activation', out=sq[:, 0:tw], in_=of[b][:, 0:tw], func=AF.Square), r=['of%d' % b], w=['sq'])
                    pt, tok = (ps[k % 4], 'ps%d' % (k % 4))
                    P.mm(pt[:, 0:tw], [(bd16[:], sq[:, 0:tw])], r=['sq', 'bd16'], w=[tok])
                    rstd_from(pt, tok, rs, 'rs', tw, 64.0)
                    P.op('dve', _mk('scalar_tensor_tensor', out=of[b][:, 0:tw], in0=of[b][:, 0:tw], scalar=gains[:, l:l + 1], in1=rs[:, 0:tw], op0=ALU.mult, op1=ALU.mult), r=['rs', 'gains'], w=['of%d' % b])
                    P.op('dve', _mk('tensor_tensor', out=mo[b][:, 0:tw], in0=of[b][:, 0:tw], in1=gt[b][:, 0:tw], op=ALU.mult), r=['of%d' % b, 'gt%d' % b], w=['mo%d' % b])
                    P.dma('pool', mixT[128 * pr:128 * pr + 128, t0:t0 + tw], mo[b][:, 0:tw], r=['mo%d' % b], w=['mixd'])
            P.barrier()

        def phase_attn(l, with_ctx):
            FA.reset()
            BA.reset()
            BW.reset()
            ktb = [BW.take(T, parts=96) for _ in range(2)]
            vtb = [BW.take(66 * 65, 'p (k e) -> p k e', e=65) for _ in range(2)]
            qtb = [BA.take(512, parts=96) for _ in range(2)]
            pb16 = [BA.take(512) for _ in range(4)]
            mo = [BA.take(512, parts=64) for _ in range(2)]
            osb = [FA.take(512, parts=64) for _ in range(2)]
            rden = FA.take(512)
            scale = 96.0 ** (-0.5)
            jobs = []
            for h in range(8):
                qtiles = [(t0, tw, list(range(66))) for t0, tw, m in TILES if m == 0]
                if with_ctx:
                    qtiles.append((TL, TC, [64, 65]))
                for t0, tw, kcs in qtiles:
                    jobs.append(dict(h=h, hb=h % 2, t0=t0, tw=tw, kcs=kcs, qb=len(jobs) % 2, first_of_head=(t0 == 0)))
            items = []
            for ji, jb in enumerate(jobs):
                for n, kc in enumerate(jb['kcs']):
                    items.append((ji, n, kc))

            def load_head(h):
                hb = h % 2
                P.dma('sp', ktb[hb][:, :], KT[h, :, :], w=['kt%d' % hb])
                P.dma('sp', vtb[hb][:, :, :], VV[:, h, :].rearrange('(k p) e -> p k e', p=128), w=['vt%d' % hb])

            def load_q(ji):
                jb = jobs[ji]
                P.dma('sp', qtb[jb['qb']][:, 0:jb['tw']], QT[jb['h'], :, jb['t0']:jb['t0'] + jb['tw']], w=['q%d' % jb['qb']])

            def emit_qk(idx):
                ji, n, kc = items[idx]
                jb = jobs[ji]
                tw, hb, qb = (jb['tw'], jb['hb'], jb['qb'])
                if n == 0 and ji + 1 < len(jobs):
                    load_q(ji + 1)
                bank = idx % 4
                P.mm(ps[bank][:, 0:tw], [(ktb[hb][:, 128 * kc:128 * kc + 128], qtb[qb][:, 0:tw])], r=['kt%d' % hb, 'q%d' % qb], w=['ps%d' % bank])

            def emit_pv(idx):
                ji, n, kc = items[idx]
                jb = jobs[ji]
                tw, hb, qb, h, t0 = (jb['tw'], jb['hb'], jb['qb'], jb['h'], jb['t0'])
                bank = idx % 4
                pbuf = pb16[bank]
                tpb = 'pb%d' % bank
                po = ps[4 + qb]
                tpo = 'ps%d' % (4 + qb)
                P.op('act', _mk('activation', out=pbuf[:, 0:tw], in_=ps[bank][:, 0:tw], func=AF.Exp, scale=scale), r=['ps%d' % bank], w=[tpb])
                first = n == 0
                last = n == len(jb['kcs']) - 1
                P.op('pe', _mk('matmul', po[0:65, 0:tw], vtb[hb][:, kc, :], pbuf[:, 0:tw], start=first, stop=last), r=[tpb, 'vt%d' % hb], w=[tpo] if first else [tpo + 'x'], signal=True)
                if first and jb['first_of_head'] and h + 1 < 8:
                    load_head(h + 1)
                if last:
                    P.op('dve', _mk('reciprocal', out=rden[64:65, 0:tw], in_=po[64:65, 0:tw]), r=[tpo, tpo + 'x'], w=['rden'])
                    P.op('act', _mk('activation', out=osb[qb][0:64, 0:tw], in_=po[0:64, 0:tw], func=AF.Copy), r=[tpo, tpo + 'x'], w=['osb%d' % qb])
                    P.mm(ps[6][0:64, 0:tw], [(onesF[64:65, 0:64], rden[64:65, 0:tw])], r=['rden', 'onesF'], w=['ps6'])
                    P.op('dve', _mk('tensor_tensor', out=mo[qb][0:64, 0:tw], in0=osb[qb][0:64, 0:tw], in1=ps[6][0:64, 0:tw], op=ALU.mult), r=['osb%d' % qb, 'ps6'], w=['mo%d' % qb])
                    P.dma('pool', mixT[256 + 64 * h:256 + 64 * h + 64, t0:t0 + tw], mo[qb][0:64, 0:tw], r=['mo%d' % qb], w=['mixd'])
            load_head(0)
            load_q(0)
            LA = 3
            for i in range(len(items) + LA):
                if i < len(items):
                    emit_qk(i)
                if i >= LA:
                    emit_pv(i - LA)
            P.barrier()

        def phase_pool(l, with_ctx):
            FA.reset()
            BA.reset()
            BW.reset()
            pw = BW.take(256, 'p (c n) -> p c n', c=2)
            stg = FA.take(256, 'p (c n) -> p c n', c=2)
            for ch in range(2):
                P.dma('sp', stg[:, ch, :], I['pwb'][l, ch, :, :], w=['stg'])
            P.op('pool', _mk('tensor_copy', pw, stg), r=['stg'], w=['wres'])
            W = 512 + 16
            ub = [FA.take(2 * W, 'p (c t) -> p c t', c=2) for _ in range(2)]
            a1 = FA.take(2 * W, 'p (c t) -> p c t', c=2)
            a2 = FA.take(2 * W, 'p (c t) -> p c t', c=2)
            a3 = FA.take(W)
            a4 = FA.take(W)
            sm = FA.take(1024, 'p (c t) -> p c t', c=2)
            pl16 = BA.take(1024, 'p (c t) -> p c t', c=2)
            mo = [BA.take(512) for _ in range(4)]
            k = 0
            for ti, (t0, tw, m) in enumerate(TILES):
                if m == 1 and (not with_ctx):
                    continue
                b = ti % 2
                U = ub[b]
                seq0, seq1 = (0, TL) if m == 0 else (TL, T)
                lo = max(t0 - 8, seq0)
                hi = min(t0 + tw + 8, seq1)
                if lo > t0 - 8 or hi < t0 + tw + 8:
                    P.op('pool', _mk('memset', U[:, :, :], 0.0), w=['ub%d' % b])
                P.dma('sp', U[:, :, lo - (t0 - 8):hi - (t0 - 8)], uT[:, lo:hi].rearrange('(c p) t -> p c t', p=128), w=['ub%d' % b])
                n1 = tw + 15
                P.op('dve', _mk('tensor_tensor', out=a1[:, :, 0:n1], in0=U[:, :, 0:n1], in1=U[:, :, 1:n1 + 1], op=ALU.add), r=['ub%d' % b], w=['a1'])
                n2 = tw + 13
                P.op('dve', _mk('tensor_tensor', out=a2[:, :, 0:n2], in0=a1[:, :, 0:n2], in1=a1[:, :, 2:n2 + 2], op=ALU.add), r=['a1'], w=['a2'])
                n3 = tw + 9
                P.op('dve', _mk('tensor_tensor', out=a3[:, 0:n3], in0=a2[:, 1, 0:n3], in1=a2[:, 1, 4:n3 + 4], op=ALU.add), r=['a2'], w=['a3'])
                n4 = tw + 1
                P.op('dve', _mk('tensor_tensor', out=a4[:, 0:n4], in0=a3[:, 0:n4], in1=a3[:, 8:n4 + 8], op=ALU.add), r=['a3'], w=['a4'])
                P.op('dve', _mk('tensor_scalar', out=sm[0:64, 0, 0:tw], in0=a1[0:64, 0, 7:7 + tw], scalar1=0.5, scalar2=None, op0=ALU.mult), r=['a1'], w=['sm0a'])
                P.op('dve', _mk('tensor_scalar', out=sm[64:128, 0, 0:tw], in0=a2[64:128, 0, 6:6 + tw], scalar1=0.25, scalar2=None, op0=ALU.mult), r=['a2'], w=['sm0b'])
                P.op('dve', _mk('tensor_scalar', out=sm[0:64, 1, 0:tw], in0=a3[0:64, 4:4 + tw], scalar1=0.125, scalar2=None, op0=ALU.mult), r=['a3'], w=['sm1a'])
                P.op('dve', _mk('tensor_scalar', out=sm[64:128, 1, 0:tw], in0=a4[64:128, 0:tw], scalar1=0.0625, scalar2=None, op0=ALU.mult), r=['a4'], w=['sm1b'])
                smt = ['sm0a', 'sm0b', 'sm1a', 'sm1b']
                if t0 == seq0:
                    P.op('dve', _mk('tensor_tensor', out=sm[:, :, 0:8], in0=sm[:, :, 0:8], in1=corr[:, 0:32].rearrange('p (c t) -> p c t', c=2)[:, :, 0:8], op=ALU.mult), r=smt + ['corr'], w=smt)
                if t0 + tw == seq1:
                    P.op('dve', _mk('tensor_tensor', out=sm[:, :, tw - 8:tw], in0=sm[:, :, tw - 8:tw], in1=corr[:, 0:32].rearrange('p (c t) -> p c t', c=2)[:, :, 8:16], op=ALU.mult), r=smt + ['corr'], w=smt)
                P.op('dve', _mk('tensor_tensor', out=pl16[:, :, 0:tw], in0=sm[:, :, 0:tw], in1=U[:, :, 8:8 + tw], op=ALU.subtract), r=smt + ['ub%d' % b], w=['pl16'])
                for ch in range(2):
                    pt, tok = (ps[k % 4], 'ps%d' % (k % 4))
                    mb = mo[k % 4]
                    mtok = 'mo%d' % (k % 4)
                    k += 1
                    P.mm(pt[:, 0:tw], [(pw[:, ch, :], pl16[:, ch, 0:tw])], r=['pl16', 'wres'], w=[tok])
                    P.op('act', _mk('activation', out=mb[:, 0:tw], in_=pt[:, 0:tw], func=AF.Identity, scale=gains[:, 24 + 2 * l + ch:25 + 2 * l + ch]), r=[tok, 'gains'], w=[mtok])
                    P.dma('pool', mixT[768 + 128 * ch:768 + 128 * ch + 128, t0:t0 + tw], mb[:, 0:tw], r=[mtok], w=['mixd'])
            P.barrier()
        steps = [load_consts, phase_mod, phase_in_transpose]
        for l in range(n_layers):
            last = l == NL - 1
            steps += [
                (lambda l=l: phase_norm(l, 0)),
                (lambda l=l: phase_ffn(l, 0, I['f1i'], I['f1o'])),
                (lambda l=l: phase_norm(l, 1)),
                (lambda l=l: phase_inproj(l)),
                (lambda l=l: phase_hgrn(l)),
                (lambda l=l, last=last: phase_attn(l, not last)),
                (lambda l=l, last=last: phase_pool(l, not last)),
                (lambda l=l: phase_outproj(l)),
                (lambda l=l: phase_norm(l, 2)),
                (lambda l=l: phase_ffn(l, 2, I['f2i'], I['f2o'])),
            ]
        steps.append(phase_out_transpose)
        for si, fn in enumerate(steps):
            if upto is not None and si >= upto:
                break
            try:
                fn()
            except _Stop:
                break
        P.barrier()
        P.emit()
    return nc

def host_constants():
    c = {}
    c['ident'] = np.eye(128, dtype=np.float32)
    s = np.arange(128)[:, None]
    t = np.arange(128)[None, :]
    same = s // CH == t // CH
    c['d1f'] = (same & (s > t)).astype(np.float32)
    c['d1b'] = (same & (s < t)).astype(np.float32)
    c['trf'] = (same & (s <= t)).astype(np.float32)
    c['trb'] = (same & (s >= t)).astype(np.float32)
    s2 = np.arange(CH)[:, None]
    t2 = np.arange(CH)[None, :]
    c['maskf'] = (s2 <= t2).astype(np.float32)
    c['maskb'] = (s2 >= t2).astype(np.float32)
    c['bd64'] = (s // 64 == t // 64).astype(np.float32)
    rot = np.zeros((96, 96), np.float32)
    for i in range(16):
        rot[80 + i, 64 + i] = -1.0
        rot[64 + i, 80 + i] = 1.0
    c['rot'] = rot
    sh = np.zeros((32, 96), np.float32)
    sh[np.arange(32), 64 + np.arange(32)] = 1.0
    c['shift'] = sh
    pos = np.arange(TL)
    row = (pos // 64).astype(np.float32)
    col = (pos % 64).astype(np.float32)
    inv = (np.float32(10000.0) ** (-np.arange(8, dtype=np.float32) / np.float32(8))).astype(np.float32)
    ang = np.concatenate([row[:, None] * inv[None, :], col[:, None] * inv[None, :]], axis=1).astype(np.float32)
    cs = np.cos(ang).astype(np.float32).T
    sn = np.sin(ang).astype(np.float32).T
    cosT = np.ones((96, TL), np.float32)
    sinT = np.zeros((96, TL), np.float32)
    cosT[64:80] = cs
    cosT[80:96] = cs
    sinT[64:80] = sn
    sinT[80:96] = sn
    c['cosT'] = cosT
    c['sinT'] = sinT
    corr = np.ones((128, 2, 16), np.float32)
    for g, w in enumerate((2, 4, 8, 16)):
        ch, p0 = (g // 2, g % 2 * 64)
        for i in range(8):
            lo = max(i - w // 2, 0)
            hi = i + w - 1 - w // 2
            corr[p0:p0 + 64, ch, i] = w / float(hi - lo + 1)
            d = 7 - i
            hi2 = min(w - 1 - w // 2, d)
            cnt = hi2 + w // 2 + 1
            corr[p0:p0 + 64, ch, 8 + i] = w / float(cnt)
    c['corr'] = corr.reshape(128, 32)
    return c
_CACHE = {}

def prep_inputs(inputs, b):
    f = lambda a: np.ascontiguousarray(np.asarray(a, dtype=np.float32))
    m = {}
    m['x_in'] = f(inputs['x'][b])
    m['ctx_in'] = f(inputs['ctx'][b])
    cv = np.stack([np.asarray(inputs['c'][b]), np.asarray(inputs['c_ctx'])], 0)
    m['cvT'] = f(cv.reshape(2, 8, 128).transpose(2, 1, 0).reshape(128, 16))
    m['w_mod'] = f(inputs['w_mod'])
    m['b_mod'] = f(inputs['b_mod'])
    m['f1i'] = f(inputs['ffn1_w_in'])
    m['f1o'] = f(inputs['ffn1_w_out'])
    m['f2i'] = f(inputs['ffn2_w_in'])
    m['f2o'] = f(inputs['ffn2_w_out'])
    m['w_in'] = f(inputs['w_in'])
    m['w_out'] = f(inputs['w_out'])
    lb = np.asarray(inputs['hg_lb_logits'], np.float32)
    m['lblB'] = f(np.broadcast_to(lb.reshape(1, NL * 512), (128, NL * 512)))
    m['lblF'] = f(lb.reshape(NL, 2, 2, 128).transpose(3, 0, 1, 2).reshape(128, 16))
    m['g_hg'] = f(np.tile(np.asarray(inputs['hg_out_gain'], np.float32), (1, 2)).T)
    m['g_qa'] = f(np.asarray(inputs['mla_q_a_gain'], np.float32).reshape(NL, 3, 128).transpose(2, 0, 1).reshape(128, NL * 3))
    m['g_kva'] = f(np.asarray(inputs['mla_kv_a_gain'], np.float32).reshape(NL, 2, 128).transpose(2, 0, 1).reshape(128, NL * 2))
    m['g_q'] = f(np.asarray(inputs['mla_q_gain'], np.float32).T)
    m['g_k'] = f(np.asarray(inputs['mla_k_gain'], np.float32).T)
    m['g_ps'] = f(np.asarray(inputs['pool_scale'], np.float32).reshape(NL, 2, 128).transpose(2, 0, 1).reshape(128, NL * 2))
    m['w_uq'] = f(inputs['mla_w_uq'])
    wukv = np.asarray(inputs['mla_w_ukv'], np.float32).reshape(NL, 256, 8, 128)
    wk = np.zeros((NL, 256, 8, 96), np.float32)
    wk[..., 0:64] = wukv[..., 0:64]
    m['wk_pad'] = f(wk.reshape(NL, 256, 768))
    m['wv'] = f(wukv[..., 64:128].reshape(NL, 256, 512))
    pw = np.asarray(inputs['pool_w'], np.float32)
    pwb = np.zeros((NL, 2, 128, 128), np.float32)
    for ch in range(2):
        pwb[:, ch, 0:64, 0:64] = pw[:, 2 * ch]
        pwb[:, ch, 64:128, 64:128] = pw[:, 2 * ch + 1]
    m['pwb'] = pwb
    return m

def kernel(**inputs):
    if 'nc' not in _CACHE:
        _CACHE['nc'] = build()
        _CACHE['consts'] = host_constants()
    nc = _CACHE['nc']
    in_maps = []
    per_b = [prep_inputs(inputs, b) for b in range(4)]
    for core in range(8):
        mm = dict(per_b[core % 4])
        mm.update(_CACHE['consts'])
        in_maps.append(mm)
    res = run_bass_kernel_spmd(nc, in_maps, core_ids=list(range(8)))
    out = np.stack([np.asarray(res.results[b]['out'], np.float32) for b in range(4)], 0)
    return out
```

```python
import numpy as np
import ml_dtypes
from contextlib import ExitStack
import concourse.bass as bass
import concourse.mybir as mybir
from concourse.bass_utils import run_bass_kernel_spmd
F32 = mybir.dt.float32
BF16 = mybir.dt.bfloat16
AF = mybir.ActivationFunctionType
ALU = mybir.AluOpType
D = 1024
TL = 8192
TC = 256
T = TL + TC
DFF = 2816
NL = 4
EPS = 1e-06
INW = 2208
CH = 32
NCHUNK = T // CH
ENGS = ('pe', 'act', 'dve', 'pool', 'sp')


def _mk(method, *args, **kwargs):
    return lambda e: getattr(e, method)(*args, **kwargs)


class Ev:
    __slots__ = ('sem', 'val')

    def __init__(self, sem, val):
        self.sem = sem
        self.val = val

class Prog:

    def __init__(self, nc, n_dma_sems=12):
        self.nc = nc
        self.ops = {e: [] for e in ENGS}
        self.cnt = {e: 0 for e in ENGS}
        self.known = {e: {} for e in ENGS}
        self.last_w = {}
        self.readers = {}
        self.n_dma_sems = n_dma_sems
        self.dma_uses = {}
        self.dma_rr = {e: 0 for e in ENGS}
        self.pending = {e: [] for e in ENGS}

    def _need(self, eng, ev, waits):
        if ev is None:
            return
        if ev.val is None:
            if ev.sem == ('eng', 'pe') and eng == 'pe':
                return
            raise RuntimeError('wait on unresolved event')
        k = self.known[eng]
        if k.get(ev.sem, 0) >= ev.val:
            return
        if ev.sem == ('eng', 'pe') and eng == 'pe':
            return
        k[ev.sem] = ev.val
        waits[ev.sem] = max(waits.get(ev.sem, 0), ev.val)

    def _deps(self, eng, r, w):
        waits = {}
        for t in r:
            self._need(eng, self.last_w.get(t), waits)
        for t in w:
            self._need(eng, self.last_w.get(t), waits)
            for ev in self.readers.get(t, ()):
                self._need(eng, ev, waits)
        return waits

    def _commit(self, ev, r, w):
        for t in r:
            self.readers.setdefault(t, []).append(ev)
        for t in w:
            self.last_w[t] = ev
            self.readers[t] = []

    def op(self, eng, fn, r=(), w=(), signal=True):
        w = list(w) + ['bank' + t[2] for t in list(r) + list(w) if t.startswith('ps') and t[2:3].isdigit()]
        waits = self._deps(eng, r, w)
        if signal:
            self.cnt[eng] += 1
            ev = Ev(('eng', eng), self.cnt[eng])
            for p in self.pending[eng]:
                p.val = ev.val
            self.pending[eng] = []
        else:
            ev = Ev(('eng', eng), None)
            self.pending[eng].append(ev)
        self.ops[eng].append((fn, waits, ('eng', eng) if signal else None, 1))
        self._commit(ev, r, w)
        return ev

    def mm(self, out, pairs, r=(), w=()):
        n = len(pairs)

        def mk(i, lhsT, rhs):
            return _mk('matmul', out, lhsT, rhs, start=i == 0, stop=i == n - 1)
        ev = None
        for i, (lhsT, rhs) in enumerate(pairs):
            ev = self.op('pe', mk(i, lhsT, rhs), r=r if i == 0 else (), w=w if i == 0 else (), signal=i == n - 1)
        return ev

    def dma(self, eng, out, in_, r=(), w=(), **kw):
        waits = self._deps(eng, r, w)
        idx = self.dma_rr[eng]
        self.dma_rr[eng] = (idx + 1) % self.n_dma_sems
        key = ('dma', eng, idx)
        uses = self.dma_uses.get(key, 0)
        if uses > 0:
            k = self.known[eng]
            if k.get(key, 0) < 16 * uses:
                k[key] = 16 * uses
                waits[key] = max(waits.get(key, 0), 16 * uses)
        self.dma_uses[key] = uses + 1
        ev = Ev(key, 16 * (uses + 1))
        self.ops[eng].append((_mk('dma_start', out=out, in_=in_, **kw), waits, key, 16))
        self._commit(ev, r, w)
        return ev

    def barrier(self, final=False):
        for e in ENGS:
            waits = {}
            k = self.known[e]
            for e2 in ENGS:
                v = self.cnt[e2]
                key = ('eng', e2)
                if v > 0 and k.get(key, 0) < v:
                    k[key] = v
                    waits[key] = v
            for key, uses in self.dma_uses.items():
                v = 16 * uses
                if k.get(key, 0) < v:
                    k[key] = v
                    waits[key] = v
            self.ops[e].append((None, waits, None, 0))
        self.last_w.clear()
        self.readers.clear()

    def emit(self):
        nc = self.nc
        handles = {'pe': 'tensor', 'act': 'scalar', 'dve': 'vector', 'pool': 'gpsimd', 'sp': 'sync'}
        with ExitStack() as st:
            sems = {}
            for e in ENGS:
                sems['eng', e] = st.enter_context(nc.semaphore('s_' + e))
            for key in self.dma_uses:
                sems[key] = st.enter_context(nc.semaphore('d_%s_%d' % (key[1], key[2])))
            block = st.enter_context(nc.Block())

            def run(e):

                def body(h):
                    for fn, waits, sig, inc in self.ops[e]:
                        for s, v in waits.items():
                            h.wait_ge(sems[s], v)
                        if fn is not None:
                            ins = fn(h)
                            if sig is not None:
                                ins.then_inc(sems[sig], inc)
                return body
            for e in ENGS:
                getattr(block, handles[e])(run(e))

class Arena:

    def __init__(self, t, n):
        self.t = t
        self.n = n
        self.off = 0

    def reset(self):
        self.off = 0

    def take(self, size, pat=None, parts=128, **kw):
        assert self.off + size <= self.n, (self.off, size, self.n)
        v = self.t[0:parts, self.off:self.off + size]
        self.off += size
        if pat:
            v = v.rearrange(pat, **kw)
        return v
TILES = [(i * 512, 512, 0) for i in range(TL // 512)] + [(TL, TC, 1)]

import os
STOP = int(os.environ.get('KSTOP', '99'))
STOP2 = int(os.environ.get('KSTOP2', '0'))


class _Stop(Exception):
    pass


def build(n_layers=NL, debug=(), upto=None):
    nc = bass.Bass('TRN2', target_bir_lowering=False)

    def din(name, shape, dt=F32):
        return nc.dram_tensor(name, list(shape), dt, kind='ExternalInput').ap()

    def dscr(name, shape, dt=F32):
        if name in debug:
            return nc.dram_tensor(name, list(shape), dt, kind='ExternalOutput').ap()
        return nc.dram_tensor(name, list(shape), dt).ap()
    I = {}
    for name, shape in [('x_in', (TL, D)), ('ctx_in', (TC, D)), ('cvT', (128, 16)), ('w_mod', (NL, D, 9 * D)), ('b_mod', (NL, 9 * D)), ('f1i', (NL, D, 2 * DFF)), ('f1o', (NL, DFF, D)), ('f2i', (NL, D, 2 * DFF)), ('f2o', (NL, DFF, D)), ('w_in', (NL, D, INW)), ('w_out', (NL, D, D)), ('lblB', (128, NL * 512)), ('lblF', (128, 16)), ('g_hg', (128, NL)), ('g_qa', (128, NL * 3)), ('g_kva', (128, NL * 2)), ('g_q', (96, NL)), ('g_k', (96, NL)), ('g_ps', (128, NL * 2)), ('w_uq', (NL, 384, 768)), ('wk_pad', (NL, 256, 768)), ('wv', (NL, 256, 512)), ('pwb', (NL, 2, 128, 128)), ('ident', (128, 128)), ('d1f', (128, 128)), ('d1b', (128, 128)), ('trf', (128, 128)), ('trb', (128, 128)), ('maskf', (32, 32)), ('maskb', (32, 32)), ('bd64', (128, 128)), ('rot', (96, 96)), ('shift', (32, 96)), ('cosT', (96, TL)), ('sinT', (96, TL)), ('corr', (128, 32))]:
        I[name] = din(name, shape)
    out = nc.dram_tensor('out', [TL, D], F32, kind='ExternalOutput').ap()
    xT = dscr('xT', (D, T))
    hT = dscr('hT', (D, T), BF16)
    mixT = dscr('mixT', (D, T), BF16)
    QT = dscr('QT', (8, 96, T), BF16)
    KT = dscr('KT', (8, 96, T), BF16)
    VV = dscr('VV', (T, 8, 65), BF16)
    uT = dscr('uT', (256, T))
    KH = dscr('KH', (T, 512), BF16)
    VH = dscr('VH', (T, 256), BF16)
    HQ = dscr('HQ', (2, 256, T), BF16)
    HK = dscr('HK', (2, 256, T), BF16)
    EBE = dscr('EBE', (2, 2, 128, NCHUNK))
    GT = dscr('GT', (256, T), BF16)
    OFB = dscr('OFB', (2, 256, T))
    MODD = dscr('MODD', (128, NL * 144))
    with ExitStack() as st:

        def sb(n, s, d=F32):
            return st.enter_context(nc.sbuf_tensor(n, list(s), d))
        NBW, NBA, NFA = (36000, 20480, 13312)
        BWt = sb('BW', (128, NBW), BF16)
        BAt = sb('BA', (128, NBA), BF16)
        FAt = sb('FA', (128, NFA), F32)
        BW, BA, FA = (Arena(BWt, NBW), Arena(BAt, NBA), Arena(FAt, NFA))
        psb = [st.enter_context(nc.psum_tensor('psb%d' % i, [128, 1024], F32)) for i in range(4)]
        ps = [psb[i // 2][:, 512 * (i % 2):512 * (i % 2) + 512] for i in range(8)]
        MOD = sb('MOD', (128, NL * 144))
        SC1 = sb('SC1', (128, NL * 48))
        GHT = sb('GHT', (128, NL * 48))
        cact = sb('cact', (128, 16))
        identF = sb('identF', (128, 128))
        d1f = sb('d1fS', (128, 128))
        d1b = sb('d1bS', (128, 128))
        trf = sb('trfS', (128, 128))
        trb = sb('trbS', (128, 128))
        maskf = sb('maskfS', (32, 32))
        maskb = sb('maskbS', (32, 32))
        bdF = sb('bdF', (128, 128))
        bd16 = sb('bd16', (128, 128), BF16)
        ones16 = sb('ones16', (128, 128), BF16)
        onesF = sb('onesF', (128, 128))
        rotF = sb('rotF', (96, 96))
        shF = sb('shF', (32, 96))
        sh16 = sb('sh16', (32, 96), BF16)
        corr = sb('corrS', (128, 32))
        epsT = sb('epsT', (128, 1))
        gains = sb('gains', (128, 64))
        lbF = sb('lbF', (128, 16))
        omlF = sb('omlF', (128, 16))
        lbtmp = sb('lbtmp', (128, 16))
        S32 = sb('S32', (128, 4 * 64))
        S16 = sb('S16', (128, 4 * 64), BF16)
        P = Prog(nc)
        ckc = [0]

        def ck():
            ckc[0] += 1
            if ckc[0] == STOP2:
                P.barrier()
                raise _Stop()

        def mod_ap(tile, l, s, c, m, n=48):
            i = ((l * (n // 16) + s) * 8 + c) * 2 + m
            return tile[:, i:i + 1]

        def load_consts():
            for dst, name in [(identF, 'ident'), (d1f, 'd1f'), (d1b, 'd1b'), (trf, 'trf'), (trb, 'trb'), (maskf, 'maskf'), (maskb, 'maskb'), (bdF, 'bd64'), (rotF, 'rot'), (shF, 'shift'), (corr, 'corr'), (cact, 'cvT'), (lbF, 'lblF')]:
                P.dma('sp', dst[:], I[name][:, :], w=[name])
            P.dma('sp', gains[:, 0:4], I['g_hg'][:, :], w=['gains'])
            P.dma('sp', gains[:, 4:16], I['g_qa'][:, :], w=['gains'])
            P.dma('sp', gains[:, 16:24], I['g_kva'][:, :], w=['gains'])
            P.dma('sp', gains[:, 24:32], I['g_ps'][:, :], w=['gains'])
            P.dma('sp', gains[0:96, 32:36], I['g_q'][:, :], w=['gains'])
            P.dma('sp', gains[0:96, 36:40], I['g_k'][:, :], w=['gains'])
            P.op('dve', _mk('memset', onesF[:], 1.0), w=['onesF'])
            P.op('dve', _mk('memset', epsT[:], EPS), w=['epsT'])
            P.op('pool', _mk('memset', ones16[:], 1.0), w=['ones16'])
            P.op('pool', _mk('tensor_copy', bd16[:], bdF[:]), r=['bd64'], w=['bd16'])
            P.op('pool', _mk('tensor_copy', sh16[:], shF[:]), r=['shift'], w=['sh16'])
            P.op('act', _mk('activation', out=cact[:], in_=cact[:], func=AF.Silu), r=['cvT'], w=['cvT'])
            L = lambda l: lbF[:, 4 * l:4 * l + 4]
            mx = lbtmp[:, 0:4]
            sm = lbtmp[:, 4:8]
            rc = lbtmp[:, 8:12]
            P.op('dve', _mk('tensor_tensor', out=mx, in0=L(0), in1=L(1), op=ALU.max), r=['lblF'], w=['lbt'])
            P.op('dve', _mk('tensor_tensor', out=mx, in0=mx, in1=L(2), op=ALU.max), r=['lbt'], w=['lbt'])
            P.op('dve', _mk('tensor_tensor', out=mx, in0=mx, in1=L(3), op=ALU.max), r=['lbt'], w=['lbt'])
            for l in range(4):
                P.op('dve', _mk('tensor_tensor', out=L(l), in0=L(l), in1=mx, op=ALU.subtract), r=['lbt', 'lblF'], w=['lblF'])
            P.op('act', _mk('activation', out=lbF[:], in_=lbF[:], func=AF.Exp), r=['lblF'], w=['lblF'])
            P.op('dve', _mk('tensor_tensor', out=sm, in0=L(0), in1=L(1), op=ALU.add), r=['lblF'], w=['lbt'])
            P.op('dve', _mk('tensor_tensor', out=sm, in0=sm, in1=L(2), op=ALU.add), r=['lbt'], w=['lbt'])
            P.op('dve', _mk('tensor_tensor', out=sm, in0=sm, in1=L(3), op=ALU.add), r=['lbt'], w=['lbt'])
            P.op('dve', _mk('reciprocal', out=rc, in_=sm), r=['lbt'], w=['lbt'])
            for l in range(4):
                P.op('dve', _mk('tensor_tensor', out=L(l), in0=L(l), in1=rc, op=ALU.mult), r=['lbt', 'lblF'], w=['lblF'])
            P.op('dve', _mk('memset', L(0), 0.0), r=['lblF'], w=['lblF'])
            P.op('dve', _mk('tensor_tensor', out=L(2), in0=L(2), in1=L(1), op=ALU.add), r=['lblF'], w=['lblF'])
            P.op('dve', _mk('tensor_tensor', out=L(3), in0=L(3), in1=L(2), op=ALU.add), r=['lblF'], w=['lblF'])
            P.op('dve', _mk('tensor_scalar', out=omlF[:], in0=lbF[:], scalar1=-1.0, scalar2=1.0, op0=ALU.mult, op1=ALU.add), r=['lblF'], w=['omlF'])
            P.barrier()

        def phase_mod():
            FA.reset()
            stg = [FA.take(4096, 'p (c n) -> p c n', c=8) for _ in range(2)]
            brow = [FA.take(512, parts=1) for _ in range(2)]
            k = 0
            for l in range(n_layers):
                for nb in range(18):
                    b = k % 2
                    P.dma('sp', stg[b], I['w_mod'][l, :, nb * 512:(nb + 1) * 512].rearrange('(c p) n -> p c n', p=128), w=['stg%d' % b])
                    P.dma('pool', brow[b], I['b_mod'][l:l + 1, nb * 512:(nb + 1) * 512], w=['brow%d' % b])
                    pt = ps[k % 4]
                    for fc in range(4):
                        pairs = [(stg[b][:, c, 128 * fc:128 * fc + 128], cact[:, 2 * c:2 * c + 2]) for c in range(8)]
                        pairs.append((brow[b][0:1, 128 * fc:128 * fc + 128], onesF[0:1, 0:2]))
                        P.mm(pt[:, 2 * fc:2 * fc + 2], pairs, r=['stg%d' % b, 'brow%d' % b, 'cvT', 'onesF'], w=['ps%d' % (k % 4)])
                    g0 = l * 144 + nb * 8
                    P.op('dve', _mk('tensor_copy', MOD[:, g0:g0 + 8], pt[:, 0:8]), r=['ps%d' % (k % 4)], w=['MOD'])
                    k += 1
            for l in range(n_layers):
                for s in range(3):
                    src = MOD[:, l * 144 + (3 * s + 1) * 16:l * 144 + (3 * s + 2) * 16]
                    dst = SC1[:, (l * 3 + s) * 16:(l * 3 + s + 1) * 16]
                    P.op('dve', _mk('tensor_scalar', out=dst, in0=src, scalar1=1.0, scalar2=None, op0=ALU.add), r=['MOD'], w=['SC1'])
                    srcg = MOD[:, l * 144 + (3 * s + 2) * 16:l * 144 + (3 * s + 3) * 16]
                    dstg = GHT[:, (l * 3 + s) * 16:(l * 3 + s + 1) * 16]
                    fac = 1.0 if s == 1 else 0.5
                    P.op('dve', _mk('tensor_scalar', out=dstg, in0=srcg, scalar1=fac, scalar2=None, op0=ALU.mult), r=['MOD'], w=['GHT'])
            if 'MODD' in debug:
                P.dma('sp', MODD[:, :], MOD[:], r=['MOD'], w=['MODD'])
            P.barrier()

        def shift_ap(l, s, c, m):
            i = l * 144 + 3 * s * 16 + c * 2 + m
            return MOD[:, i:i + 1]

        def phase_in_transpose():
            FA.reset()
            xb = [FA.take(1024) for _ in range(4)]
            xt = FA.take(4096, 'p (c t) -> p c t', c=8)
            for t0, tw, m in TILES:
                nb = tw // 128
                for i in range(nb):
                    src = I['x_in'][t0 + 128 * i:t0 + 128 * i + 128, :] if m == 0 else I['ctx_in'][128 * i:128 * i + 128, :]
                    P.dma('sp', xb[i], src, w=['xb%d' % i])
                for c in range(8):
                    for i in range(nb):
                        P.op('pe', _mk('transpose', ps[c][:, 128 * i:128 * i + 128], xb[i][:, 128 * c:128 * c + 128], identF[:]), r=['xb%d' % i, 'ident'], w=['ps%d' % c])
                    if c % 2 == 0:
                        P.op('act', _mk('activation', out=xt[:, c, 0:tw], in_=ps[c][:, 0:tw], func=AF.Copy), r=['ps%d' % c], w=['xt%d' % c])
                    else:
                        P.op('dve', _mk('tensor_copy', xt[:, c, 0:tw], ps[c][:, 0:tw]), r=['ps%d' % c], w=['xt%d' % c])
                P.dma('pool', xT[:, t0:t0 + tw].rearrange('(c p) t -> p c t', p=128), xt[:, :, 0:tw], r=['xt%d' % c for c in range(8)], w=['xT%d' % t0])
            P.barrier()

        def phase_out_transpose():
            FA.reset()
            xt = [FA.take(4096, 'p (c t) -> p c t', c=8) for _ in range(2)]
            ob = [FA.take(1024) for _ in range(4)]
            for ti, (t0, tw, m) in enumerate(TILES):
                if m == 1:
                    continue
                b = ti % 2
                P.dma('sp', xt[b][:, :, 0:tw], xT[:, t0:t0 + tw].rearrange('(c p) t -> p c t', p=128), w=['xt%d' % b])
                for i in range(tw // 128):
                    for c in range(8):
                        pt = ps[(i * 8 + c) % 8]
                        P.op('pe', _mk('transpose', pt[:, 0:128], xt[b][:, c, 128 * i:128 * i + 128], identF[:]), r=['xt%d' % b, 'ident'], w=['ps%d' % ((i * 8 + c) % 8)])
                        if c % 2 == 0:
                            P.op('act', _mk('activation', out=ob[i][:, 128 * c:128 * c + 128], in_=pt[:, 0:128], func=AF.Copy), r=['ps%d' % ((i * 8 + c) % 8)], w=['ob%d_%d' % (i, c)])
                        else:
                            P.op('dve', _mk('tensor_copy', ob[i][:, 128 * c:128 * c + 128], pt[:, 0:128]), r=['ps%d' % ((i * 8 + c) % 8)], w=['ob%d_%d' % (i, c)])
                    P.dma('pool', out[t0 + 128 * i:t0 + 128 * i + 128, :], ob[i], r=['ob%d_%d' % (i, c) for c in range(8)], w=['out%d_%d' % (t0, i)])
            P.barrier()

        def phase_norm(l, s):
            FA.reset()
            BA.reset()
            xt = [FA.take(4096, 'p (c t) -> p c t', c=8) for _ in range(2)]
            tmp = [FA.take(512) for _ in range(2)]
            rstd = FA.take(512)
            sq = BA.take(4096, 'p (c t) -> p c t', c=8)
            h = [BA.take(4096, 'p (c t) -> p c t', c=8) for _ in range(2)]
            for ti, (t0, tw, m) in enumerate(TILES):
                b = ti % 2
                X = xt[b]
                P.dma('sp', X[:, :, 0:tw], xT[:, t0:t0 + tw].rearrange('(c p) t -> p c t', p=128), w=['xt%d' % b])
                P.op('act', _mk('activation', out=sq[:, :, 0:tw], in_=X[:, :, 0:tw], func=AF.Square), r=['xt%d' % b], w=['sq'])
                P.mm(ps[0][:, 0:tw], [(ones16[:], sq[:, c, 0:tw]) for c in range(8)], r=['sq', 'ones16'], w=['ps0'])
                P.op('act', _mk('activation', out=rstd[:, 0:tw], in_=ps[0][:, 0:tw], func=AF.Sqrt, bias=epsT[:, 0:1], scale=1.0 / D), r=['ps0', 'epsT'], w=['rstd'])
                P.op('dve', _mk('reciprocal', out=rstd[:, 0:tw], in_=rstd[:, 0:tw]), r=['rstd'], w=['rstd'])
                for c in range(8):
                    tb = tmp[c % 2]
                    P.op('dve', _mk('scalar_tensor_tensor', out=tb[:, 0:tw], in0=X[:, c, 0:tw], scalar=mod_ap(SC1, l, s, c, m), in1=rstd[:, 0:tw], op0=ALU.mult, op1=ALU.mult), r=['xt%d' % b, 'rstd', 'SC1'], w=['tmp%d' % (c % 2)])
                    P.op('act', _mk('activation', out=h[b][:, c, 0:tw], in_=tb[:, 0:tw], func=AF.Identity, bias=shift_ap(l, s, c, m), scale=1.0), r=['tmp%d' % (c % 2), 'MOD'], w=['h%d_%d' % (b, c)])
                P.dma('pool', hT[:, t0:t0 + tw].rearrange('(c p) t -> p c t', p=128), h[b][:, :, 0:tw], r=['h%d_%d' % (b, c) for c in range(8)], w=['hT%d' % t0])
            P.barrier()

        def residual_update(pt, ptok, X, xtok, j, tw, l, s, m):
            P.op('dve', _mk('scalar_tensor_tensor', out=X[:, j, 0:tw], in0=pt[:, 0:tw], scalar=mod_ap(GHT, l, s, j, m), in1=X[:, j, 0:tw], op0=ALU.mult, op1=ALU.add), r=[ptok, 'GHT'], w=[xtok + '_%d' % j])

        def phase_ffn(l, s, wi, wo):
            NH = 11
            for half in range(2):
                FA.reset()
                BA.reset()
                BW.reset()
                wg = BW.take(8 * 1408, 'p (c n) -> p c n', c=8)
                wu = BW.take(8 * 1408, 'p (c n) -> p c n', c=8)
                wob = BW.take(NH * 1024, 'p (c n) -> p c n', c=NH)
                stg = [FA.take(1408) for _ in range(2)]
                k = 0
                f0 = half * 1408
                for c in range(8):
                    for dst, col0 in ((wg, f0), (wu, DFF + f0)):
                        bb = k % 2
                        P.dma('sp', stg[bb], wi[l, 128 * c:128 * c + 128, col0:col0 + 1408], w=['stg%d' % bb])
                        P.op('pool', _mk('tensor_copy', dst[:, c, :], stg[bb][:, 0:1408]), r=['stg%d' % bb], w=['wres'])
                        k += 1
                for i in range(NH):
                    bb = k % 2
                    P.dma('sp', stg[bb][:, 0:1024], wo[l, f0 + 128 * i:f0 + 128 * i + 128, :], w=['stg%d' % bb])
                    P.op('pool', _mk('tensor_copy', wob[:, i, :], stg[bb][:, 0:1024]), r=['stg%d' % bb], w=['wres'])
                    k += 1
                xt = [FA.take(4096, 'p (c t) -> p c t', c=8) for _ in range(2)]
                sg = [FA.take(512) for _ in range(2)]
                ht = [BA.take(4096, 'p (c t) -> p c t', c=8) for _ in range(2)]
                act = BA.take(NH * 512, 'p (c t) -> p c t', c=NH)
                for ti, (t0, tw, m) in enumerate(TILES):
                    b = ti % 2
                    H = ht[b]
                    X = xt[b]
                    P.dma('sp', H[:, :, 0:tw], hT[:, t0:t0 + tw].rearrange('(c p) t -> p c t', p=128), w=['ht%d' % b])
                    P.dma('sp', X[:, :, 0:tw], xT[:, t0:t0 + tw].rearrange('(c p) t -> p c t', p=128), r=['xT%d' % t0], w=['xt%d_%d' % (b, j) for j in range(8)])
                    for i in range(NH):
                        pg = ps[2 * i % 4]
                        pu = ps[(2 * i + 1) % 4]
                        tg = 'ps%d' % (2 * i % 4)
                        tu = 'ps%d' % ((2 * i + 1) % 4)
                        P.mm(pg[:, 0:tw], [(wg[:, c, 128 * i:128 * i + 128], H[:, c, 0:tw]) for c in range(8)], r=['ht%d' % b, 'wres'], w=[tg])
                        P.mm(pu[:, 0:tw], [(wu[:, c, 128 * i:128 * i + 128], H[:, c, 0:tw]) for c in range(8)], r=['ht%d' % b, 'wres'], w=[tu])
                        sgb = sg[i % 2]
                        P.op('act', _mk('activation', out=sgb[:, 0:tw], in_=pg[:, 0:tw], func=AF.Silu), r=[tg], w=['sg%d' % (i % 2)])
                        P.op('dve', _mk('tensor_tensor', out=act[:, i, 0:tw], in0=sgb[:, 0:tw], in1=pu[:, 0:tw], op=ALU.mult), r=['sg%d' % (i % 2), tu], w=['act%d' % i])
                    for j in range(8):
                        py = ps[4 + j % 2]
                        ty = 'ps%d' % (4 + j % 2)
                        P.mm(py[:, 0:tw], [(wob[:, i, 128 * j:128 * j + 128], act[:, i, 0:tw]) for i in range(NH)], r=['act%d' % i for i in range(NH)] + ['wres'], w=[ty])
                        residual_update(py, ty, X, 'xt%d' % b, j, tw, l, s, m)
                    P.dma('pool', xT[:, t0:t0 + tw].rearrange('(c p) t -> p c t', p=128), X[:, :, 0:tw], r=['xt%d_%d' % (b, j) for j in range(8)], w=['xT%d' % t0])
                P.barrier()

        def phase_outproj(l):
            FA.reset()
            BA.reset()
            BW.reset()
            wo = BW.take(8 * 1024, 'p (c n) -> p c n', c=8)
            stg = [FA.take(1024) for _ in range(2)]
            for c in range(8):
                bb = c % 2
                P.dma('sp', stg[bb], I['w_out'][l, 128 * c:128 * c + 128, :], w=['stg%d' % bb])
                P.op('pool', _mk('tensor_copy', wo[:, c, :], stg[bb][:]), r=['stg%d' % bb], w=['wres'])
            xt = [FA.take(4096, 'p (c t) -> p c t', c=8) for _ in range(2)]
            mt = [BA.take(4096, 'p (c t) -> p c t', c=8) for _ in range(2)]
            for ti, (t0, tw, m) in enumerate(TILES):
                b = ti % 2
                M = mt[b]
                X = xt[b]
                P.dma('sp', M[:, :, 0:tw], mixT[:, t0:t0 + tw].rearrange('(c p) t -> p c t', p=128), w=['mt%d' % b])
                P.dma('sp', X[:, :, 0:tw], xT[:, t0:t0 + tw].rearrange('(c p) t -> p c t', p=128), w=['xt%d_%d' % (b, j) for j in range(8)])
                for j in range(8):
                    py = ps[j % 4]
                    ty = 'ps%d' % (j % 4)
                    P.mm(py[:, 0:tw], [(wo[:, c, 128 * j:128 * j + 128], M[:, c, 0:tw]) for c in range(8)], r=['mt%d' % b, 'wres'], w=[ty])
                    residual_update(py, ty, X, 'xt%d' % b, j, tw, l, 1, m)
                P.dma('pool', xT[:, t0:t0 + tw].rearrange('(c p) t -> p c t', p=128), X[:, :, 0:tw], r=['xt%d_%d' % (b, j) for j in range(8)], w=['xT%d' % t0])
            P.barrier()

        def rstd_from(pt, ptok, dst, dtok, tw, n, parts=128):
            P.op('act', _mk('activation', out=dst[0:parts, 0:tw], in_=pt[0:parts, 0:tw], func=AF.Sqrt, bias=epsT[0:parts, 0:1], scale=1.0 / n), r=[ptok, 'epsT'], w=[dtok])
            P.op('dve', _mk('reciprocal', out=dst[0:parts, 0:tw], in_=dst[0:parts, 0:tw]), r=[dtok], w=[dtok])

        def phase_inproj(l):
            FA.reset()
            BA.reset()
            BW.reset()
            win = BW.take(8 * INW, 'p (c n) -> p c n', c=8)
            wuq = BW.take(3 * 768, 'p (c n) -> p c n', c=3)
            wkp = BW.take(2 * 768, 'p (c n) -> p c n', c=2)
            wvv = BW.take(2 * 512, 'p (c n) -> p c n', c=2)
            stg = [FA.take(INW) for _ in range(2)]
            k = 0
            for c in range(8):
                bb = k % 2
                P.dma('sp', stg[bb], I['w_in'][l, 128 * c:128 * c + 128, :], w=['stg%d' % bb])
                P.op('pool', _mk('tensor_copy', win[:, c, :], stg[bb][:]), r=['stg%d' % bb], w=['wres'])
                k += 1
            for dst, name, ncc, n in ((wuq, 'w_uq', 3, 768), (wkp, 'wk_pad', 2, 768), (wvv, 'wv', 2, 512)):
                for c in range(ncc):
                    bb = k % 2
                    P.dma('sp', stg[bb][:, 0:n], I[name][l, 128 * c:128 * c + 128, :], w=['stg%d' % bb])
                    P.op('pool', _mk('tensor_copy', dst[:, c, :], stg[bb][:, 0:n]), r=['stg%d' % bb], w=['wres'])
                    k += 1
            P.barrier()
            FA.reset()
            LBB = FA.take(512)
            OMLB = FA.take(512)
            lbl = FA.take(2048)
            tmpA = FA.take(512)
            tmpB = FA.take(512)
            P.dma('sp', lbl, I['lblB'][:, :], w=['lbl'])
            Lr = lambda i: lbl[:, 512 * i:512 * i + 512]
            P.op('dve', _mk('tensor_tensor', out=tmpA, in0=Lr(0), in1=Lr(1), op=ALU.max), r=['lbl'], w=['tA'])
            P.op('dve', _mk('tensor_tensor', out=tmpA, in0=tmpA, in1=Lr(2), op=ALU.max), r=['tA'], w=['tA'])
            P.op('dve', _mk('tensor_tensor', out=tmpA, in0=tmpA, in1=Lr(3), op=ALU.max), r=['tA'], w=['tA'])
            for i in range(4):
                P.op('dve', _mk('tensor_tensor', out=Lr(i), in0=Lr(i), in1=tmpA, op=ALU.subtract), r=['tA', 'lbl'], w=['lbl'])
            P.op('act', _mk('activation', out=lbl, in_=lbl, func=AF.Exp), r=['lbl'], w=['lbl'])
            P.op('dve', _mk('tensor_tensor', out=tmpB, in0=Lr(0), in1=Lr(1), op=ALU.add), r=['lbl'], w=['tB'])
            P.op('dve', _mk('tensor_tensor', out=tmpB, in0=tmpB, in1=Lr(2), op=ALU.add), r=['tB'], w=['tB'])
            P.op('dve', _mk('tensor_tensor', out=tmpB, in0=tmpB, in1=Lr(3), op=ALU.add), r=['tB'], w=['tB'])
            P.op('dve', _mk('reciprocal', out=tmpB, in_=tmpB), r=['tB'], w=['tB'])
            if l == 0:
                P.op('dve', _mk('memset', LBB, 0.0), w=['LBB'])
            else:
                P.op('dve', _mk('tensor_copy', LBB, Lr(1)), r=['lbl'], w=['LBB'])
                for i in range(2, l + 1):
                    P.op('dve', _mk('tensor_tensor', out=LBB, in0=LBB, in1=Lr(i), op=ALU.add), r=['lbl', 'LBB'], w=['LBB'])
                P.op('dve', _mk('tensor_tensor', out=LBB, in0=LBB, in1=tmpB, op=ALU.mult), r=['tB', 'LBB'], w=['LBB'])
            P.op('dve', _mk('tensor_scalar', out=OMLB, in0=LBB, scalar1=-1.0, scalar2=1.0, op0=ALU.mult, op1=ALU.add), r=['LBB'], w=['OMLB'])
            P.barrier()
            FA.off = 1024
            if STOP == 0:
                P.barrier()
                return
            SL = [FA.take(512) for _ in range(12)]
            ST = ['S%d' % i for i in range(12)]
            cqF = FA.take(1536, 'p (c t) -> p c t', c=3)
            ckvF = FA.take(1024, 'p (c t) -> p c t', c=2)
            rs = FA.take(512)
            csT = FA.take(512, parts=96)
            snT = FA.take(512, parts=96)
            uF = FA.take(1024, 'p (c t) -> p c t', c=2)
            ebT = FA.take(64)
            ht = [BW.take(4096, 'p (c t) -> p c t', c=8) for _ in range(2)]
            kh16 = [BA.take(512) for _ in range(4)]
            vh16 = [BA.take(256) for _ in range(4)]
            qk16 = [BA.take(512) for _ in range(4)]
            g16 = [BA.take(512) for _ in range(2)]
            sq3 = BA.take(1536, 'p (c t) -> p c t', c=3)
            cqn = BA.take(1536, 'p (c t) -> p c t', c=3)
            ckvn = BA.take(1024, 'p (c t) -> p c t', c=2)
            kpe16 = BA.take(512, parts=32)
            sqh = [BA.take(512) for _ in range(4)]
            o16 = [BA.take(512) for _ in range(4)]
            v16 = [BA.take(520, 'p (h e) -> p h e', h=8) for _ in range(2)]
            for b in range(2):
                P.op('pool', _mk('memset', v16[b][:, :, 64:65], 1.0), w=['v16_%d' % b])
            gq = lambda c: gains[:, 4 + l * 3 + c:5 + l * 3 + c]
            gkv = lambda c: gains[:, 16 + l * 2 + c:17 + l * 2 + c]
            cnt = {'ps': 0}

            def nps():
                i = cnt['ps'] % 8
                cnt['ps'] += 1
                return (ps[i], 'ps%d' % i)

            def nps4x():
                i = cnt['ps'] % 4
                cnt['ps'] += 1
                return (ps[i], 'ps%d' % i)

            def featmm(col0, ncols, H, b, tw):
                pt, tok = nps()
                P.mm(pt[0:ncols, 0:tw], [(win[:, c, col0:col0 + ncols], H[:, c, 0:tw]) for c in range(8)], r=['ht%d' % b, 'wres'], w=[tok])
                return (pt, tok)

            def heads_block(mmfn, gcol, rope, dst_dram, t0, tw):
                gsc = gains[0:96, gcol:gcol + 1]
                for h0 in (0, 4):
                    J = range(4)
                    for j in J:
                        mmfn(h0 + j, ps[j], 'ps%d' % j)
                    for j in J:
                        P.op('act', _mk('activation', out=sqh[j][0:96, 0:tw], in_=ps[j][0:96, 0:tw], func=AF.Square), r=['ps%d' % j], w=['sqh%d' % j])
                    for j in J:
                        P.mm(ps[4 + j][0:96, 0:tw], [(ones16[0:96, 0:96], sqh[j][0:96, 0:tw])], r=['sqh%d' % j, 'ones16'], w=['ps%d' % (4 + j)])
                    for j in J:
                        P.op('act', _mk('activation', out=SL[3 * j][0:96, 0:tw], in_=ps[4 + j][0:96, 0:tw], func=AF.Sqrt, bias=epsT[0:96, 0:1], scale=1.0 / 96.0), r=['ps%d' % (4 + j), 'epsT'], w=[ST[3 * j]])
                    for j in J:
                        P.op('dve', _mk('reciprocal', out=SL[3 * j][0:96, 0:tw], in_=SL[3 * j][0:96, 0:tw]), r=[ST[3 * j]], w=[ST[3 * j]])
                    for j in J:
                        P.op('dve', _mk('scalar_tensor_tensor', out=SL[3 * j + 1][0:96, 0:tw], in0=ps[j][0:96, 0:tw], scalar=gsc, in1=SL[3 * j][0:96, 0:tw], op0=ALU.mult, op1=ALU.mult), r=['ps%d' % j, ST[3 * j], 'gains'], w=[ST[3 * j + 1]])
                    if rope:
                        for j in J:
                            P.mm(ps[4 + j][0:96, 0:tw], [(rotF[:, :], SL[3 * j + 1][0:96, 0:tw])], r=[ST[3 * j + 1], 'rot'], w=['ps%d' % (4 + j)])
                        for j in J:
                            P.op('dve', _mk('tensor_tensor', out=SL[3 * j + 2][0:96, 0:tw], in0=SL[3 * j + 1][0:96, 0:tw], in1=csT[0:96, 0:tw], op=ALU.mult), r=[ST[3 * j + 1], 'cs'], w=[ST[3 * j + 2]])
                        for j in J:
                            P.op('dve', _mk('tensor_tensor', out=SL[3 * j + 1][0:96, 0:tw], in0=ps[4 + j][0:96, 0:tw], in1=snT[0:96, 0:tw], op=ALU.mult), r=['ps%d' % (4 + j), 'sn'], w=[ST[3 * j + 1]])
                        for j in J:
                            P.op('dve', _mk('tensor_tensor', out=o16[j][0:96, 0:tw], in0=SL[3 * j + 2][0:96, 0:tw], in1=SL[3 * j + 1][0:96, 0:tw], op=ALU.add), r=[ST[3 * j + 2], ST[3 * j + 1]], w=['o16_%d' % j])
                    else:
                        for j in J:
                            P.op('act', _mk('activation', out=o16[j][0:96, 0:tw], in_=SL[3 * j + 1][0:96, 0:tw], func=AF.Copy), r=[ST[3 * j + 1]], w=['o16_%d' % j])
                    for j in J:
                        P.dma('sp', dst_dram[h0 + j, :, t0:t0 + tw], o16[j][0:96, 0:tw], r=['o16_%d' % j], w=['hd'])

            for ti, (t0, tw, m) in enumerate(TILES):
                b = ti % 2
                H = ht[b]
                if ti == 0:
                    P.dma('sp', H[:, :, 0:tw], hT[:, t0:t0 + tw].rearrange('(c p) t -> p c t', p=128), w=['ht%d' % b])
                if ti + 1 < len(TILES):
                    nt0, ntw, _nm = TILES[ti + 1]
                    P.dma('sp', ht[1 - b][:, :, 0:ntw], hT[:, nt0:nt0 + ntw].rearrange('(c p) t -> p c t', p=128), w=['ht%d' % (1 - b)])
                rope = m == 0
                if rope:
                    P.dma('sp', csT[:, 0:tw], I['cosT'][:, t0:t0 + tw], w=['cs'])
                    P.dma('sp', snT[:, 0:tw], I['sinT'][:, t0:t0 + tw], w=['sn'])
                NB = range(tw // 128)
                Fs = lambda i: SL[3 * i]
                Ks = lambda i: SL[3 * i + 1]
                Es = lambda i: SL[3 * i + 2]
                tF = lambda i: ST[3 * i]
                tK = lambda i: ST[3 * i + 1]
                tE = lambda i: ST[3 * i + 2]
                for i in NB:
                    P.mm(ps[i][:, 0:512], [(H[:, c, 128 * i:128 * i + 128], win[:, c, 256:768]) for c in range(8)], r=['ht%d' % b, 'wres'], w=['ps%d' % i])
                for i in NB:
                    P.op('act', _mk('activation', out=Fs(i), in_=ps[i][:, 0:512], func=AF.Sigmoid), r=['ps%d' % i], w=[tF(i)])
                for i in NB:
                    P.op('dve', _mk('tensor_tensor', out=Fs(i), in0=Fs(i), in1=OMLB, op=ALU.mult), r=[tF(i), 'OMLB'], w=[tF(i)])
                    P.op('dve', _mk('tensor_tensor', out=Fs(i), in0=Fs(i), in1=LBB, op=ALU.add), r=[tF(i), 'LBB'], w=[tF(i)])
                    P.op('dve', _mk('tensor_scalar', out=Ks(i), in0=Fs(i), scalar1=-1.0, scalar2=1.0, op0=ALU.mult, op1=ALU.add), r=[tF(i)], w=[tK(i)])
                    P.op('dve', _mk('tensor_scalar', out=Fs(i), in0=Fs(i), scalar1=1e-06, scalar2=1.0, op0=ALU.max, op1=ALU.min), r=[tF(i)], w=[tF(i)])
                for i in NB:
                    P.op('act', _mk('activation', out=Fs(i), in_=Fs(i), func=AF.Ln), r=[tF(i)], w=[tF(i)])
                for i in NB:
                    lf = Fs(i)
                    P.mm(ps[i][:, 0:256], [(d1f[:], lf[:, 0:256])], r=[tF(i), 'd1f'], w=['ps%d' % i])
                    P.mm(ps[i][:, 256:512], [(d1b[:], lf[:, 256:512])], r=[tF(i), 'd1b'], w=['ps%d' % i])
                    for g in range(4):
                        dd, pr = (g // 2, g % 2)
                        P.mm(ps[4 + g][:, 128 * i:128 * i + 128], [(lf[:, dd * 256 + pr * 128:dd * 256 + pr * 128 + 128], (trf if dd == 0 else trb)[:])], r=[tF(i), 'trf', 'trb'], w=['ps%d' % (4 + g)])
                for i in NB:
                    P.op('act', _mk('activation', out=Es(i), in_=ps[i][:, 0:512], func=AF.Exp), r=['ps%d' % i], w=[tE(i)])
                for i in NB:
                    P.op('dve', _mk('tensor_tensor', out=kh16[i], in0=Ks(i), in1=Es(i), op=ALU.mult), r=[tK(i), tE(i)], w=['kh16_%d' % i])
                    P.dma('sp', KH[t0 + 128 * i:t0 + 128 * i + 128, :], kh16[i], r=['kh16_%d' % i], w=['KHd'])
                for i in NB:
                    P.mm(ps[i][:, 0:256], [(H[:, c, 128 * i:128 * i + 128], win[:, c, 768:1024]) for c in range(8)], r=['ht%d' % b, 'wres'], w=['ps%d' % i])
                for i in NB:
                    P.op('act', _mk('activation', out=vh16[i], in_=ps[i][:, 0:256], func=AF.Copy), r=['ps%d' % i], w=['vh16_%d' % i])
                    P.dma('sp', VH[t0 + 128 * i:t0 + 128 * i + 128, :], vh16[i], r=['vh16_%d' % i], w=['VHd'])
                cnt['ps'] = 0
                for g in range(4):
                    dd, pr = (g // 2, g % 2)
                    pb_, tb_ = (ps[4 + g], 'ps%d' % (4 + g))
                    s0 = 3 * (g % 3)
                    eb, enb, kkf = (SL[s0], SL[s0 + 1], SL[s0 + 2])
                    teb, tenb, tkkf = (ST[s0], ST[s0 + 1], ST[s0 + 2])
                    qs, tqs = (SL[9 + pr], ST[9 + pr])
                    P.op('act', _mk('activation', out=eb[:, 0:tw], in_=pb_[:, 0:tw], func=AF.Exp), r=[tb_], w=[teb])
                    P.op('act', _mk('activation', out=enb[:, 0:tw], in_=pb_[:, 0:tw], func=AF.Exp, scale=-1.0), r=[tb_], w=[tenb])
                    if dd == 0:
                        pq, tq = nps4x()
                        P.mm(pq[:, 0:tw], [(win[:, c, 128 * pr:128 * pr + 128], H[:, c, 0:tw]) for c in range(8)], r=['ht%d' % b, 'wres'], w=[tq])
                        P.op('act', _mk('activation', out=qs[:, 0:tw], in_=pq[:, 0:tw], func=AF.Silu), r=[tq], w=[tqs])
                    qb, tqb = (qk16[2 * (g % 2)], 'qk16_%d' % (2 * (g % 2)))
                    P.op('dve', _mk('tensor_tensor', out=qb[:, 0:tw], in0=qs[:, 0:tw], in1=eb[:, 0:tw], op=ALU.mult), r=[tqs, teb], w=[tqb])
                    P.dma('sp', HQ[dd, 128 * pr:128 * pr + 128, t0:t0 + tw], qb[:, 0:tw], r=[tqb], w=['HQd'])
                    pk, tkk = nps4x()
                    c0_ = 256 + dd * 256 + 128 * pr
                    P.mm(pk[:, 0:tw], [(win[:, c, c0_:c0_ + 128], H[:, c, 0:tw]) for c in range(8)], r=['ht%d' % b, 'wres'], w=[tkk])
                    P.op('act', _mk('activation', out=kkf[:, 0:tw], in_=pk[:, 0:tw], func=AF.Sigmoid, scale=-1.0), r=[tkk], w=[tkkf])
                    kb, tkb = (qk16[2 * (g % 2) + 1], 'qk16_%d' % (2 * (g % 2) + 1))
                    gi = l * 4 + dd * 2 + pr
                    P.op('dve', _mk('scalar_tensor_tensor', out=kb[:, 0:tw], in0=kkf[:, 0:tw], scalar=omlF[:, gi:gi + 1], in1=enb[:, 0:tw], op0=ALU.mult, op1=ALU.mult), r=[tkkf, tenb, 'omlF'], w=[tkb])
                    P.dma('sp', HK[dd, 128 * pr:128 * pr + 128, t0:t0 + tw], kb[:, 0:tw], r=[tkb], w=['HKd'])
                    nch = tw // CH
                    ebv = eb[:, 0:tw].rearrange('p (n s) -> p n s', s=CH)
                    sel = ebv[:, :, CH - 1:CH] if dd == 0 else ebv[:, :, 0:1]
                    P.op('dve', _mk('tensor_copy', ebT[:, 16 * g:16 * g + nch].rearrange('p (n o) -> p n o', o=1), sel), r=[teb], w=['ebT%d' % g])
                    P.dma('sp', EBE[dd, pr, :, t0 // CH:t0 // CH + nch], ebT[:, 16 * g:16 * g + nch], r=['ebT%d' % g], w=['EBEd'])
                cnt['ps'] = 0
                for pr in range(2):
                    pg, tg = featmm(1024 + 128 * pr, 128, H, b, tw)
                    P.op('act', _mk('activation', out=g16[pr][:, 0:tw], in_=pg[:, 0:tw], func=AF.Silu), r=[tg], w=['g16_%d' % pr])
                    P.dma('sp', GT[128 * pr:128 * pr + 128, t0:t0 + tw], g16[pr][:, 0:tw], r=['g16_%d' % pr], w=['GTd'])
                for ch in range(2):
                    pu, tu = featmm(1952 + 128 * ch, 128, H, b, tw)
                    P.op('dve', _mk('tensor_copy', uF[:, ch, 0:tw], pu[:, 0:tw]), r=[tu], w=['uF%d' % ch])
                P.dma('sp', uT[:, t0:t0 + tw].rearrange('(c p) t -> p c t', p=128), uF[:, :, 0:tw], r=['uF0', 'uF1'], w=['uTd'])
                pcs = []
                for c in range(3):
                    pcs.append(featmm(1280 + 128 * c, 128, H, b, tw))
                for c in range(3):
                    pc, tc_ = pcs[c]
                    P.op('act', _mk('activation', out=sq3[:, c, 0:tw], in_=pc[:, 0:tw], func=AF.Square), r=[tc_], w=['sq3_%d' % c])
                for c in range(3):
                    pc, tc_ = pcs[c]
                    P.op('dve', _mk('tensor_copy', cqF[:, c, 0:tw], pc[:, 0:tw]), r=[tc_], w=['cqF%d' % c])
                pss, tss = nps()
                P.mm(pss[:, 0:tw], [(ones16[:], sq3[:, c, 0:tw]) for c in range(3)], r=['sq3_0', 'sq3_1', 'sq3_2', 'ones16'], w=[tss])
                rstd_from(pss, tss, rs, 'rs', tw, 384.0)
                for c in range(3):
                    P.op('dve', _mk('scalar_tensor_tensor', out=cqn[:, c, 0:tw], in0=cqF[:, c, 0:tw], scalar=gq(c), in1=rs[:, 0:tw], op0=ALU.mult, op1=ALU.mult), r=['cqF%d' % c, 'rs', 'gains'], w=['cqn%d' % c])

                def qmm(h, pt, tok, tw=tw):
                    P.mm(pt[0:96, 0:tw], [(wuq[:, c, 96 * h:96 * h + 96], cqn[:, c, 0:tw]) for c in range(3)], r=['cqn0', 'cqn1', 'cqn2', 'wres'], w=[tok])
                heads_block(qmm, 32 + l, rope, QT, t0, tw)
                pcs = []
                for c in range(2):
                    pcs.append(featmm(1664 + 128 * c, 128, H, b, tw))
                for c in range(2):
                    pc, tc_ = pcs[c]
                    P.op('act', _mk('activation', out=sq3[:, c, 0:tw], in_=pc[:, 0:tw], func=AF.Square), r=[tc_], w=['sq3_%d' % c])
                for c in range(2):
                    pc, tc_ = pcs[c]
                    P.op('dve', _mk('tensor_copy', ckvF[:, c, 0:tw], pc[:, 0:tw]), r=[tc_], w=['ckvF%d' % c])
                pss, tss = nps()
                P.mm(pss[:, 0:tw], [(ones16[:], sq3[:, c, 0:tw]) for c in range(2)], r=['sq3_0', 'sq3_1', 'ones16'], w=[tss])
                rstd_from(pss, tss, rs, 'rs', tw, 256.0)
                for c in range(2):
                    P.op('dve', _mk('scalar_tensor_tensor', out=ckvn[:, c, 0:tw], in0=ckvF[:, c, 0:tw], scalar=gkv(c), in1=rs[:, 0:tw], op0=ALU.mult, op1=ALU.mult), r=['ckvF%d' % c, 'rs', 'gains'], w=['ckvn%d' % c])
                pkp, tkp = featmm(1920, 32, H, b, tw)
                P.op('act', _mk('activation', out=kpe16[0:32, 0:tw], in_=pkp[0:32, 0:tw], func=AF.Copy), r=[tkp], w=['kpe16'])

                def kmm(h, pt, tok, tw=tw):
                    P.mm(pt[0:96, 0:tw], [(wkp[:, c, 96 * h:96 * h + 96], ckvn[:, c, 0:tw]) for c in range(2)] + [(sh16[:, :], kpe16[0:32, 0:tw])], r=['ckvn0', 'ckvn1', 'kpe16', 'sh16', 'wres'], w=[tok])
                heads_block(kmm, 36 + l, rope, KT, t0, tw)
                for i in range(tw // 128):
                    bb = i % 2
                    pv, tv = nps()
                    P.mm(pv[:, 0:512], [(ckvn[:, c, 128 * i:128 * i + 128], wvv[:, c, :]) for c in range(2)], r=['ckvn0', 'ckvn1', 'wres'], w=[tv])
                    P.op('act', _mk('activation', out=v16[bb][:, :, 0:64], in_=pv[:, 0:512].rearrange('p (h e) -> p h e', h=8), func=AF.Copy), r=[tv], w=['v16_%d' % bb])
                    P.dma('sp', VV[t0 + 128 * i:t0 + 128 * i + 128, :, :], v16[bb], r=['v16_%d' % bb], w=['VVd'])
            P.barrier()

        def phase_hgrn(l):
            FA.reset()
            BA.reset()
            BW.reset()
            chains = [(dd, pr) for dd in range(2) for pr in range(2)]
            qt = {}
            kt = {}
            kh = {}
            vh = {}
            eb = {}
            osb = {}
            a16 = {}
            for ci, ch in enumerate(chains):
                qt[ch] = BW.take(512)
                kt[ch] = BW.take(512)
                kh[ch] = BW.take(2048, 'p (n c) -> p n c', n=16, parts=32)
                vh[ch] = BW.take(2048, 'p (n c) -> p n c', n=16, parts=32)
                eb[ch] = FA.take(16)
                osb[ch] = FA.take(512)
                a16[ch] = BA.take(64, 'p (h c) -> p h c', h=2, parts=32)
            P.op('dve', _mk('memset', S32[:], 0.0), w=['S32_%d' % i for i in range(4)])
            P.op('pool', _mk('memset', S16[:], 0.0), w=['S16_%d' % i for i in range(4)])
            order = {0: [TILES[16]] + TILES[0:16], 1: [TILES[16]] + TILES[15::-1]}
            for step in range(17):
                for ci, ch in enumerate(chains):
                    dd, pr = ch
                    t0, tw, m = order[dd][step]
                    nch = tw // CH
                    c0 = 'c%d' % ci
                    P.dma('sp', qt[ch][:, 0:tw], HQ[dd, 128 * pr:128 * pr + 128, t0:t0 + tw], w=[c0 + 'q'])
                    P.dma('sp', kt[ch][:, 0:tw], HK[dd, 128 * pr:128 * pr + 128, t0:t0 + tw], w=[c0 + 'k'])
                    P.dma('sp', kh[ch][:, 0:nch, :], KH[t0:t0 + tw, dd * 256 + pr * 128:dd * 256 + pr * 128 + 128].rearrange('(n s) c -> s n c', s=CH), w=[c0 + 'kh'])
                    P.dma('sp', vh[ch][:, 0:nch, :], VH[t0:t0 + tw, pr * 128:pr * 128 + 128].rearrange('(n s) c -> s n c', s=CH), w=[c0 + 'vh'])
                    P.dma('sp', eb[ch][:, 0:nch], EBE[dd, pr, :, t0 // CH:t0 // CH + nch], w=[c0 + 'eb'])
                nchs = order[0][step][1] // CH
                for cidx in range(nchs):
                    info = []
                    for ci, ch in enumerate(chains):
                        dd, pr = ch
                        t0, tw, m = order[dd][step]
                        nch = tw // CH
                        cc = cidx if dd == 0 else nch - 1 - cidx
                        info.append(dict(ci=ci, ch=ch, dd=dd, c0='c%d' % ci, bank=ps[ci], btok='ps%d' % ci, obank=ps[4 + ci], otok='ps%d' % (4 + ci),
                                         msk=maskf if dd == 0 else maskb, s32=S32[:, 64 * ci:64 * ci + 64], s16=S16[:, 64 * ci:64 * ci + 64],
                                         cc=cc, sl=slice(CH * cc, CH * cc + CH)))
                    first = cidx == 0
                    for d in info:
                        ch, cc, sl, c0, bank, btok = (d['ch'], d['cc'], d['sl'], d['c0'], d['bank'], d['btok'])
                        for hh in range(2):
                            pb = 64 * hh
                            P.mm(bank[pb:pb + 64, 0:64], [(kh[ch][0:32, cc, pb:pb + 64], vh[ch][0:32, cc, pb:pb + 64])], r=[c0 + 'kh', c0 + 'vh'], w=[btok + 'U'])
                            P.mm(bank[0:32, 64 + 32 * hh:96 + 32 * hh], [(kt[ch][pb:pb + 64, sl], qt[ch][pb:pb + 64, sl])], r=[c0 + 'k', c0 + 'q'], w=[btok + 'A%d' % hh])
                    for d in info:
                        ch, c0, bank, btok, msk = (d['ch'], d['c0'], d['bank'], d['btok'], d['msk'])
                        for hh in range(2):
                            P.op('dve', _mk('tensor_tensor', out=a16[ch][0:32, hh, :], in0=bank[0:32, 64 + 32 * hh:96 + 32 * hh], in1=msk[:, :], op=ALU.mult), r=[btok + 'A%d' % hh, 'maskf', 'maskb'], w=[c0 + 'a%d' % hh])
                    for d in info:
                        ch, cc, sl, c0, ci, obank, otok, s16 = (d['ch'], d['cc'], d['sl'], d['c0'], d['ci'], d['obank'], d['otok'], d['s16'])
                        for hh in range(2):
                            pb = 64 * hh
                            P.mm(obank[pb:pb + 64, sl], [(s16[pb:pb + 64, :], qt[ch][pb:pb + 64, sl]), (vh[ch][0:32, cc, pb:pb + 64], a16[ch][0:32, hh, :])], r=['S16_%d' % ci, c0 + 'q', c0 + 'vh', c0 + 'a%d' % hh], w=[otok] if first else [otok + 'x'])
                    for d in info:
                        ch, cc, c0, ci, bank, btok, s32 = (d['ch'], d['cc'], d['c0'], d['ci'], d['bank'], d['btok'], d['s32'])
                        P.op('dve', _mk('scalar_tensor_tensor', out=s32, in0=s32, scalar=eb[ch][:, cc:cc + 1], in1=bank[:, 0:64], op0=ALU.mult, op1=ALU.add), r=[btok + 'U', c0 + 'eb', 'S32_%d' % ci], w=['S32_%d' % ci])
                    for d in info:
                        ci, s32, s16 = (d['ci'], d['s32'], d['s16'])
                        P.op('act', _mk('activation', out=s16, in_=s32, func=AF.Copy), r=['S32_%d' % ci], w=['S16_%d' % ci])
                for ci, ch in enumerate(chains):
                    dd, pr = ch
                    t0, tw, m = order[dd][step]
                    c0 = 'c%d' % ci
                    obank = ps[4 + ci]
                    otok = 'ps%d' % (4 + ci)
                    P.op('act', _mk('activation', out=osb[ch][:, 0:tw], in_=obank[:, 0:tw], func=AF.Copy), r=[otok, otok + 'x'], w=[c0 + 'o'])
                    P.dma('pool', OFB[dd, 128 * pr:128 * pr + 128, t0:t0 + tw], osb[ch][:, 0:tw], r=[c0 + 'o'], w=['OFBd'])
            P.barrier()
            FA.reset()
            BA.reset()
            of = [FA.take(512) for _ in range(2)]
            obb = [FA.take(512) for _ in range(2)]
            rs = FA.take(512)
            gt = [BA.take(512) for _ in range(2)]
            sq = BA.take(512)
            mo = [BA.take(512) for _ in range(2)]
            k = 0
            for t0, tw, m in TILES:
                for pr in range(2):
                    b = k % 2
                    k += 1
                    P.dma('sp', of[b][:, 0:tw], OFB[0, 128 * pr:128 * pr + 128, t0:t0 + tw], w=['of%d' % b])
                    P.dma('sp', obb[b][:, 0:tw], OFB[1, 128 * pr:128 * pr + 128, t0:t0 + tw], w=['ob%d' % b])
                    P.dma('sp', gt[b][:, 0:tw], GT[128 * pr:128 * pr + 128, t0:t0 + tw], w=['gt%d' % b])
                    P.op('dve', _mk('tensor_tensor', out=of[b][:, 0:tw], in0=of[b][:, 0:tw], in1=obb[b][:, 0:tw], op=ALU.add), r=['ob%d' % b], w=['of%d' % b])
                    P.op('act', _mk('activation', out=sq[:, 0:tw], in_=of[b][:, 0:tw], func=AF.Square), r=['of%d' % b], w=['sq'])
                    pt, tok = (ps[k % 4], 'ps%d' % (k % 4))
                    P.mm(pt[:, 0:tw], [(bd16[:], sq[:, 0:tw])], r=['sq', 'bd16'], w=[tok])
                    rstd_from(pt, tok, rs, 'rs', tw, 64.0)
                    P.op('dve', _mk('scalar_tensor_tensor', out=of[b][:, 0:tw], in0=of[b][:, 0:tw], scalar=gains[:, l:l + 1], in1=rs[:, 0:tw], op0=ALU.mult, op1=ALU.mult), r=['rs', 'gains'], w=['of%d' % b])
                    P.op('dve', _mk('tensor_tensor', out=mo[b][:, 0:tw], in0=of[b][:, 0:tw], in1=gt[b][:, 0:tw], op=ALU.mult), r=['of%d' % b, 'gt%d' % b], w=['mo%d' % b])
                    P.dma('pool', mixT[128 * pr:128 * pr + 128, t0:t0 + tw], mo[b][:, 0:tw], r=['mo%d' % b], w=['mixd'])
            P.barrier()

        def phase_attn(l, with_ctx):
            FA.reset()
            BA.reset()
            BW.reset()
            ktb = [BW.take(T, parts=96) for _ in range(2)]
            vtb = [BW.take(66 * 65, 'p (k e) -> p k e', e=65) for _ in range(2)]
            qtb = [BA.take(512, parts=96) for _ in range(2)]
            pb16 = [BA.take(512) for _ in range(4)]
            mo = [BA.take(512, parts=64) for _ in range(2)]
            osb = [FA.take(512, parts=64) for _ in range(2)]
            rden = FA.take(512)
            scale = 96.0 ** (-0.5)
            jobs = []
            for h in range(8):
                qtiles = [(t0, tw, list(range(66))) for t0, tw, m in TILES if m == 0]
                if with_ctx:
                    qtiles.append((TL, TC, [64, 65]))
                for t0, tw, kcs in qtiles:
                    jobs.append(dict(h=h, hb=h % 2, t0=t0, tw=tw, kcs=kcs, qb=len(jobs) % 2, first_of_head=(t0 == 0)))
            items = []
            for ji, jb in enumerate(jobs):
                for n, kc in enumerate(jb['kcs']):
                    items.append((ji, n, kc))

            def load_head(h):
                hb = h % 2
                P.dma('sp', ktb[hb][:, :], KT[h, :, :], w=['kt%d' % hb])
                P.dma('sp', vtb[hb][:, :, :], VV[:, h, :].rearrange('(k p) e -> p k e', p=128), w=['vt%d' % hb])

            def load_q(ji):
                jb = jobs[ji]
                P.dma('sp', qtb[jb['qb']][:, 0:jb['tw']], QT[jb['h'], :, jb['t0']:jb['t0'] + jb['tw']], w=['q%d' % jb['qb']])

            pbp = [BA.take(1024, 'p (k t) -> p k t', k=2) for _ in range(3)]
            npair = len(items) // 2

            def emit_qk(pi_):
                for half in range(2):
                    ji, n, kc = items[2 * pi_ + half]
                    jb = jobs[ji]
                    tw, hb, qb = (jb['tw'], jb['hb'], jb['qb'])
                    if n == 0 and ji + 1 < len(jobs):
                        load_q(ji + 1)
                    bank = 2 * (pi_ % 3) + half
                    P.mm(ps[bank][:, 0:tw], [(ktb[hb][:, 128 * kc:128 * kc + 128], qtb[qb][:, 0:tw])], r=['kt%d' % hb, 'q%d' % qb], w=['ps%d' % bank])

            def emit_pv(pi_):
                ji, n0, kc0 = items[2 * pi_]
                jb = jobs[ji]
                tw, hb, qb, h, t0 = (jb['tw'], jb['hb'], jb['qb'], jb['h'], jb['t0'])
                pb_ = pi_ % 3
                pbuf = pbp[pb_]
                tpb = 'pbp%d' % pb_
                po = ps[6 + qb]
                tpo = 'ps%d' % (6 + qb)
                sview = psb[pb_][:, :].rearrange('p (k t) -> p k t', k=2)
                P.op('act', _mk('activation', out=pbuf[:, :, 0:tw], in_=sview[:, :, 0:tw], func=AF.Exp, scale=scale), r=['ps%d' % (2 * pb_), 'ps%d' % (2 * pb_ + 1)], w=[tpb])
                for half in range(2):
                    ji2, n, kc = items[2 * pi_ + half]
                    assert ji2 == ji
                    first = n == 0
                    last = n == len(jb['kcs']) - 1
                    P.op('pe', _mk('matmul', po[0:65, 0:tw], vtb[hb][:, kc, :], pbuf[:, half, 0:tw], start=first, stop=last), r=[tpb, 'vt%d' % hb], w=[tpo] if first else [tpo + 'x'], signal=True)
                    if first and jb['first_of_head'] and h + 1 < 8:
                        load_head(h + 1)
                    if last:
                        P.op('dve', _mk('reciprocal', out=rden[64:65, 0:tw], in_=po[64:65, 0:tw]), r=[tpo, tpo + 'x'], w=['rden'])
                        P.op('act', _mk('activation', out=osb[qb][0:64, 0:tw], in_=po[0:64, 0:tw], func=AF.Copy), r=[tpo, tpo + 'x'], w=['osb%d' % qb])
                        P.mm(po[0:64, 0:tw], [(onesF[64:65, 0:64], rden[64:65, 0:tw])], r=['rden', 'onesF', 'osb%d' % qb], w=[tpo, tpo + 'x'])
                        P.op('dve', _mk('tensor_tensor', out=mo[qb][0:64, 0:tw], in0=osb[qb][0:64, 0:tw], in1=po[0:64, 0:tw], op=ALU.mult), r=['osb%d' % qb, tpo], w=['mo%d' % qb])
                        P.dma('pool', mixT[256 + 64 * h:256 + 64 * h + 64, t0:t0 + tw], mo[qb][0:64, 0:tw], r=['mo%d' % qb], w=['mixd'])
            load_head(0)
            load_q(0)
            LA = 2
            for i in range(npair + LA):
                if i < npair:
                    emit_qk(i)
                if i >= LA:
                    emit_pv(i - LA)
            P.barrier()

        def phase_pool(l, with_ctx):
            FA.reset()
            BA.reset()
            BW.reset()
            pw = BW.take(256, 'p (c n) -> p c n', c=2)
            stg = FA.take(256, 'p (c n) -> p c n', c=2)
            for ch in range(2):
                P.dma('sp', stg[:, ch, :], I['pwb'][l, ch, :, :], w=['stg'])
            P.op('pool', _mk('tensor_copy', pw, stg), r=['stg'], w=['wres'])
            W = 512 + 16
            ub = [FA.take(2 * W, 'p (c t) -> p c t', c=2) for _ in range(2)]
            a1 = FA.take(2 * W, 'p (c t) -> p c t', c=2)
            a2 = FA.take(2 * W, 'p (c t) -> p c t', c=2)
            a3 = FA.take(W)
            a4 = FA.take(W)
            sm = FA.take(1024, 'p (c t) -> p c t', c=2)
            pl16 = BA.take(1024, 'p (c t) -> p c t', c=2)
            mo = [BA.take(512) for _ in range(4)]
            k = 0
            for ti, (t0, tw, m) in enumerate(TILES):
                if m == 1 and (not with_ctx):
                    continue
                b = ti % 2
                U = ub[b]
                seq0, seq1 = (0, TL) if m == 0 else (TL, T)
                lo = max(t0 - 8, seq0)
                hi = min(t0 + tw + 8, seq1)
                if lo > t0 - 8 or hi < t0 + tw + 8:
                    P.op('pool', _mk('memset', U[:, :, :], 0.0), w=['ub%d' % b])
                P.dma('sp', U[:, :, lo - (t0 - 8):hi - (t0 - 8)], uT[:, lo:hi].rearrange('(c p) t -> p c t', p=128), w=['ub%d' % b])
                n1 = tw + 15
                P.op('dve', _mk('tensor_tensor', out=a1[:, :, 0:n1], in0=U[:, :, 0:n1], in1=U[:, :, 1:n1 + 1], op=ALU.add), r=['ub%d' % b], w=['a1'])
                n2 = tw + 13
                P.op('dve', _mk('tensor_tensor', out=a2[:, :, 0:n2], in0=a1[:, :, 0:n2], in1=a1[:, :, 2:n2 + 2], op=ALU.add), r=['a1'], w=['a2'])
                n3 = tw + 9
                P.op('dve', _mk('tensor_tensor', out=a3[:, 0:n3], in0=a2[:, 1, 0:n3], in1=a2[:, 1, 4:n3 + 4], op=ALU.add), r=['a2'], w=['a3'])
                n4 = tw + 1
                P.op('dve', _mk('tensor_tensor', out=a4[:, 0:n4], in0=a3[:, 0:n4], in1=a3[:, 8:n4 + 8], op=ALU.add), r=['a3'], w=['a4'])
                P.op('dve', _mk('tensor_scalar', out=sm[0:64, 0, 0:tw], in0=a1[0:64, 0, 7:7 + tw], scalar1=0.5, scalar2=None, op0=ALU.mult), r=['a1'], w=['sm0a'])
                P.op('dve', _mk('tensor_scalar', out=sm[64:128, 0, 0:tw], in0=a2[64:128, 0, 6:6 + tw], scalar1=0.25, scalar2=None, op0=ALU.mult), r=['a2'], w=['sm0b'])
                P.op('dve', _mk('tensor_scalar', out=sm[0:64, 1, 0:tw], in0=a3[0:64, 4:4 + tw], scalar1=0.125, scalar2=None, op0=ALU.mult), r=['a3'], w=['sm1a'])
                P.op('dve', _mk('tensor_scalar', out=sm[64:128, 1, 0:tw], in0=a4[64:128, 0:tw], scalar1=0.0625, scalar2=None, op0=ALU.mult), r=['a4'], w=['sm1b'])
                smt = ['sm0a', 'sm0b', 'sm1a', 'sm1b']
                if t0 == seq0:
                    P.op('dve', _mk('tensor_tensor', out=sm[:, :, 0:8], in0=sm[:, :, 0:8], in1=corr[:, 0:32].rearrange('p (c t) -> p c t', c=2)[:, :, 0:8], op=ALU.mult), r=smt + ['corr'], w=smt)
                if t0 + tw == seq1:
                    P.op('dve', _mk('tensor_tensor', out=sm[:, :, tw - 8:tw], in0=sm[:, :, tw - 8:tw], in1=corr[:, 0:32].rearrange('p (c t) -> p c t', c=2)[:, :, 8:16], op=ALU.mult), r=smt + ['corr'], w=smt)
                P.op('dve', _mk('tensor_tensor', out=pl16[:, :, 0:tw], in0=sm[:, :, 0:tw], in1=U[:, :, 8:8 + tw], op=ALU.subtract), r=smt + ['ub%d' % b], w=['pl16'])
                for ch in range(2):
                    pt, tok = (ps[k % 4], 'ps%d' % (k % 4))
                    mb = mo[k % 4]
                    mtok = 'mo%d' % (k % 4)
                    k += 1
                    P.mm(pt[:, 0:tw], [(pw[:, ch, :], pl16[:, ch, 0:tw])], r=['pl16', 'wres'], w=[tok])
                    P.op('act', _mk('activation', out=mb[:, 0:tw], in_=pt[:, 0:tw], func=AF.Identity, scale=gains[:, 24 + 2 * l + ch:25 + 2 * l + ch]), r=[tok, 'gains'], w=[mtok])
                    P.dma('pool', mixT[768 + 128 * ch:768 + 128 * ch + 128, t0:t0 + tw], mb[:, 0:tw], r=[mtok], w=['mixd'])
            P.barrier()
        steps = [load_consts, phase_mod, phase_in_transpose]
        for l in range(n_layers):
            last = l == NL - 1
            steps += [
                (lambda l=l: phase_norm(l, 0)),
                (lambda l=l: phase_ffn(l, 0, I['f1i'], I['f1o'])),
                (lambda l=l: phase_norm(l, 1)),
                (lambda l=l: phase_inproj(l)),
                (lambda l=l: phase_hgrn(l)),
                (lambda l=l, last=last: phase_attn(l, not last)),
                (lambda l=l, last=last: phase_pool(l, not last)),
                (lambda l=l: phase_outproj(l)),
                (lambda l=l: phase_norm(l, 2)),
                (lambda l=l: phase_ffn(l, 2, I['f2i'], I['f2o'])),
            ]
        steps.append(phase_out_transpose)
        for si, fn in enumerate(steps):
            if upto is not None and si >= upto:
                break
            try:
                fn()
            except _Stop:
                break
        P.barrier()
        P.emit()
    return nc

def host_constants():
    c = {}
    c['ident'] = np.eye(128, dtype=np.float32)
    s = np.arange(128)[:, None]
    t = np.arange(128)[None, :]
    same = s // CH == t // CH
    c['d1f'] = (same & (s > t)).astype(np.float32)
    c['d1b'] = (same & (s < t)).astype(np.float32)
    c['trf'] = (same & (s <= t)).astype(np.float32)
    c['trb'] = (same & (s >= t)).astype(np.float32)
    s2 = np.arange(CH)[:, None]
    t2 = np.arange(CH)[None, :]
    c['maskf'] = (s2 <= t2).astype(np.float32)
    c['maskb'] = (s2 >= t2).astype(np.float32)
    c['bd64'] = (s // 64 == t // 64).astype(np.float32)
    rot = np.zeros((96, 96), np.float32)
    for i in range(16):
        rot[80 + i, 64 + i] = -1.0
        rot[64 + i, 80 + i] = 1.0
    c['rot'] = rot
    sh = np.zeros((32, 96), np.float32)
    sh[np.arange(32), 64 + np.arange(32)] = 1.0
    c['shift'] = sh
    pos = np.arange(TL)
    row = (pos // 64).astype(np.float32)
    col = (pos % 64).astype(np.float32)
    inv = (np.float32(10000.0) ** (-np.arange(8, dtype=np.float32) / np.float32(8))).astype(np.float32)
    ang = np.concatenate([row[:, None] * inv[None, :], col[:, None] * inv[None, :]], axis=1).astype(np.float32)
    cs = np.cos(ang).astype(np.float32).T
    sn = np.sin(ang).astype(np.float32).T
    cosT = np.ones((96, TL), np.float32)
    sinT = np.zeros((96, TL), np.float32)
    cosT[64:80] = cs
    cosT[80:96] = cs
    sinT[64:80] = sn
    sinT[80:96] = sn
    c['cosT'] = cosT
    c['sinT'] = sinT
    corr = np.ones((128, 2, 16), np.float32)
    for g, w in enumerate((2, 4, 8, 16)):
        ch, p0 = (g // 2, g % 2 * 64)
        for i in range(8):
            lo = max(i - w // 2, 0)
            hi = i + w - 1 - w // 2
            corr[p0:p0 + 64, ch, i] = w / float(hi - lo + 1)
            d = 7 - i
            hi2 = min(w - 1 - w // 2, d)
            cnt = hi2 + w // 2 + 1
            corr[p0:p0 + 64, ch, 8 + i] = w / float(cnt)
    c['corr'] = corr.reshape(128, 32)
    return c
_CACHE = {}

def prep_inputs(inputs, b):
    f = lambda a: np.ascontiguousarray(np.asarray(a, dtype=np.float32))
    m = {}
    m['x_in'] = f(inputs['x'][b])
    m['ctx_in'] = f(inputs['ctx'][b])
    cv = np.stack([np.asarray(inputs['c'][b]), np.asarray(inputs['c_ctx'])], 0)
    m['cvT'] = f(cv.reshape(2, 8, 128).transpose(2, 1, 0).reshape(128, 16))
    m['w_mod'] = f(inputs['w_mod'])
    m['b_mod'] = f(inputs['b_mod'])
    m['f1i'] = f(inputs['ffn1_w_in'])
    m['f1o'] = f(inputs['ffn1_w_out'])
    m['f2i'] = f(inputs['ffn2_w_in'])
    m['f2o'] = f(inputs['ffn2_w_out'])
    m['w_in'] = f(inputs['w_in'])
    m['w_out'] = f(inputs['w_out'])
    lb = np.asarray(inputs['hg_lb_logits'], np.float32)
    m['lblB'] = f(np.broadcast_to(lb.reshape(1, NL * 512), (128, NL * 512)))
    m['lblF'] = f(lb.reshape(NL, 2, 2, 128).transpose(3, 0, 1, 2).reshape(128, 16))
    m['g_hg'] = f(np.tile(np.asarray(inputs['hg_out_gain'], np.float32), (1, 2)).T)
    m['g_qa'] = f(np.asarray(inputs['mla_q_a_gain'], np.float32).reshape(NL, 3, 128).transpose(2, 0, 1).reshape(128, NL * 3))
    m['g_kva'] = f(np.asarray(inputs['mla_kv_a_gain'], np.float32).reshape(NL, 2, 128).transpose(2, 0, 1).reshape(128, NL * 2))
    m['g_q'] = f(np.asarray(inputs['mla_q_gain'], np.float32).T)
    m['g_k'] = f(np.asarray(inputs['mla_k_gain'], np.float32).T)
    m['g_ps'] = f(np.asarray(inputs['pool_scale'], np.float32).reshape(NL, 2, 128).transpose(2, 0, 1).reshape(128, NL * 2))
    m['w_uq'] = f(inputs['mla_w_uq'])
    wukv = np.asarray(inputs['mla_w_ukv'], np.float32).reshape(NL, 256, 8, 128)
    wk = np.zeros((NL, 256, 8, 96), np.float32)
    wk[..., 0:64] = wukv[..., 0:64]
    m['wk_pad'] = f(wk.reshape(NL, 256, 768))
    m['wv'] = f(wukv[..., 64:128].reshape(NL, 256, 512))
    pw = np.asarray(inputs['pool_w'], np.float32)
    pwb = np.zeros((NL, 2, 128, 128), np.float32)
    for ch in range(2):
        pwb[:, ch, 0:64, 0:64] = pw[:, 2 * ch]
        pwb[:, ch, 64:128, 64:128] = pw[:, 2 * ch + 1]
    m['pwb'] = pwb
    return m

def kernel(**inputs):
    if 'nc' not in _CACHE:
        _CACHE['nc'] = build()
        _CACHE['consts'] = host_constants()
    nc = _CACHE['nc']
    in_maps = []
    per_b = [prep_inputs(inputs, b) for b in range(4)]
    for core in range(8):
        mm = dict(per_b[core % 4])
        mm.update(_CACHE['consts'])
        in_maps.append(mm)
    res = run_bass_kernel_spmd(nc, in_maps, core_ids=list(range(8)))
    out = np.stack([np.asarray(res.results[b]['out'], np.float32) for b in range(4)], 0)
    return out
```
